# Optimizing a Trainium2 kernel written in Bass

```python
import math
import jax, jax.numpy as jnp
from jax import lax
import numpy as np

D_MODEL = 1024
BATCH = 4
SEQ = 8192
DEPTH = 4

N_MIXERS = 2
N_META = 16
GRID_W = 64
NA_KH = 8
NA_KW = 16
NA_HEADS = 16
NA_HEAD_DIM = D_MODEL // NA_HEADS
DA_HEADS = 8
DA_HEAD_DIM = D_MODEL // (2 * DA_HEADS)
T5_BUCKETS = 32
T5_MAX_DIST = 128
Q_BLOCK = 128
D_FF = int(math.ceil(8 * D_MODEL / 3 / 128)) * 128
FFN_RES = 0.5
RMS_EPS = 1e-6
N_A_LAYERS = (DEPTH + 1) // 2
N_B_LAYERS = DEPTH // 2

kernel_name = "hybrid_natten_diffattn_macaron_encoder"


def rmsnorm(x, g):
    xf = x.astype(jnp.float32)
    y = xf * lax.rsqrt(jnp.mean(xf * xf, axis=-1, keepdims=True) + RMS_EPS)
    return (y * g.astype(jnp.float32)).astype(x.dtype)


def swiglu(x, w_gate, w_up, w_down):
    return (jax.nn.silu(x @ w_gate) * (x @ w_up)) @ w_down


def t5_bucket(rel):
    nb = T5_BUCKETS // 2
    max_exact = nb // 2
    ret = jnp.where(rel > 0, nb, 0)
    n = jnp.abs(rel)
    nf = jnp.maximum(n, 1).astype(jnp.float32)
    large = max_exact + (jnp.log(nf / max_exact) / math.log(T5_MAX_DIST / max_exact)
                         * (nb - max_exact)).astype(jnp.int32)
    large = jnp.minimum(large, nb - 1)
    return ret + jnp.where(n < max_exact, n, large)


def neighborhood_attention(h, w_qkv, b_qkv, w_o, b_o, rpb, meta_bias):
    B, T, _ = h.shape
    n_tok = T - N_META
    rows = n_tok // GRID_W
    kh = min(NA_KH, rows)
    qkv = (h @ w_qkv + b_qkv).reshape(B, T, 3, NA_HEADS, NA_HEAD_DIM)
    q = qkv[:, :, 0] * (NA_HEAD_DIM ** -0.5)
    k = qkv[:, :, 1]
    v = qkv[:, :, 2]
    qm, km, vm = q[:, :N_META], k[:, :N_META], v[:, :N_META]
    grid_shape = (B, rows, GRID_W, NA_HEADS, NA_HEAD_DIM)
    qg = q[:, N_META:].reshape(grid_shape)
    kg = k[:, N_META:].reshape(grid_shape)
    vg = v[:, N_META:].reshape(grid_shape)

    lm = jnp.einsum('bqhd,bmhd->bhqm', qm, km).astype(jnp.float32) + meta_bias[None, :, None, :]
    om = jnp.einsum('bhqm,bmhd->bqhd', jax.nn.softmax(lm, axis=-1).astype(vm.dtype), vm)

    cols = np.arange(GRID_W)
    cs = np.clip(cols - NA_KW // 2, 0, GRID_W - NA_KW)
    col_idx = cs[:, None] + np.arange(NA_KW)[None, :]
    dx = col_idx - cols[:, None] + (NA_KW - 1)

    def row_fn(r):
        rs = jnp.clip(r - kh // 2, 0, rows - kh)
        k_rows = lax.dynamic_slice_in_dim(kg, rs, kh, axis=1)
        v_rows = lax.dynamic_slice_in_dim(vg, rs, kh, axis=1)
        k_win = k_rows[:, :, col_idx]
        v_win = v_rows[:, :, col_idx]
        q_row = lax.dynamic_index_in_dim(qg, r, axis=1, keepdims=False)
        dy = rs + jnp.arange(kh) - r + (NA_KH - 1)
        bias = rpb[:, dy][:, :, dx].transpose(0, 2, 1, 3)
        ln = jnp.einsum('bchd,brckhd->bhcrk', q_row, k_win).astype(jnp.float32) + bias[None]
        lmeta = jnp.einsum('bchd,bmhd->bhcm', q_row, km).astype(jnp.float32) + meta_bias[None, :, None, :]
        logits = jnp.concatenate([ln.reshape(B, NA_HEADS, GRID_W, kh * NA_KW), lmeta], axis=-1)
        p = jax.nn.softmax(logits, axis=-1).astype(v.dtype)
        pn = p[..., :kh * NA_KW].reshape(B, NA_HEADS, GRID_W, kh, NA_KW)
        pm = p[..., kh * NA_KW:]
        return (jnp.einsum('bhcrk,brckhd->bchd', pn, v_win)
                + jnp.einsum('bhcm,bmhd->bchd', pm, vm))

    og = lax.map(row_fn, jnp.arange(rows))
    og = og.transpose(1, 0, 2, 3, 4).reshape(B, n_tok, D_MODEL)
    o = jnp.concatenate([om.reshape(B, N_META, D_MODEL), og], axis=1)
    return o @ w_o + b_o


def diff_attention(h, w_qkv, w_o, lam_p, subln_g, rel_table, lambda_init):
    B, T, _ = h.shape
    n_tok = T - N_META
    nb = n_tok // Q_BLOCK
    d = DA_HEAD_DIM
    q, k, v = jnp.split(h @ w_qkv, 3, axis=-1)
    q = (q * (d ** -0.5)).reshape(B, T, DA_HEADS, 2, d).transpose(0, 2, 3, 1, 4)
    k = k.reshape(B, T, DA_HEADS, 2, d).transpose(0, 2, 3, 1, 4)
    v = v.reshape(B, T, DA_HEADS, 2 * d).transpose(0, 2, 1, 3)
    lp = lam_p.astype(jnp.float32)
    lam = jnp.exp(jnp.sum(lp[0] * lp[1])) - jnp.exp(jnp.sum(lp[2] * lp[3])) + lambda_init
    key_pos = jnp.arange(T)

    def attend(q_blk, q_pos):
        bias = rel_table[t5_bucket(key_pos[None, :] - q_pos[:, None])]
        logits = (jnp.einsum('bhsqd,bhskd->bhsqk', q_blk, k).astype(jnp.float32)
                  + bias.transpose(2, 0, 1).astype(jnp.float32)[None, :, None])
        p = jax.nn.softmax(logits, axis=-1)
        a = (p[:, :, 0] - lam * p[:, :, 1]).astype(v.dtype)
        return jnp.einsum('bhqk,bhkd->bhqd', a, v)

    om = attend(q[:, :, :, :N_META], jnp.arange(N_META))
    qr = q[:, :, :, N_META:].reshape(B, DA_HEADS, 2, nb, Q_BLOCK, d).transpose(3, 0, 1, 2, 4, 5)

    def block_fn(args):
        q_blk, i = args
        return attend(q_blk, N_META + i * Q_BLOCK + jnp.arange(Q_BLOCK))

    orr = lax.map(block_fn, (qr, jnp.arange(nb)))
    orr = orr.transpose(1, 2, 0, 3, 4).reshape(B, DA_HEADS, n_tok, 2 * d)
    o = jnp.concatenate([om, orr], axis=2)
    o = rmsnorm(o, subln_g) * (1.0 - lambda_init)
    o = o.transpose(0, 2, 1, 3).reshape(B, T, D_MODEL)
    return o @ w_o


def setup_inputs(seed: int = 0) -> dict:
    key = jax.random.key(seed)
    ks = jax.random.split(key, 20)
    D, F = D_MODEL, D_FF
    nrm = jax.random.normal
    return {
        "x": nrm(ks[0], (BATCH, SEQ, D), jnp.float32),
        "meta_tokens": nrm(ks[1], (N_META, D), jnp.float32),
        "norm_g": 1.0 + 0.02 * nrm(ks[2], (DEPTH, 6, D), jnp.float32),
        "ffn_w_gate": nrm(ks[3], (DEPTH, 2, D, F), jnp.float32) * D ** -0.5,
        "ffn_w_up": nrm(ks[4], (DEPTH, 2, D, F), jnp.float32) * D ** -0.5,
        "ffn_w_down": nrm(ks[5], (DEPTH, 2, F, D), jnp.float32) * F ** -0.5,
        "na_w_qkv": nrm(ks[6], (N_A_LAYERS, D, 3 * D), jnp.float32) * D ** -0.5,
        "na_b_qkv": 0.02 * nrm(ks[7], (N_A_LAYERS, 3 * D), jnp.float32),
        "na_w_o": nrm(ks[8], (N_A_LAYERS, D, D), jnp.float32) * D ** -0.5,
        "na_b_o": 0.02 * nrm(ks[9], (N_A_LAYERS, D), jnp.float32),
        "na_rpb": 0.1 * nrm(ks[10], (N_A_LAYERS, NA_HEADS, 2 * NA_KH - 1, 2 * NA_KW - 1), jnp.float32),
        "na_meta_bias": 0.1 * nrm(ks[11], (N_A_LAYERS, NA_HEADS, N_META), jnp.float32),
        "da_w_qkv": nrm(ks[12], (N_B_LAYERS, D, 3 * D), jnp.float32) * D ** -0.5,
        "da_w_o": nrm(ks[13], (N_B_LAYERS, D, D), jnp.float32) * D ** -0.5,
        "da_lambda": 0.1 * nrm(ks[14], (N_B_LAYERS, 4, DA_HEAD_DIM), jnp.float32),
        "da_subln_g": 1.0 + 0.02 * nrm(ks[15], (N_B_LAYERS, 2 * DA_HEAD_DIM), jnp.float32),
        "t5_rel_bias": 0.1 * nrm(ks[16], (T5_BUCKETS, DA_HEADS), jnp.float32),
    }


def reference(x, meta_tokens, norm_g, ffn_w_gate, ffn_w_up, ffn_w_down,
              na_w_qkv, na_b_qkv, na_w_o, na_b_o, na_rpb, na_meta_bias,
              da_w_qkv, da_w_o, da_lambda, da_subln_g, t5_rel_bias):
    B = x.shape[0]
    meta = jnp.broadcast_to(meta_tokens[None].astype(x.dtype), (B, N_META, D_MODEL))
    h = jnp.concatenate([meta, x], axis=1)
    for i in range(DEPTH):
        g = norm_g[i]
        f1 = swiglu(rmsnorm(h, g[0]), ffn_w_gate[i, 0], ffn_w_up[i, 0], ffn_w_down[i, 0])
        h = h + FFN_RES * rmsnorm(f1, g[1])
        j = i // N_MIXERS
        hn = rmsnorm(h, g[2])
        if i % N_MIXERS == 0:
            m = neighborhood_attention(hn, na_w_qkv[j], na_b_qkv[j], na_w_o[j], na_b_o[j],
                                       na_rpb[j], na_meta_bias[j])
        else:
            lambda_init = 0.8 - 0.6 * math.exp(-0.3 * i)
            m = diff_attention(hn, da_w_qkv[j], da_w_o[j], da_lambda[j], da_subln_g[j],
                               t5_rel_bias, lambda_init)
        h = h + rmsnorm(m, g[3])
        f2 = swiglu(rmsnorm(h, g[4]), ffn_w_gate[i, 1], ffn_w_up[i, 1], ffn_w_down[i, 1])
        h = h + FFN_RES * rmsnorm(f2, g[5])
    return h[:, N_META:]
```

```python
import math
from contextlib import ExitStack

import numpy as np
import concourse.bass as bass
import concourse.mybir as mybir
from concourse.bass_utils import run_bass_kernel_spmd

F32 = mybir.dt.float32
BF16 = mybir.dt.bfloat16
AF = mybir.ActivationFunctionType
ALU = mybir.AluOpType

D = 1024
NMETA = 16
GW = 64
RMS_EPS = 1e-6
NEG = -30000.0

ENGS = ("pe", "act", "dve", "pool", "sp")
EPOCH = 30000
NDMA_SEMS = {"sp": 24, "pool": 16, "act": 4, "pe": 2, "dve": 2}


class Buf:
    __slots__ = ("name", "last_w", "rd_eng", "rd_dma", "excl")

    def __init__(self, name, excl=False):
        self.name = name
        self.excl = excl
        self.last_w = None
        self.rd_eng = {}
        self.rd_dma = []


class Op:
    __slots__ = ("eng", "fn", "dma", "deps", "signal", "token", "idx", "prev_dma", "inc")

    def __init__(self, eng, fn, dma):
        self.eng = eng
        self.fn = fn
        self.dma = dma
        self.inc = 16
        self.deps = []
        self.signal = False
        self.token = None
        self.prev_dma = None


class Arena:
    def __init__(self, t, nbytes):
        self.t = t
        self.nbytes = nbytes
        self.off = 0
        self.base = 0

    def alloc(self, shape, dtype):
        n = 1
        for s in shape:
            n *= s
        esz = 4 if dtype == F32 else 2
        nb = (n * esz + 31) // 32 * 32
        assert self.off + nb <= self.nbytes, ("arena overflow", self.off, nb)
        v = self.t[:, self.off // 4:(self.off + nb) // 4]
        self.off += nb
        if dtype != F32:
            v = v.bitcast(dtype)
        v = v[:, 0:n]
        if len(shape) == 2:
            return v.rearrange("p (a b) -> p a b", b=shape[1])
        if len(shape) == 3:
            return v.rearrange("p (a b c) -> p a b c", b=shape[1], c=shape[2])
        return v

    def mark(self):
        self.base = self.off

    def reset(self):
        self.off = self.base


class Prog:
    def __init__(self, nc, stack):
        self.nc = nc
        self.stack = stack
        self.ops = []
        self.live = []

    def buf(self, name="b"):
        b = Buf(name)
        self.live.append(b)
        return b

    def bufs(self, n, name="b"):
        return [self.buf(name) for _ in range(n)]

    def add(self, eng, fn, reads=(), writes=(), dma=False, inc=16):
        op = Op(eng, fn, dma)
        op.inc = inc
        op.idx = len(self.ops)
        deps = {}
        xr = [b for b in reads if b.excl]
        if xr:
            writes = list(writes) + [b for b in xr if b not in writes]
        for b in reads:
            if b.last_w is not None:
                deps[b.last_w.idx] = (b.last_w, True)
        for b in writes:
            if b.last_w is not None and b.last_w.idx not in deps:
                deps[b.last_w.idx] = (b.last_w, False)
            for r in b.rd_eng.values():
                if r.idx not in deps:
                    deps[r.idx] = (r, False)
            for r in b.rd_dma:
                if r.idx not in deps:
                    deps[r.idx] = (r, False)
        for p, raw in deps.values():
            if p is op:
                continue
            same = (p.eng == op.eng) and (not p.dma) and (not op.dma)
            if same and (op.eng == "pe" or not raw):
                continue
            op.deps.append(p)
            p.signal = True
        if fn is not None:
            for b in writes:
                b.last_w = op
                b.rd_eng = {}
                b.rd_dma = []
            for b in reads:
                if b.last_w is not op:
                    if dma:
                        b.rd_dma.append(op)
                    else:
                        b.rd_eng[eng] = op
        self.ops.append(op)
        return op

    def dma(self, eng, out, in_, reads=(), writes=(), **kw):
        return self.add(eng, lambda e: e.dma_start(out=out, in_=in_, **kw), reads, writes, dma=True)

    def barrier(self, extra=()):
        bl = list(self.live) + list(extra)
        for e in ENGS:
            self.add(e, None, reads=bl, writes=bl)
        self.live = []

    def mm(self, out, lhsT, rhs, start, stop, reads, writes):
        return self.add("pe", lambda e: e.matmul(out=out, lhsT=lhsT, rhs=rhs, start=start, stop=stop), reads, writes)

    def tr(self, out, in_, ident, reads, writes):
        return self.add("pe", lambda e: e.transpose(out=out, in_=in_, identity=ident), reads, writes)

    def act(self, out, in_, func, reads, writes, **kw):
        return self.add("act", lambda e: e.activation(out=out, in_=in_, func=func, **kw), reads, writes)

    def stt(self, out, in0, scalar, in1, op0, op1, reads, writes):
        return self.add("dve", lambda e: e.scalar_tensor_tensor(out=out, in0=in0, scalar=scalar, in1=in1, op0=op0, op1=op1), reads, writes)

    def tt(self, eng, out, in0, in1, op, reads, writes):
        return self.add(eng, lambda e: e.tensor_tensor(out=out, in0=in0, in1=in1, op=op), reads, writes)

    def cp(self, eng, out, in_, reads, writes):
        if eng == "act":
            return self.add("act", lambda e: e.copy(out=out, in_=in_), reads, writes)
        return self.add(eng, lambda e: e.tensor_copy(out=out, in_=in_), reads, writes)

    def recip(self, out, in_, reads, writes):
        return self.add("dve", lambda e: e.reciprocal(out=out, in_=in_), reads, writes)

    def memset(self, eng, ap, val, writes):
        return self.add(eng, lambda e: e.memset(ap, val), (), writes)

    def emit(self):
        nc = self.nc
        st = self.stack
        cnt = {e: 0 for e in ENGS}
        epoch_sems = {e: [] for e in ENGS}
        dma_sems = {e: [] for e in ENGS}
        dma_use = {e: [] for e in ENGS}
        dma_last = {e: [] for e in ENGS}
        dma_rr = {e: 0 for e in ENGS}
        for op in self.ops:
            if op.fn is None:
                continue
            e = op.eng
            if op.dma and op.inc == 1:
                sem = st.enter_context(nc.semaphore(f"cc_{op.idx}"))
                op.token = (sem, 1)
                op.signal = True
            elif op.dma:
                if not dma_sems[e]:
                    for j in range(NDMA_SEMS[e]):
                        dma_sems[e].append(st.enter_context(nc.semaphore(f"d_{e}_{j}")))
                        dma_use[e].append(0)
                        dma_last[e].append(None)
                j = dma_rr[e] % len(dma_sems[e])
                dma_rr[e] += 1
                dma_use[e][j] += 1
                op.prev_dma = dma_last[e][j]
                op.token = (dma_sems[e][j], 16 * dma_use[e][j])
                dma_last[e][j] = op.token
                op.signal = True
            elif op.signal:
                k = cnt[e] // EPOCH
                if k >= len(epoch_sems[e]):
                    epoch_sems[e].append(st.enter_context(nc.semaphore(f"s_{e}_{k}")))
                cnt[e] += 1
                op.token = (epoch_sems[e][k], cnt[e] - k * EPOCH)
        per_eng = {e: [] for e in ENGS}
        for op in self.ops:
            per_eng[op.eng].append(op)
        ninst = {e: 0 for e in ENGS}

        def run(e, eo):
            waited = {}
            for op in per_eng[e]:
                toks = [p.token for p in op.deps]
                if op.prev_dma is not None:
                    toks.append(op.prev_dma)
                need = {}
                for (s, v) in toks:
                    key = id(s)
                    if waited.get(key, 0) >= v:
                        continue
                    if key not in need or need[key][1] < v:
                        need[key] = (s, v)
                for key, (s, v) in need.items():
                    eo.wait_ge(s, v)
                    waited[key] = v
                    ninst[e] += 1
                if op.fn is None:
                    continue
                inst = op.fn(eo)
                ninst[e] += 1
                if op.signal:
                    inst.then_inc(op.token[0], op.inc if op.dma else 1)

        with nc.Block() as block:
            @block.tensor
            def _(eo):
                run("pe", eo)

            @block.scalar
            def _(eo):
                run("act", eo)

            @block.vector
            def _(eo):
                run("dve", eo)

            @block.gpsimd
            def _(eo):
                run("pool", eo)

            @block.sync
            def _(eo):
                run("sp", eo)
        self.ninst = ninst


class Cfg:
    def __init__(self, SEQ=8192, DEPTH=4, DFF=2816, B=4):
        self.SEQ, self.DEPTH, self.DFF, self.B = SEQ, DEPTH, DFF, B
        self.ROWS = SEQ // GW
        self.NG = SEQ // 2
        self.NGT = self.NG // 128
        self.NCH = self.NGT + 1
        self.TCP = self.NCH * 128
        self.FCH = DFF // 128
        self.NQ = self.NGT // 4
        self.NKC = 1 + 2 * self.NGT
        self.NLA = (DEPTH + 1) // 2
        self.NLB = DEPTH // 2
        self.NVAR = 5
        assert self.NGT % 4 == 0 and self.NGT >= 8

    def groups(self, gs=4):
        g = [[(0, NMETA)]]
        for c in range(self.NGT // gs):
            g.append([(1 + gs * c + j, 128) for j in range(gs)])
        return g

    def var_of_block(self, b):
        if b == 0:
            return 0
        if b == 1:
            return 1
        if b == self.NGT - 2:
            return 3
        if b == self.NGT - 1:
            return 4
        return 2

    def rep_block(self, v):
        return [0, 1, 2, self.NGT - 2, self.NGT - 1][v]


def t5_bucket_np(rel):
    nb, me = 16, 8
    rel = np.asarray(rel, np.int64)
    ret = np.where(rel > 0, nb, 0)
    n = np.abs(rel)
    nf = np.maximum(n, 1).astype(np.float32)
    large = me + (np.log(nf / np.float32(me)) / np.float32(math.log(128 / me)) * np.float32(nb - me)).astype(np.int32)
    large = np.minimum(large, nb - 1)
    return ret + np.where(n < me, n, large)


def na_mask_index(cfg, half):
    NB = 2 * cfg.NGT
    rows = cfg.ROWS
    kh = min(8, rows)
    p = np.arange(128)
    out = np.full((cfg.NVAR, 6, 128, 128), 465, np.int64)
    for v in range(cfg.NVAR):
        i = half * cfg.NGT + cfg.rep_block(v)
        qr = 2 * i + p // 64
        qc = p % 64
        rs = np.clip(qr - kh // 2, 0, rows - kh)
        cs = np.clip(qc - 8, 0, GW - 16)
        offs = [-2, -1, 0, 1, 2, 3 if v == 0 else (-3 if v == 4 else None)]
        for s in range(6):
            if offs[s] is None:
                continue
            g = i + offs[s]
            if g < 0 or g >= NB:
                continue
            kr = 2 * g + p // 64
            kc = p % 64
            vis = ((kr[:, None] >= rs[None, :]) & (kr[:, None] < rs[None, :] + kh)
                   & (kc[:, None] >= cs[None, :]) & (kc[:, None] < cs[None, :] + 16))
            dy = kr[:, None] - qr[None, :] + 7
            dx = kc[:, None] - qc[None, :] + 15
            idx = dy * 31 + dx
            out[v, s] = np.where(vis, idx, 465)
    return out


def build_na_mask(cfg, rpb, half):
    idx = na_mask_index(cfg, half)
    ext = np.concatenate([rpb.reshape(rpb.shape[0], 16, 465),
                          np.full((rpb.shape[0], 16, 1), NEG, np.float32)], axis=2)
    m = ext[:, :, idx]
    return np.ascontiguousarray(m.transpose(0, 2, 4, 1, 3, 5)).astype(np.float32)


def build_da_tables(cfg, tbl, half):
    NG, NGT, NKC, NQ = cfg.NG, cfg.NGT, cfg.NKC, cfg.NQ
    p = np.arange(128)[:, None]
    x = np.arange(1152)[None, :]
    tblT = tbl.T
    G = np.zeros((8, 128, 2, 1152), np.float32)
    for r in range(2):
        rel = (r - half) * NG + p - x + 512
        G[:, :, r, :] = tblT[:, t5_bucket_np(rel)]
    m = np.arange(16)[:, None]
    qf = np.arange(512)[None, :]
    Gm = tblT[:, t5_bucket_np(m - (16 + half * NG + qf))].astype(np.float32)
    pp = np.arange(128)[:, None]
    Gx = np.zeros((8, 128, 2, 512), np.float32)
    Gx[:, :, 0, :] = tblT[:, t5_bucket_np((0 - half) * NG + 128 * (NGT - 1) + pp - qf)]
    Gx[:, :, 1, :] = tblT[:, t5_bucket_np((1 - half) * NG + pp - 512 * (NQ - 1) - qf)]
    cb = np.zeros((8, NQ + 1, NKC), np.float32)
    for qc in range(1, NQ + 1):
        qpos = 16 + half * NG + (qc - 1) * 512
        for kc in range(NKC):
            kpos = 0 if kc == 0 else 16 + (kc - 1) * 128
            cb[:, qc, kc] = tblT[:, t5_bucket_np(kpos - qpos)]
    cb = np.ascontiguousarray(np.broadcast_to(cb.reshape(8, 1, -1), (8, 128, (NQ + 1) * NKC)))
    kp = np.zeros((128, NKC), np.int64)
    kp[:, 0] = np.arange(128)
    for kc in range(1, NKC):
        kp[:, kc] = 16 + (kc - 1) * 128 + np.arange(128)
    q = np.arange(16)[None, None, :]
    Bm = tblT[:, t5_bucket_np(kp[:, :, None] - q)].astype(np.float32)
    return G, np.ascontiguousarray(Gm), cb, np.ascontiguousarray(Bm.reshape(8, 128, NKC * 16)), Gx.reshape(8, 128, 1024)


class Builder:
    def __init__(self, cfg, mode, seg):
        self.cfg = cfg
        self.mode = mode
        self.seg = seg
        self.nc = bass.Bass("TRN2", target_bir_lowering=False)
        self.dr = {}
        self.in_names = []
        self.out_names = []

    def dram(self, name, shape, dtype, kind):
        t = self.nc.dram_tensor(name, list(shape), dtype, kind=kind)
        self.dr[name] = t.ap()
        if kind == "ExternalInput":
            self.in_names.append(name)
        elif kind == "ExternalOutput":
            self.out_names.append(name)
        return self.dr[name]

    def build(self):
        cfg = self.cfg
        nc = self.nc
        DEPTH, DFF = cfg.DEPTH, cfg.DFF
        seg, fused = self.seg, self.mode == "fused"
        with ExitStack() as st:
            P = Prog(nc, st)
            self.P = P
            at = st.enter_context(nc.sbuf_tensor("arena", [128, 200 * 256], F32))
            self.A = Arena(at, 200 * 1024)
            self.ps = st.enter_context(nc.psum_tensor("ps", [128, 8, 512], F32))
            self.pb = [Buf(f"bank{i}", excl=True) for i in range(8)]
            EI = "ExternalInput"
            self.dram("norm_g", [DEPTH, 6, D], F32, EI)
            self.dram("ffn_w_gate", [DEPTH, 2, D, DFF], F32, EI)
            self.dram("ffn_w_up", [DEPTH, 2, D, DFF], F32, EI)
            self.dram("ffn_w_down", [DEPTH, 2, DFF, D], F32, EI)
            self.dram("na_w_qkv", [cfg.NLA, D, 3 * D], F32, EI)
            self.dram("na_b_qkv", [cfg.NLA, 3 * D], F32, EI)
            self.dram("na_w_o", [cfg.NLA, D, D], F32, EI)
            self.dram("na_b_o", [cfg.NLA, D], F32, EI)
            self.dram("na_meta_bias", [cfg.NLA, 16, 16], F32, EI)
            self.dram("na_mask", [cfg.NLA, cfg.NVAR, 128, 16 * 6 * 128], F32, EI)
            if cfg.NLB:
                self.dram("da_w_qkv", [cfg.NLB, D, 3 * D], F32, EI)
                self.dram("da_w_o", [cfg.NLB, D, D], F32, EI)
                self.dram("da_lambda", [cfg.NLB, 4 * 64], F32, EI)
                self.dram("da_subln_g", [cfg.NLB, 128], F32, EI)
                self.dram("da_G", [8, 128, 2 * 1152], F32, EI)
                self.dram("da_Gm", [8, 16, 512], F32, EI)
                self.dram("da_Gx", [8, 128, 1024], F32, EI)
                self.dram("da_cb", [8, 128, (cfg.NQ + 1) * cfg.NKC], F32, EI)
                self.dram("da_Bm", [8, 128, cfg.NKC * 16], F32, EI)
            TCP, NCH = cfg.TCP, cfg.NCH
            self.dram("h_in", [TCP, D], F32, EI)
            self.b_hin = [Buf("hin") for _ in range(NCH)]
            if fused:
                self.dram("h", [TCP, D], F32, "Internal")
                self.dram("out", [cfg.NG, D], F32, "ExternalOutput")
                for nm in ("QT", "KT", "AOT"):
                    self.dram(nm, [8 * 128 * TCP], BF16, "Internal")
                self.dram("V", [8 * 128 * TCP], BF16, "Internal")
                CH = 131072
                for nm, shp in (("KT_all", [8, 2, 128 * TCP]), ("V_all", [8, 2, 128 * TCP]),
                                ("KTlo_all", [2, 2 * CH]), ("KThi_all", [2, 2 * CH]),
                                ("Vlo_all", [2, 2 * CH]), ("Vhi_all", [2, 2 * CH])):
                    t = self.nc.dram_tensor(nm, shp, BF16, kind="Internal", addr_space="Local")
                    self.dr[nm] = t.ap()
            else:
                if seg > 0:
                    self.dram("QT", [8 * 128 * TCP], BF16, EI)
                    self.dram("KT", [8 * 128 * TCP], BF16, EI)
                    self.dram("V", [8 * 128 * TCP], BF16, EI)
                    if (seg - 1) % 2 == 1:
                        self.dram("KT_all", [8, 2, 128 * TCP], BF16, EI)
                        self.dram("V_all", [8, 2, 128 * TCP], BF16, EI)
                    else:
                        for nm in ("KTlo_all", "KThi_all", "Vlo_all", "Vhi_all"):
                            self.dram(nm, [2, 2 * 131072], BF16, EI)
                    import os
                    self.attn_only = "attnonly" in os.environ.get("KDBG1", "")
                    self.dram("AOT", [8 * 128 * TCP], BF16, "ExternalOutput" if self.attn_only else "Internal")
                    self.dram("h", [TCP, D], F32, "Internal")
                if seg > 0 and self.attn_only:
                    pass
                elif seg < DEPTH:
                    self.dram("h_out", [TCP, D], F32, "ExternalOutput")
                    self.dram("QT_o", [8 * 128 * TCP], BF16, "ExternalOutput")
                    self.dram("KT_o", [8 * 128 * TCP], BF16, "ExternalOutput")
                    self.dram("V_o", [8 * 128 * TCP], BF16, "ExternalOutput")
                else:
                    self.dram("out", [cfg.NG, D], F32, "ExternalOutput")
            self.b_h = [Buf("h") for _ in range(NCH)]
            self.b_hout = [Buf("hout") for _ in range(NCH)]
            self.b_out = [Buf("out") for _ in range(NCH)]
            self.b_QT, self.b_KT, self.b_V, self.b_AOT = Buf("QT"), Buf("KT"), Buf("V"), Buf("AOT")
            self.b_KTall, self.b_Vall = Buf("KTall"), Buf("Vall")

            self.consts()
            dr = self.dr
            if fused:
                hcur = ("h_in", self.b_hin)
                for i in range(DEPTH):
                    self.stage_ffn(i, 0, hcur, ("h", self.b_h))
                    hcur = ("h", self.b_h)
                    self.stage_qkv(i, hcur, "QT", "KT", "V")
                    self.stage_gather(i)
                    self.stage_attn(i)
                    self.stage_oproj(i, hcur, hcur)
                    last = (i == DEPTH - 1)
                    self.stage_ffn(i, 1, hcur, ("out", self.b_out) if last else hcur)
                P.barrier(self.b_out)
            else:
                hcur = ("h_in", self.b_hin)
                if seg > 0:
                    i = seg - 1
                    self.stage_attn(i)
                    if self.attn_only:
                        P.barrier([self.b_AOT])
                        P.emit()
                        self.ninst = P.ninst
                        return nc
                    self.stage_oproj(i, hcur, ("h", self.b_h))
                    hcur = ("h", self.b_h)
                    if seg == DEPTH:
                        self.stage_ffn(i, 1, hcur, ("out", self.b_out))
                    else:
                        self.stage_ffn(i, 1, hcur, hcur)
                if seg < DEPTH:
                    import os
                    dbg = os.environ.get("KDBG", "ffn,qkv")
                    if "ffn" in dbg:
                        self.stage_ffn(seg, 0, hcur, ("h_out", self.b_hout))
                    if "qkv" in dbg:
                        self.stage_qkv(seg, ("h_out", self.b_hout) if "ffn" in dbg else hcur, "QT_o", "KT_o", "V_o")
                P.barrier(self.b_out + self.b_hout + [self.b_QT, self.b_KT, self.b_V])
            P.emit()
            self.ninst = P.ninst
        return nc

    def consts(self):
        P, A = self.P, self.A
        self.b_c = P.buf("consts")
        self.identf = A.alloc([128], F32)
        self.ident = A.alloc([128], BF16)
        self.ones_f = A.alloc([128], F32)
        self.ones_b = A.alloc([128], BF16)
        self.eps = A.alloc([1], F32)
        b = self.b_c
        P.memset("pool", self.identf, 0.0, [b])
        P.add("pool", lambda e: e.affine_select(out=self.identf, in_=self.identf, pattern=[[-1, 128]],
                                                compare_op=ALU.not_equal, fill=1.0, base=0,
                                                channel_multiplier=1), [b], [b])
        P.cp("dve", self.ident, self.identf, [b], [b])
        P.memset("dve", self.ones_f, 1.0, [b])
        P.memset("dve", self.ones_b, 1.0, [b])
        P.memset("dve", self.eps, RMS_EPS, [b])
        A.mark()

    def load_w_cast(self, dst, src, rows_chunks, ncols, bufs):
        P = self.P
        step = ncols
        while step > 2048:
            step //= 2
        i = 0
        for k in range(rows_chunks):
            for c0 in range(0, ncols, step):
                P.dma("pool", dst[:, k, c0:c0 + step], src[k * 128:(k + 1) * 128, c0:c0 + step], (), [bufs[i]])
                i += 1
        return i

    def load_rep(self, dst, vec, b):
        self.P.dma("sp", dst, vec.partition_broadcast(128), (), [b])

    def rstd_from_ss(self, ss, rstd, np_, n, b_ss, b_rstd):
        P = self.P
        P.act(rstd[:np_], ss[:np_], AF.Sqrt, [b_ss, self.b_c], [b_rstd], bias=self.eps[:np_], scale=1.0 / n)
        P.recip(rstd[:np_], rstd[:np_], [b_rstd], [b_rstd])

    def norm_T(self, W, grp, hsrc, g_rep, b_g, gi):
        P, ps, pb = self.P, self.ps, self.pb
        hname, hbufs = hsrc
        hd = self.dr[hname]
        slot = gi % 2
        hb, b_hb = W["hb"][slot], W["b_hb"][slot]
        xnT, b_xnT = W["xnT"][slot], W["b_xnT"][slot]
        N = 0
        ptr = ps[:, 7, :].bitcast(BF16).rearrange("p (k n) -> p k n", n=128)
        for j, (t, np_) in enumerate(grp):
            P.dma("sp", hb[:np_, j, :], hd[t * 128:t * 128 + np_, :], [hbufs[t]], [b_hb[j]])
            sl = (gi * len(grp) + j) % 2
            ss, rstd, xn = W["ss"][sl], W["rstd"][sl], W["xn"][sl]
            b_ss, b_rstd, b_xn = W["b_ss"][sl], W["b_rstd"][sl], W["b_xn"][sl]
            P.act(W["junk"][:np_], hb[:np_, j, :], AF.Square, [b_hb[j]], [W["b_junk"], b_ss], accum_out=ss[:np_])
            self.rstd_from_ss(ss, rstd, np_, D, b_ss, b_rstd)
            P.stt(xn[:np_], hb[:np_, j, :], rstd[:np_], g_rep[:np_], ALU.mult, ALU.mult, [b_hb[j], b_rstd, b_g], [b_xn])
            for k in range(8):
                P.tr(ptr[:, k, :np_], xn[:np_, k * 128:(k + 1) * 128], self.ident[:np_, :np_], [b_xn, self.b_c], [pb[7]])
            P.cp("dve" if j % 2 else "act", xnT[:, :, N:N + np_], ptr[:, :, :np_], [pb[7]], [b_xnT])
            N += np_
        return N

    def work_common(self, gs=4, lean=False):
        P, A = self.P, self.A
        W = {}
        hb0 = A.alloc([gs, D], F32)
        W["hb"] = [hb0, hb0 if lean else A.alloc([gs, D], F32)]
        bh0 = P.bufs(gs, "hb")
        W["b_hb"] = [bh0, bh0 if lean else P.bufs(gs, "hb")]
        W["xnT"] = [A.alloc([8, gs * 128], BF16) for _ in range(2)]
        W["b_xnT"] = P.bufs(2, "xnT")
        W["xn"] = [A.alloc([D], BF16) for _ in range(2)]
        W["b_xn"] = P.bufs(2, "xn")
        W["ss"] = [A.alloc([1], F32) for _ in range(2)]
        W["b_ss"] = P.bufs(2, "ss")
        W["rstd"] = [A.alloc([1], F32) for _ in range(2)]
        W["b_rstd"] = P.bufs(2, "rstd")
        W["junk"] = A.alloc([D], BF16)
        W["b_junk"] = P.buf("junk")
        tmp0 = A.alloc([D], F32)
        W["tmp"] = [tmp0, tmp0 if lean else A.alloc([D], F32)]
        bt0 = P.buf("tmp")
        W["b_tmp"] = [bt0, bt0 if lean else P.buf("tmp")]
        W["ho"] = [A.alloc([D], F32) for _ in range(2)]
        W["b_ho"] = P.bufs(2, "ho")
        W["ss2"] = [A.alloc([1], F32) for _ in range(2)]
        W["b_ss2"] = P.bufs(2, "ss2")
        W["rstd2"] = [A.alloc([1], F32) for _ in range(2)]
        W["b_rstd2"] = P.bufs(2, "rstd2")
        return W

    def dst_rows(self, hdst, t, np_):
        name, bufs = hdst
        d = self.dr[name]
        if name == "out":
            if t == 0:
                return None, None
            return d[(t - 1) * 128:(t - 1) * 128 + np_, :], bufs[t]
        return d[t * 128:t * 128 + np_, :], bufs[t]

    def resid_out(self, W, cnt, np_, src, src_bufs, hb_j, b_hb_j, g_rep, b_g, scale, hdst, t):
        P = self.P
        sl = cnt % 2
        ss2, rstd2, tmp, ho = W["ss2"][sl], W["rstd2"][sl], W["tmp"][sl], W["ho"][sl]
        b_ss2, b_rstd2, b_tmp, b_ho = W["b_ss2"][sl], W["b_rstd2"][sl], W["b_tmp"][sl], W["b_ho"][sl]
        P.act(W["junk"][:np_], src, AF.Square, src_bufs, [W["b_junk"], b_ss2], accum_out=ss2[:np_])
        self.rstd_from_ss(ss2, rstd2, np_, D, b_ss2, b_rstd2)
        P.stt(tmp[:np_], src, rstd2[:np_], g_rep[:np_], ALU.mult, ALU.mult, list(src_bufs) + [b_rstd2, b_g], [b_tmp])
        P.stt(ho[:np_], tmp[:np_], float(scale), hb_j, ALU.mult, ALU.add, [b_tmp, b_hb_j], [b_ho])
        dst, bd = self.dst_rows(hdst, t, np_)
        if dst is not None:
            P.dma("sp", dst, ho[:np_], [b_ho], [bd])

    def stage_ffn(self, li, w, hsrc, hdst):
        cfg, P, A, ps, pb, dr = self.cfg, self.P, self.A, self.ps, self.pb, self.dr
        A.reset()
        FCH, DFF = cfg.FCH, cfg.DFF
        Wg = A.alloc([8, DFF], BF16)
        Wu = A.alloc([8, DFF], BF16)
        Wd = A.alloc([FCH, D], BF16)
        b_wg, b_wu, b_wd = P.bufs(32, "wg"), P.bufs(32, "wu"), P.bufs(FCH, "wd")
        n = self.load_w_cast(Wg, dr["ffn_w_gate"][li, w], 8, DFF, b_wg)
        b_wg = b_wg[:n]
        n = self.load_w_cast(Wu, dr["ffn_w_up"][li, w], 8, DFF, b_wu)
        b_wu = b_wu[:n]
        self.load_w_cast(Wd, dr["ffn_w_down"][li, w], FCH, D, b_wd)
        gin, gout = A.alloc([D], F32), A.alloc([D], F32)
        b_gin, b_gout = P.buf("gin"), P.buf("gout")
        self.load_rep(gin, dr["norm_g"][li, 4 * w, :], b_gin)
        self.load_rep(gout, dr["norm_g"][li, 4 * w + 1, :], b_gout)
        gs = 2 if DFF > 2048 else 4
        W = self.work_common(gs, lean=(gs == 2))
        HT = A.alloc([FCH, gs * 128], BF16)
        b_HT = P.buf("HT")
        sg = [A.alloc([gs * 128], F32) for _ in range(2)]
        b_sg = P.bufs(2, "sg")
        cnt = 0
        import os
        kparts = os.environ.get("KPARTS", "norm,gu,down")
        kgroups = int(os.environ.get("KGROUPS", "1000"))
        for gi, grp in enumerate(cfg.groups(gs)):
            if gi >= kgroups:
                break
            N = self.norm_T(W, grp, hsrc, gin, b_gin, gi)
            slot = gi % 2
            xnT, b_xnT = W["xnT"][slot], W["b_xnT"][slot]
            hb, b_hb = W["hb"][slot], W["b_hb"][slot]
            if "gu" not in kparts:
                continue
            for f in range(FCH):
                s2 = f % 2
                bg, bu = 2 * s2, 2 * s2 + 1
                for k in range(8):
                    P.mm(ps[:, bg, :N], Wg[:, k, f * 128:(f + 1) * 128], xnT[:, k, :N], k == 0, k == 7,
                         [b_xnT] + b_wg, [pb[bg]])
                for k in range(8):
                    P.mm(ps[:, bu, :N], Wu[:, k, f * 128:(f + 1) * 128], xnT[:, k, :N], k == 0, k == 7,
                         [b_xnT] + b_wu, [pb[bu]])
                P.act(sg[s2][:, :N], ps[:, bg, :N], AF.Silu, [pb[bg]], [b_sg[s2]])
                P.tt("dve", HT[:, f, :N], sg[s2][:, :N], ps[:, bu, :N], ALU.mult, [b_sg[s2], pb[bu]], [b_HT])
            if "down" not in kparts:
                continue
            for j, (t, np_) in enumerate(grp):
                pd0 = 4 if cnt % 2 == 0 else 2
                for half in range(2):
                    for f in range(FCH):
                        P.mm(ps[:np_, pd0 + half, :], HT[:, f, j * 128:j * 128 + np_], Wd[:, f, half * 512:(half + 1) * 512],
                             f == 0, f == FCH - 1, [b_HT, b_wd[f]], [pb[pd0 + half]])
                src = ps[:np_, pd0:pd0 + 2, :]
                self.resid_out(W, cnt, np_, src, [pb[pd0], pb[pd0 + 1]], hb[:np_, j, :], b_hb[j], gout, b_gout, 0.5, hdst, t)
                cnt += 1
        P.barrier()

    def stage_qkv(self, li, hsrc, nQ, nK, nV):
        cfg, P, A, ps, pb, dr = self.cfg, self.P, self.A, self.ps, self.pb, self.dr
        A.reset()
        is_na = (li % 2 == 0)
        jj = li // 2
        TCP, NCH = cfg.TCP, cfg.NCH
        Wq = A.alloc([8, 3 * D], BF16)
        b_w = P.bufs(16, "wqkv")
        self.load_w_cast(Wq, dr["na_w_qkv" if is_na else "da_w_qkv"][jj], 8, 3 * D, b_w)
        g2 = A.alloc([D], F32)
        b_g2 = P.buf("g2")
        self.load_rep(g2, dr["norm_g"][li, 2, :], b_g2)
        if is_na:
            bqk = A.alloc([16], F32)
            b_bqk = P.buf("bqk")
            for kq in range(16):
                P.dma("sp", bqk[:, kq:kq + 1], dr["na_b_qkv"][jj, kq * 128:(kq + 1) * 128].rearrange("(p o) -> p o", o=1),
                      (), [b_bqk])
            bv = A.alloc([D], F32)
            b_bv = P.buf("bv")
            self.load_rep(bv, dr["na_b_qkv"][jj, 2 * D:3 * D], b_bv)
        W = self.work_common()
        qk = [A.alloc([16, 512], BF16) for _ in range(2)]
        b_qk = P.bufs(2, "qk")
        vs = [A.alloc([D], BF16) for _ in range(2)]
        b_vs = P.bufs(2, "vs")
        if is_na:
            QTd = dr[nQ].rearrange("(c p k n) -> c p k n", p=128, k=8, n=128)
            KTd = dr[nK].rearrange("(c p k n) -> c p k n", p=128, k=8, n=128)
            Vd = dr[nV].rearrange("(c p f) -> c p f", p=128, f=D)
        else:
            QTd = dr[nQ].rearrange("(k p n) -> p k n", p=128, n=TCP)
            KTd = dr[nK].rearrange("(k p n) -> p k n", p=128, n=TCP)
            Vd = dr[nV].rearrange("(h p c d) -> p h c d", p=128, c=NCH, d=128)
        cnt = 0
        for gi, grp in enumerate(cfg.groups()):
            N = self.norm_T(W, grp, hsrc, g2, b_g2, gi)
            slot = gi % 2
            xnT, b_xnT = W["xnT"][slot], W["b_xnT"][slot]
            qs, b_qs = qk[slot], b_qk[slot]
            for kq in range(16):
                bk = kq % 4
                for k in range(8):
                    P.mm(ps[:, bk, :N], Wq[:, k, kq * 128:(kq + 1) * 128], xnT[:, k, :N], k == 0, k == 7,
                         [b_xnT] + b_w, [pb[bk]])
                if is_na:
                    P.act(qs[:, kq, :N], ps[:, bk, :N], AF.Identity, [pb[bk], b_bqk], [b_qs], bias=bqk[:, kq:kq + 1])
                else:
                    P.cp("act" if kq % 2 else "dve", qs[:, kq, :N], ps[:, bk, :N], [pb[bk]], [b_qs])
            t0 = grp[0][0]
            if is_na:
                for j, (t, np_) in enumerate(grp):
                    P.dma("sp", QTd[t][:, :, 0:np_], qs[:, 0:8, j * 128:j * 128 + np_], [b_qs], [self.b_QT])
                    P.dma("sp", KTd[t][:, :, 0:np_], qs[:, 8:16, j * 128:j * 128 + np_], [b_qs], [self.b_KT])
            else:
                P.dma("sp", QTd[:, :, t0 * 128:t0 * 128 + N], qs[:, 0:8, :N], [b_qs], [self.b_QT])
                P.dma("sp", KTd[:, :, t0 * 128:t0 * 128 + N], qs[:, 8:16, :N], [b_qs], [self.b_KT])
            for j, (t, np_) in enumerate(grp):
                sl = cnt % 2
                for half in range(2):
                    for k in range(8):
                        P.mm(ps[:np_, 4 + half, :], xnT[:, k, j * 128:j * 128 + np_],
                             Wq[:, k, 2 * D + half * 512:2 * D + (half + 1) * 512], k == 0, k == 7,
                             [b_xnT] + b_w, [pb[4 + half]])
                src = ps[:np_, 4:6, :]
                if is_na:
                    P.tt("dve", vs[sl][:np_], src, bv[:np_], ALU.add, [pb[4], pb[5], b_bv], [b_vs[sl]])
                else:
                    P.cp("dve", vs[sl][:np_], src, [pb[4], pb[5]], [b_vs[sl]])
                if is_na:
                    P.dma("sp", Vd[t][0:np_, :], vs[sl][:np_], [b_vs[sl]], [self.b_V])
                else:
                    P.dma("sp", Vd[0:np_, :, t, :], vs[sl][:np_].rearrange("p (h d) -> p h d", h=8), [b_vs[sl]], [self.b_V])
                cnt += 1
        P.barrier()

    def stage_gather(self, li):
        P, dr, cfg = self.P, self.dr, self.cfg
        groups = [[2 * i, 2 * i + 1] for i in range(cfg.B)]
        TCP, NGT = cfg.TCP, cfg.NGT
        CH = 131072

        def cc(src2d, dst2d, rb, wb):
            P.add("pool", lambda e: e.collective_compute("AllGather", ALU.bypass, replica_groups=groups,
                                                         ins=[src2d], outs=[dst2d]), [rb], [wb], dma=True, inc=1)

        if li % 2 == 0:
            for nm, bsrc, bdst in (("KT", self.b_KT, self.b_KTall), ("V", self.b_V, self.b_Vall)):
                lo = dr[nm][CH:3 * CH].rearrange("(a n) -> a n", n=1024)
                hi = dr[nm][(NGT - 1) * CH:(NGT + 1) * CH].rearrange("(a n) -> a n", n=1024)
                cc(lo, dr[nm + "lo_all"].rearrange("r (a n) -> (r a) n", n=1024), bsrc, Buf("x"))
                cc(hi, dr[nm + "hi_all"].rearrange("r (a n) -> (r a) n", n=1024), bsrc, Buf("x"))
        else:
            HS = 128 * TCP
            for nm, bsrc, bdst in (("KT", self.b_KT, self.b_KTall), ("V", self.b_V, self.b_Vall)):
                for h in range(8):
                    cc(dr[nm][h * HS:(h + 1) * HS].rearrange("(a n) -> a n", n=TCP),
                       dr[nm + "_all"][h].rearrange("r (a n) -> (r a) n", n=TCP), bsrc, Buf("x"))
        self.gather_bufs = None
        P.barrier([self.b_KT, self.b_V])

    def stage_attn(self, li):
        if li % 2 == 0:
            self.stage_na(li)
        else:
            self.stage_da(li)

    def stage_oproj(self, li, hsrc, hdst):
        cfg, P, A, ps, pb, dr = self.cfg, self.P, self.A, self.ps, self.pb, self.dr
        A.reset()
        is_na = (li % 2 == 0)
        jj = li // 2
        TCP = cfg.TCP
        Wo = A.alloc([8, D], BF16)
        b_w = P.bufs(8, "wo")
        self.load_w_cast(Wo, dr["na_w_o" if is_na else "da_w_o"][jj], 8, D, b_w)
        g3 = A.alloc([D], F32)
        b_g3 = P.buf("g3")
        self.load_rep(g3, dr["norm_g"][li, 3, :], b_g3)
        if is_na:
            bo = A.alloc([D], F32)
            b_bo = P.buf("bo")
            self.load_rep(bo, dr["na_b_o"][jj, :], b_bo)
        W = self.work_common()
        aoT = [A.alloc([8, 512], BF16) for _ in range(2)]
        b_ao = P.bufs(2, "aoT")
        msb = [A.alloc([D], F32) for _ in range(2)]
        b_msb = P.bufs(2, "msb")
        AOd = dr["AOT"].rearrange("(k p n) -> p k n", p=128, n=TCP)
        hname, hbufs = hsrc
        hd = dr[hname]
        cnt = 0
        for gi, grp in enumerate(cfg.groups()):
            slot = gi % 2
            hb, b_hb = W["hb"][slot], W["b_hb"][slot]
            N = sum(np_ for _, np_ in grp)
            t0 = grp[0][0]
            P.dma("sp", aoT[slot][:, :, :N], AOd[:, :, t0 * 128:t0 * 128 + N], [self.b_AOT], [b_ao[slot]])
            for j, (t, np_) in enumerate(grp):
                P.dma("sp", hb[:np_, j, :], hd[t * 128:t * 128 + np_, :], [hbufs[t]], [b_hb[j]])
            for j, (t, np_) in enumerate(grp):
                pd0 = 4 if cnt % 2 == 0 else 2
                for half in range(2):
                    for k in range(8):
                        P.mm(ps[:np_, pd0 + half, :], aoT[slot][:, k, j * 128:j * 128 + np_], Wo[:, k, half * 512:(half + 1) * 512],
                             k == 0, k == 7, [b_ao[slot]] + b_w, [pb[pd0 + half]])
                src = ps[:np_, pd0:pd0 + 2, :]
                srcb = [pb[pd0], pb[pd0 + 1]]
                if is_na:
                    sl = cnt % 2
                    P.tt("dve", msb[sl][:np_], src, bo[:np_], ALU.add, srcb + [b_bo], [b_msb[sl]])
                    src, srcb = msb[sl][:np_], [b_msb[sl]]
                self.resid_out(W, cnt, np_, src, srcb, hb[:np_, j, :], b_hb[j], g3, b_g3, 1.0, hdst, t)
                cnt += 1
        P.barrier()

    def stage_na(self, li):
        cfg, P, A, ps, pb, dr = self.cfg, self.P, self.A, self.ps, self.pb, self.dr
        A.reset()
        jj = li // 2
        TCP, NCH, NGT = cfg.TCP, cfg.NCH, cfg.NGT
        QTd = dr["QT"].rearrange("(c p k n) -> c p k n", p=128, k=8, n=128)
        KTo = dr["KT"].rearrange("(c p k n) -> c p k n", p=128, k=8, n=128)
        Vo = dr["V"].rearrange("(c p f) -> c p f", p=128, f=D)
        KTlo = dr["KTlo_all"].rearrange("r (c p k n) -> r c p k n", p=128, k=8, n=128)
        KThi = dr["KThi_all"].rearrange("r (c p k n) -> r c p k n", p=128, k=8, n=128)
        Vlo = dr["Vlo_all"].rearrange("r (c p f) -> r c p f", p=128, f=D)
        Vhi = dr["Vhi_all"].rearrange("r (c p f) -> r c p f", p=128, f=D)
        AOd = dr["AOT"].rearrange("(k p n) -> p k n", p=128, n=TCP)
        maskd = dr["na_mask"][jj]
        metab = A.alloc([16], F32)
        b_mb = P.buf("metab")
        P.dma("sp", metab[0:16, :], dr["na_meta_bias"][jj].rearrange("h m -> m h"), (), [b_mb], allow_slow_non_contiguous=True)
        KTm = A.alloc([8, 16], BF16)
        Vm = A.alloc([D], BF16)
        b_km = P.buf("kvm")
        P.dma("sp", KTm, KTo[0][:, :, 0:16], [self.b_KT], [b_km])
        P.dma("sp", Vm[0:16, :], Vo[0][0:16, :], [self.b_V], [b_km])
        mk = [A.alloc([16 * 6 * 128], F32) for _ in range(2)]
        b_mk = P.bufs(2, "mk")
        QTb = [A.alloc([8, 128], BF16) for _ in range(2)]
        b_q = P.bufs(2, "QTb")
        KTw = [A.alloc([6, 8 * 128], BF16) for _ in range(2)]
        Vw = [A.alloc([6, D], BF16) for _ in range(2)]
        b_kw = [P.bufs(6, "KTw") for _ in range(2)]
        b_vw = [P.bufs(6, "Vw") for _ in range(2)]
        sm = [A.alloc([6 * 128], F32) for _ in range(2)]
        b_sm = P.bufs(2, "sm")
        pt = [A.alloc([6 * 128], BF16) for _ in range(3)]
        b_pt = P.bufs(3, "pt")
        ptm = [A.alloc([128], BF16) for _ in range(3)]
        b_ptm = P.bufs(3, "ptm")
        rd = [A.alloc([128], F32) for _ in range(2)]
        b_rd = P.bufs(2, "rd")
        ao = [A.alloc([8, 128], BF16) for _ in range(2)]
        b_ao = P.bufs(2, "ao")

        blocks = [-1] + list(range(NGT))

        def issue_loads(bi):
            b = blocks[bi]
            sl = bi % 2
            chunk = 0 if b < 0 else b + 1
            nq = 16 if b < 0 else 128
            P.dma("sp", QTb[sl][:, :, :nq], QTd[chunk][:, :, 0:nq], [self.b_QT], [b_q[sl]])
            if b < 0:
                return
            offs = [-2, -1, 0, 1, 2] + ([3] if b == 0 else ([-3] if b == NGT - 1 else []))
            for s, of in enumerate(offs):
                l = b + of
                if l < 0:
                    ksrc, vsrc, rb = KThi[0][l + 2], Vhi[0][l + 2], [self.b_KTall, self.b_Vall]
                elif l >= NGT:
                    ksrc, vsrc, rb = KTlo[1][l - NGT], Vlo[1][l - NGT], [self.b_KTall, self.b_Vall]
                else:
                    ksrc, vsrc, rb = KTo[l + 1], Vo[l + 1], [self.b_KT, self.b_V]
                P.dma("sp", KTw[sl][:, s, :], ksrc.rearrange("p k n -> p (k n)"), rb, [b_kw[sl][s]])
                P.dma("pool", Vw[sl][:, s, :], vsrc, rb, [b_vw[sl][s]])

        cur_var = {"v": None, "slot": 0}

        def ensure_mask(b):
            v = cfg.var_of_block(b)
            if cur_var["v"] != v:
                cur_var["slot"] ^= 1
                cur_var["v"] = v
                P.dma("sp", mk[cur_var["slot"]], maskd[v], (), [b_mk[cur_var["slot"]]])
            return cur_var["slot"]

        issue_loads(0)
        hcount = 0
        for bi, b in enumerate(blocks):
            if bi + 1 < len(blocks):
                issue_loads(bi + 1)
            sl = bi % 2
            nq = 16 if b < 0 else 128
            nsl = 0 if b < 0 else (6 if b in (0, NGT - 1) else 5)
            if b >= 0:
                ms = ensure_mask(b)
                mkv = mk[ms].rearrange("p (h s q) -> p h s q", h=16, s=6)
            for k in range(8):
                bo = 4 + k % 2
                for hh in range(2):
                    h = 2 * k + hh
                    r0 = 64 * hh
                    st_ = hcount % 2
                    bx, by = 2 * st_, 2 * st_ + 1
                    pi = hcount % 3
                    hcount += 1
                    for s in range(nsl):
                        outp = ps[:, bx, s * 128:s * 128 + nq] if s < 4 else ps[:, by, (s - 4) * 128:(s - 4) * 128 + nq]
                        P.mm(outp, KTw[sl][r0:r0 + 64, s, k * 128:(k + 1) * 128], QTb[sl][r0:r0 + 64, k, :nq], True, True,
                             [b_kw[sl][s], b_q[sl]], [pb[bx] if s < 4 else pb[by]])
                    P.mm(ps[0:16, by, 256:256 + nq], KTm[r0:r0 + 64, k, :], QTb[sl][r0:r0 + 64, k, :nq], True, True,
                         [b_km, b_q[sl]], [pb[by]])
                    if nsl:
                        smv = sm[st_]
                        P.stt(smv[:, 0:512], ps[:, bx, :], 0.125, mkv[:, h, 0:4, :].rearrange("p s q -> p (s q)"),
                              ALU.mult, ALU.add, [pb[bx], b_mk[ms]], [b_sm[st_]])
                        P.stt(smv[:, 512:nsl * 128], ps[:, by, 0:(nsl - 4) * 128], 0.125,
                              mkv[:, h, 4:nsl, :].rearrange("p s q -> p (s q)"), ALU.mult, ALU.add,
                              [pb[by], b_mk[ms]], [b_sm[st_]])
                        P.act(pt[pi][:, 0:nsl * 128], smv[:, 0:nsl * 128], AF.Exp, [b_sm[st_]], [b_pt[pi]])
                    P.act(ptm[pi][0:16, :nq], ps[0:16, by, 256:256 + nq], AF.Exp, [pb[by], b_mb], [b_ptm[pi]],
                          bias=metab[0:16, h:h + 1], scale=0.125)
                    oc = 256 * hh
                    for which in range(2):
                        col = oc + 128 * which
                        for s in range(nsl):
                            lhs = Vw[sl][:, s, k * 128:(k + 1) * 128] if which == 0 else self.ones_b
                            rds = [b_vw[sl][s], b_pt[pi]] if which == 0 else [self.b_c, b_pt[pi]]
                            P.mm(ps[:, bo, col:col + nq], lhs, pt[pi][:, s * 128:s * 128 + nq], s == 0, False, rds, [pb[bo]])
                        lhs = Vm[0:16, k * 128:(k + 1) * 128] if which == 0 else self.ones_b[0:16, :]
                        P.mm(ps[:, bo, col:col + nq], lhs, ptm[pi][0:16, :nq], nsl == 0, True,
                             [b_km, self.b_c, b_ptm[pi]], [pb[bo]])
                rs = k % 2
                P.recip(rd[rs][0:64, :nq], ps[0:64, bo, 128:128 + nq], [pb[bo]], [b_rd[rs]])
                P.recip(rd[rs][64:128, :nq], ps[64:128, bo, 384:384 + nq], [pb[bo]], [b_rd[rs]])
                P.tt("dve", ao[sl][0:64, k, :nq], ps[0:64, bo, 0:nq], rd[rs][0:64, :nq], ALU.mult, [pb[bo], b_rd[rs]], [b_ao[sl]])
                P.tt("dve", ao[sl][64:128, k, :nq], ps[64:128, bo, 256:256 + nq], rd[rs][64:128, :nq], ALU.mult,
                     [pb[bo], b_rd[rs]], [b_ao[sl]])
            col0 = 0 if b < 0 else (b + 1) * 128
            P.dma("sp", AOd[:, :, col0:col0 + nq], ao[sl][:, :, :nq], [b_ao[sl]], [self.b_AOT])
        P.barrier()

    def stage_da(self, li):
        cfg, P, A, ps, pb, dr = self.cfg, self.P, self.A, self.ps, self.pb, self.dr
        A.reset()
        jj = li // 2
        TCP, NCH, NGT, NKC, NQ = cfg.TCP, cfg.NCH, cfg.NGT, cfg.NKC, cfg.NQ
        lam_init = 0.8 - 0.6 * math.exp(-0.3 * li)
        QTd = dr["QT"].rearrange("(k p n) -> k p n", p=128, n=TCP)
        KTa = dr["KT_all"].rearrange("k r (p n) -> k p r n", p=128)
        Va = dr["V_all"].rearrange("h r (p c d) -> h p r c d", p=128, c=NCH)
        AOd = dr["AOT"].rearrange("(k p n) -> k p n", p=128, n=TCP)
        lam = A.alloc([4, 64], F32)
        prod = A.alloc([2, 64], F32)
        sc = A.alloc([8], F32)
        b_l = P.buf("lam")
        P.dma("sp", lam, dr["da_lambda"][jj].rearrange("(a d) -> a d", a=4).partition_broadcast(128), (), [b_l])
        P.tt("dve", prod[:, 0, :], lam[:, 0, :], lam[:, 1, :], ALU.mult, [b_l], [b_l])
        P.tt("dve", prod[:, 1, :], lam[:, 2, :], lam[:, 3, :], ALU.mult, [b_l], [b_l])
        P.add("dve", lambda e: e.tensor_reduce(out=sc[:, 0:2], in_=prod, axis=mybir.AxisListType.X, op=ALU.add), [b_l], [b_l])
        P.act(sc[:, 2:4], sc[:, 0:2], AF.Exp, [b_l], [b_l])
        P.tt("dve", sc[:, 4:5], sc[:, 3:4], sc[:, 2:3], ALU.subtract, [b_l], [b_l])
        P.add("dve", lambda e: e.tensor_scalar(out=sc[:, 5:6], in0=sc[:, 4:5], scalar1=-lam_init, scalar2=None, op0=ALU.add), [b_l], [b_l])
        neglam = sc[:, 5:6]
        P.dma("sp", sc[:, 6:7], dr["da_subln_g"][jj].rearrange("(p o) -> p o", o=1), (), [b_l])
        P.add("dve", lambda e: e.tensor_scalar(out=sc[:, 7:8], in0=sc[:, 6:7], scalar1=1.0 - lam_init, scalar2=None, op0=ALU.mult), [b_l], [b_l])
        gsc = sc[:, 7:8]
        KTh = [A.alloc([2, TCP], BF16) for _ in range(2)]
        Vh = [A.alloc([2 * NCH, 128], BF16) for _ in range(2)]
        QTh = [A.alloc([TCP], BF16) for _ in range(2)]
        Gh = [A.alloc([2, 1152], F32) for _ in range(2)]
        Gmh = [A.alloc([512], F32) for _ in range(2)]
        Gxh = [A.alloc([2, 512], F32) for _ in range(2)]
        cbh = [A.alloc([(NQ + 1) * NKC], F32) for _ in range(2)]
        Bmh = [A.alloc([NKC, 16], F32) for _ in range(2)]
        b_hd = [P.bufs(8, "hd") for _ in range(2)]
        pt = [[A.alloc([512], BF16) for _ in range(3)] for _ in range(2)]
        b_pt = [P.bufs(3, "pt") for _ in range(2)]
        tmpb = [A.alloc([512], F32) for _ in range(2)]
        b_tmpb = P.bufs(2, "tmpb")
        accd = A.alloc([2, 512], F32)
        b_accd = P.bufs(2, "accd")
        rden = A.alloc([2, 512], F32)
        b_rden = P.bufs(2, "rden")
        o0 = A.alloc([512], F32)
        o1 = A.alloc([512], F32)
        sq = A.alloc([512], F32)
        rt = A.alloc([512], F32)
        b_o = P.bufs(4, "o")
        aob = [A.alloc([512], BF16) for _ in range(2)]
        b_aob = P.bufs(2, "aob")

        def load_head(h):
            sl = h % 2
            bh = b_hd[sl]
            P.dma("sp", KTh[sl], KTa[h], [self.b_KTall], [bh[0]])
            P.dma("pool", Vh[sl].rearrange("p (r c) d -> p r c d", r=2), Va[h], [self.b_Vall], [bh[1]])
            P.dma("sp", QTh[sl], QTd[h], [self.b_QT], [bh[2]])
            P.dma("sp", Gh[sl].rearrange("p r x -> p (r x)"), dr["da_G"][h], (), [bh[3]])
            P.dma("sp", Gmh[sl][0:16, :], dr["da_Gm"][h], (), [bh[4]])
            P.dma("sp", Gxh[sl].rearrange("p r x -> p (r x)"), dr["da_Gx"][h], (), [bh[7]])
            P.dma("sp", cbh[sl], dr["da_cb"][h], (), [bh[5]])
            P.dma("sp", Bmh[sl].rearrange("p c q -> p (c q)"), dr["da_Bm"][h], (), [bh[6]])

        load_head(0)
        it = 0
        oc = 0
        for h in range(8):
            if h + 1 < 8:
                load_head(h + 1)
            sl = h % 2
            bh = b_hd[sl]
            for qc in range(NQ + 1):
                N = 16 if qc == 0 else 512
                q0 = 0 if qc == 0 else 128 + (qc - 1) * 512
                P.memset("pool", accd[:, 0, :N], 0.0, [b_accd[0]])
                P.memset("pool", accd[:, 1, :N], 0.0, [b_accd[1]])

                def kinfo(kc):
                    if kc == 0:
                        return 16, 0, 0
                    r, jl = (kc - 1) // NGT, (kc - 1) % NGT
                    return 128, r, jl + 1

                def qk(kc, itn):
                    nk, r, ch = kinfo(kc)
                    st_ = itn % 2
                    for s in range(2):
                        bk = 2 * st_ + s
                        P.mm(ps[:nk, bk, :N], KTh[sl][64 * s:64 * s + 64, r, ch * 128:ch * 128 + nk],
                             QTh[sl][64 * s:64 * s + 64, q0:q0 + N], True, True, [bh[0], bh[2]], [pb[bk]])

                qk(0, it)
                for kc in range(NKC):
                    if kc + 1 < NKC:
                        qk(kc + 1, it + 1)
                    nk, r, ch = kinfo(kc)
                    st_ = it % 2
                    pi = it % 3
                    it += 1
                    table = None
                    if qc == 0:
                        table, tb = Bmh[sl][:nk, kc, :], bh[6]
                    elif kc == 0:
                        if qc == 1:
                            table, tb = Gmh[sl][0:16, :], bh[4]
                    else:
                        d = (ch - 1) - 4 * (qc - 1)
                        if -1 <= d <= 4:
                            table, tb = Gh[sl][:, r, 512 - 128 * d:1024 - 128 * d], bh[3]
                        elif qc == 1 and r == 0 and ch == NGT:
                            table, tb = Gxh[sl][:, 0, :], bh[7]
                        elif qc == NQ and r == 1 and ch == 1:
                            table, tb = Gxh[sl][:, 1, :], bh[7]
                    for s in range(2):
                        bk = 2 * st_ + s
                        if table is not None:
                            P.stt(tmpb[s][:nk, :N], ps[:nk, bk, :N], 0.125, table, ALU.mult, ALU.add, [pb[bk], tb], [b_tmpb[s]])
                            P.act(pt[s][pi][:nk, :N], tmpb[s][:nk, :N], AF.Exp, [b_tmpb[s]], [b_pt[s][pi]])
                        else:
                            ci = qc * NKC + kc
                            P.act(pt[s][pi][:nk, :N], ps[:nk, bk, :N], AF.Exp, [pb[bk], bh[5]], [b_pt[s][pi]],
                                  bias=cbh[sl][:nk, ci:ci + 1], scale=0.125)
                    for s in range(2):
                        P.mm(ps[:, 4 + s, :N], Vh[sl][:nk, r * NCH + ch, :], pt[s][pi][:nk, :N], kc == 0, kc == NKC - 1,
                             [bh[1], b_pt[s][pi]], [pb[4 + s]])
                    P.tt("dve", accd[:nk, 0, :N], accd[:nk, 0, :N], pt[0][pi][:nk, :N], ALU.add, [b_accd[0], b_pt[0][pi]], [b_accd[0]])
                    P.tt("pool", accd[:nk, 1, :N], accd[:nk, 1, :N], pt[1][pi][:nk, :N], ALU.add, [b_accd[1], b_pt[1][pi]], [b_accd[1]])
                for s in range(2):
                    P.mm(ps[:, 6 + s, :N], self.ones_f, accd[:, s, :N], True, True, [self.b_c, b_accd[s]], [pb[6 + s]])
                    P.recip(rden[:, s, :N], ps[:, 6 + s, :N], [pb[6 + s]], [b_rden[s]])
                P.tt("dve", o0[:, :N], ps[:, 4, :N], rden[:, 0, :N], ALU.mult, [pb[4], b_rden[0]], [b_o[0]])
                P.stt(o1[:, :N], ps[:, 5, :N], neglam, rden[:, 1, :N], ALU.mult, ALU.mult, [pb[5], b_rden[1], b_l], [b_o[1]])
                P.tt("dve", o0[:, :N], o0[:, :N], o1[:, :N], ALU.add, [b_o[0], b_o[1]], [b_o[0]])
                P.act(sq[:, :N], o0[:, :N], AF.Square, [b_o[0]], [b_o[2]])
                P.mm(ps[:, 6, :N], self.ones_f, sq[:, :N], True, True, [self.b_c, b_o[2]], [pb[6]])
                P.act(rt[:, :N], ps[:, 6, :N], AF.Sqrt, [pb[6], self.b_c], [b_o[3]], bias=self.eps, scale=1.0 / 128)
                P.recip(rt[:, :N], rt[:, :N], [b_o[3]], [b_o[3]])
                ob = oc % 2
                oc += 1
                P.stt(aob[ob][:, :N], o0[:, :N], gsc, rt[:, :N], ALU.mult, ALU.mult, [b_o[0], b_o[3], b_l], [b_aob[ob]])
                P.dma("sp", AOd[h][:, q0:q0 + N], aob[ob][:, :N], [b_aob[ob]], [self.b_AOT])
        P.barrier()


PARAM_NAMES = ["norm_g", "ffn_w_gate", "ffn_w_up", "ffn_w_down", "na_w_qkv", "na_b_qkv", "na_w_o", "na_b_o",
               "na_meta_bias", "da_w_qkv", "da_w_o", "da_lambda", "da_subln_g"]

MODE = "fused"
_last_ninst = None


def run_forward(cfg, inputs, mode=None):
    mode = mode or MODE
    B, SEQ, DEPTH = cfg.B, cfg.SEQ, cfg.DEPTH
    ncores = 2 * B
    x = np.asarray(inputs["x"], np.float32)
    meta = np.asarray(inputs["meta_tokens"], np.float32)
    params = {}
    for nm in PARAM_NAMES:
        a = np.ascontiguousarray(np.asarray(inputs[nm], np.float32))
        if nm == "da_lambda":
            a = a.reshape(a.shape[0], 256)
        params[nm] = a
    rpb = np.asarray(inputs["na_rpb"], np.float32)
    tbl = np.asarray(inputs["t5_rel_bias"], np.float32)
    per_core = []
    for c in range(ncores):
        b, half = c // 2, c % 2
        d = dict(params)
        d["na_mask"] = build_na_mask(cfg, rpb, half).reshape(cfg.NLA, cfg.NVAR, 128, 16 * 6 * 128)
        if cfg.NLB:
            G, Gm, cb, Bm, Gx = build_da_tables(cfg, tbl, half)
            d["da_G"] = G.reshape(8, 128, 2 * 1152)
            d["da_Gm"], d["da_cb"], d["da_Bm"], d["da_Gx"] = Gm, cb, Bm, Gx
        else:
            for nm in ("da_w_qkv", "da_w_o", "da_lambda", "da_subln_g"):
                d.pop(nm, None)
        h0 = np.zeros((cfg.TCP, D), np.float32)
        h0[:NMETA] = meta
        h0[128:] = x[b, half * cfg.NG:(half + 1) * cfg.NG]
        d["h_in"] = h0
        per_core.append(d)
    global _last_ninst
    if mode == "fused":
        bld = Builder(cfg, "fused", None)
        nc = bld.build()
        _last_ninst = bld.ninst
        in_maps = [{k: per_core[c][k] for k in bld.in_names} for c in range(ncores)]
        res = run_bass_kernel_spmd(nc, in_maps, core_ids=list(range(ncores)))
        outs = [res.results[c]["out"] for c in range(ncores)]
    else:
        state = [dict() for _ in range(ncores)]
        for seg in range(DEPTH + 1):
            bld = Builder(cfg, "multi", seg)
            nc = bld.build()
            _last_ninst = bld.ninst
            in_maps = []
            for c in range(ncores):
                m = {}
                for k in bld.in_names:
                    m[k] = state[c][k] if k in state[c] else per_core[c][k]
                in_maps.append(m)
            res = run_bass_kernel_spmd(nc, in_maps, core_ids=list(range(ncores)))
            if seg < DEPTH:
                for c in range(ncores):
                    r = res.results[c]
                    state[c] = {"h_in": r["h_out"], "QT": r["QT_o"], "KT": r["KT_o"], "V": r["V_o"]}
                CH = 131072
                HS = 128 * cfg.TCP
                for b in range(B):
                    c0, c1 = state[2 * b], state[2 * b + 1]
                    g = {}
                    if seg % 2 == 1:
                        for nm in ("KT", "V"):
                            a0, a1 = np.asarray(c0[nm]).reshape(8, HS), np.asarray(c1[nm]).reshape(8, HS)
                            g[nm + "_all"] = np.ascontiguousarray(np.stack([a0, a1], axis=1))
                    else:
                        for nm in ("KT", "V"):
                            a0, a1 = np.asarray(c0[nm]), np.asarray(c1[nm])
                            g[nm + "lo_all"] = np.stack([a0[CH:3 * CH], a1[CH:3 * CH]])
                            g[nm + "hi_all"] = np.stack([a0[(cfg.NGT - 1) * CH:(cfg.NGT + 1) * CH],
                                                         a1[(cfg.NGT - 1) * CH:(cfg.NGT + 1) * CH]])
                    for c in (2 * b, 2 * b + 1):
                        state[c].update(g)
            else:
                outs = [res.results[c]["out"] for c in range(ncores)]
    out = np.zeros((B, SEQ, D), np.float32)
    for c in range(ncores):
        b, half = c // 2, c % 2
        out[b, half * cfg.NG:(half + 1) * cfg.NG] = outs[c]
    return out


def kernel(**inputs):
    cfg = Cfg()
    return run_forward(cfg, inputs)
```

```python
import math
from contextlib import ExitStack

import numpy as np
import concourse.bass as bass
import concourse.mybir as mybir
from concourse.bass_utils import run_bass_kernel_spmd

F32 = mybir.dt.float32
BF16 = mybir.dt.bfloat16
AF = mybir.ActivationFunctionType
ALU = mybir.AluOpType

D = 1024
NMETA = 16
GW = 64
RMS_EPS = 1e-6
NEG = -30000.0

ENGS = ("pe", "act", "dve", "pool", "sp")
EPOCH = 30000
NDMA_SEMS = {"sp": 24, "pool": 16, "act": 4, "pe": 2, "dve": 2}


class Buf:
    __slots__ = ("name", "last_w", "rd_eng", "rd_dma", "excl")

    def __init__(self, name, excl=False):
        self.name = name
        self.excl = excl
        self.last_w = None
        self.rd_eng = {}
        self.rd_dma = []


class Op:
    __slots__ = ("eng", "fn", "dma", "deps", "signal", "token", "idx", "prev_dma", "inc")

    def __init__(self, eng, fn, dma):
        self.eng = eng
        self.fn = fn
        self.dma = dma
        self.inc = 16
        self.deps = []
        self.signal = False
        self.token = None
        self.prev_dma = None


class Arena:
    def __init__(self, t, nbytes):
        self.t = t
        self.nbytes = nbytes
        self.off = 0
        self.base = 0

    def alloc(self, shape, dtype):
        n = 1
        for s in shape:
            n *= s
        esz = 4 if dtype == F32 else 2
        nb = (n * esz + 31) // 32 * 32
        assert self.off + nb <= self.nbytes, ("arena overflow", self.off, nb)
        v = self.t[:, self.off // 4:(self.off + nb) // 4]
        self.off += nb
        if dtype != F32:
            v = v.bitcast(dtype)
        v = v[:, 0:n]
        if len(shape) == 2:
            return v.rearrange("p (a b) -> p a b", b=shape[1])
        if len(shape) == 3:
            return v.rearrange("p (a b c) -> p a b c", b=shape[1], c=shape[2])
        return v

    def mark(self):
        self.base = self.off

    def reset(self):
        self.off = self.base


class Prog:
    def __init__(self, nc, stack):
        self.nc = nc
        self.stack = stack
        self.ops = []
        self.live = []

    def buf(self, name="b"):
        b = Buf(name)
        self.live.append(b)
        return b

    def bufs(self, n, name="b"):
        return [self.buf(name) for _ in range(n)]

    def add(self, eng, fn, reads=(), writes=(), dma=False, inc=16):
        op = Op(eng, fn, dma)
        op.inc = inc
        op.idx = len(self.ops)
        deps = {}
        xr = [b for b in reads if b.excl]
        if xr:
            writes = list(writes) + [b for b in xr if b not in writes]
        for b in reads:
            if b.last_w is not None:
                deps[b.last_w.idx] = (b.last_w, True)
        for b in writes:
            if b.last_w is not None and b.last_w.idx not in deps:
                deps[b.last_w.idx] = (b.last_w, False)
            for r in b.rd_eng.values():
                if r.idx not in deps:
                    deps[r.idx] = (r, False)
            for r in b.rd_dma:
                if r.idx not in deps:
                    deps[r.idx] = (r, False)
        for p, raw in deps.values():
            if p is op:
                continue
            same = (p.eng == op.eng) and (not p.dma) and (not op.dma)
            if same and (op.eng == "pe" or not raw):
                continue
            op.deps.append(p)
            p.signal = True
        if fn is not None:
            for b in writes:
                b.last_w = op
                b.rd_eng = {}
                b.rd_dma = []
            for b in reads:
                if b.last_w is not op:
                    if dma:
                        b.rd_dma.append(op)
                    else:
                        b.rd_eng[eng] = op
        self.ops.append(op)
        return op

    def dma(self, eng, out, in_, reads=(), writes=(), **kw):
        return self.add(eng, lambda e: e.dma_start(out=out, in_=in_, **kw), reads, writes, dma=True)

    def barrier(self, extra=()):
        bl = list(self.live) + list(extra)
        for e in ENGS:
            self.add(e, None, reads=bl, writes=bl)
        self.live = []

    def mm(self, out, lhsT, rhs, start, stop, reads, writes):
        return self.add("pe", lambda e: e.matmul(out=out, lhsT=lhsT, rhs=rhs, start=start, stop=stop), reads, writes)

    def tr(self, out, in_, ident, reads, writes):
        return self.add("pe", lambda e: e.transpose(out=out, in_=in_, identity=ident), reads, writes)

    def act(self, out, in_, func, reads, writes, **kw):
        return self.add("act", lambda e: e.activation(out=out, in_=in_, func=func, **kw), reads, writes)

    def stt(self, out, in0, scalar, in1, op0, op1, reads, writes):
        return self.add("dve", lambda e: e.scalar_tensor_tensor(out=out, in0=in0, scalar=scalar, in1=in1, op0=op0, op1=op1), reads, writes)

    def tt(self, eng, out, in0, in1, op, reads, writes):
        return self.add(eng, lambda e: e.tensor_tensor(out=out, in0=in0, in1=in1, op=op), reads, writes)

    def cp(self, eng, out, in_, reads, writes):
        if eng == "act":
            return self.add("act", lambda e: e.copy(out=out, in_=in_), reads, writes)
        return self.add(eng, lambda e: e.tensor_copy(out=out, in_=in_), reads, writes)

    def recip(self, out, in_, reads, writes):
        return self.add("dve", lambda e: e.reciprocal(out=out, in_=in_), reads, writes)

    def memset(self, eng, ap, val, writes):
        return self.add(eng, lambda e: e.memset(ap, val), (), writes)

    def emit(self):
        nc = self.nc
        st = self.stack
        cnt = {e: 0 for e in ENGS}
        epoch_sems = {e: [] for e in ENGS}
        dma_sems = {e: [] for e in ENGS}
        dma_use = {e: [] for e in ENGS}
        dma_last = {e: [] for e in ENGS}
        dma_rr = {e: 0 for e in ENGS}
        for op in self.ops:
            if op.fn is None:
                continue
            e = op.eng
            if op.dma and op.inc == 1:
                sem = st.enter_context(nc.semaphore(f"cc_{op.idx}"))
                op.token = (sem, 1)
                op.signal = True
            elif op.dma:
                if not dma_sems[e]:
                    for j in range(NDMA_SEMS[e]):
                        dma_sems[e].append(st.enter_context(nc.semaphore(f"d_{e}_{j}")))
                        dma_use[e].append(0)
                        dma_last[e].append(None)
                j = dma_rr[e] % len(dma_sems[e])
                dma_rr[e] += 1
                dma_use[e][j] += 1
                op.prev_dma = dma_last[e][j]
                op.token = (dma_sems[e][j], 16 * dma_use[e][j])
                dma_last[e][j] = op.token
                op.signal = True
            elif op.signal:
                k = cnt[e] // EPOCH
                if k >= len(epoch_sems[e]):
                    epoch_sems[e].append(st.enter_context(nc.semaphore(f"s_{e}_{k}")))
                cnt[e] += 1
                op.token = (epoch_sems[e][k], cnt[e] - k * EPOCH)
        per_eng = {e: [] for e in ENGS}
        for op in self.ops:
            per_eng[op.eng].append(op)
        ninst = {e: 0 for e in ENGS}

        def run(e, eo):
            waited = {}
            for op in per_eng[e]:
                toks = [p.token for p in op.deps]
                if op.prev_dma is not None:
                    toks.append(op.prev_dma)
                need = {}
                for (s, v) in toks:
                    key = id(s)
                    if waited.get(key, 0) >= v:
                        continue
                    if key not in need or need[key][1] < v:
                        need[key] = (s, v)
                for key, (s, v) in need.items():
                    eo.wait_ge(s, v)
                    waited[key] = v
                    ninst[e] += 1
                if op.fn is None:
                    continue
                inst = op.fn(eo)
                ninst[e] += 1
                if op.signal:
                    inst.then_inc(op.token[0], op.inc if op.dma else 1)

        with nc.Block() as block:
            @block.tensor
            def _(eo):
                run("pe", eo)

            @block.scalar
            def _(eo):
                run("act", eo)

            @block.vector
            def _(eo):
                run("dve", eo)

            @block.gpsimd
            def _(eo):
                run("pool", eo)

            @block.sync
            def _(eo):
                run("sp", eo)
        self.ninst = ninst


class Cfg:
    def __init__(self, SEQ=8192, DEPTH=4, DFF=2816, B=4):
        self.SEQ, self.DEPTH, self.DFF, self.B = SEQ, DEPTH, DFF, B
        self.ROWS = SEQ // GW
        self.NG = SEQ // 2
        self.NGT = self.NG // 128
        self.NCH = self.NGT + 1
        self.TCP = self.NCH * 128
        self.FCH = DFF // 128
        self.NQ = self.NGT // 4
        self.NKC = 1 + 2 * self.NGT
        self.NLA = (DEPTH + 1) // 2
        self.NLB = DEPTH // 2
        self.NVAR = 5
        assert self.NGT % 4 == 0 and self.NGT >= 8

    def groups(self, gs=4):
        g = [[(0, NMETA)]]
        for c in range(self.NGT // gs):
            g.append([(1 + gs * c + j, 128) for j in range(gs)])
        return g

    def var_of_block(self, b):
        if b == 0:
            return 0
        if b == 1:
            return 1
        if b == self.NGT - 2:
            return 3
        if b == self.NGT - 1:
            return 4
        return 2

    def rep_block(self, v):
        return [0, 1, 2, self.NGT - 2, self.NGT - 1][v]


def t5_bucket_np(rel):
    nb, me = 16, 8
    rel = np.asarray(rel, np.int64)
    ret = np.where(rel > 0, nb, 0)
    n = np.abs(rel)
    nf = np.maximum(n, 1).astype(np.float32)
    large = me + (np.log(nf / np.float32(me)) / np.float32(math.log(128 / me)) * np.float32(nb - me)).astype(np.int32)
    large = np.minimum(large, nb - 1)
    return ret + np.where(n < me, n, large)


def na_mask_index(cfg, half):
    NB = 2 * cfg.NGT
    rows = cfg.ROWS
    kh = min(8, rows)
    p = np.arange(128)
    out = np.full((cfg.NVAR, 6, 128, 128), 465, np.int64)
    for v in range(cfg.NVAR):
        i = half * cfg.NGT + cfg.rep_block(v)
        qr = 2 * i + p // 64
        qc = p % 64
        rs = np.clip(qr - kh // 2, 0, rows - kh)
        cs = np.clip(qc - 8, 0, GW - 16)
        offs = [-2, -1, 0, 1, 2, 3 if v == 0 else (-3 if v == 4 else None)]
        for s in range(6):
            if offs[s] is None:
                continue
            g = i + offs[s]
            if g < 0 or g >= NB:
                continue
            kr = 2 * g + p // 64
            kc = p % 64
            vis = ((kr[:, None] >= rs[None, :]) & (kr[:, None] < rs[None, :] + kh)
                   & (kc[:, None] >= cs[None, :]) & (kc[:, None] < cs[None, :] + 16))
            dy = kr[:, None] - qr[None, :] + 7
            dx = kc[:, None] - qc[None, :] + 15
            idx = dy * 31 + dx
            out[v, s] = np.where(vis, idx, 465)
    return out


def build_na_mask(cfg, rpb, half):
    idx = na_mask_index(cfg, half)
    ext = np.concatenate([rpb.reshape(rpb.shape[0], 16, 465),
                          np.full((rpb.shape[0], 16, 1), NEG, np.float32)], axis=2)
    m = ext[:, :, idx]
    return np.ascontiguousarray(m.transpose(0, 2, 4, 1, 3, 5)).astype(np.float32)


def build_da_tables(cfg, tbl, half):
    NG, NGT, NKC, NQ = cfg.NG, cfg.NGT, cfg.NKC, cfg.NQ
    p = np.arange(128)[:, None]
    x = np.arange(1152)[None, :]
    tblT = tbl.T
    G = np.zeros((8, 128, 2, 1152), np.float32)
    for r in range(2):
        rel = (r - half) * NG + p - x + 512
        G[:, :, r, :] = tblT[:, t5_bucket_np(rel)]
    m = np.arange(16)[:, None]
    qf = np.arange(512)[None, :]
    Gm = tblT[:, t5_bucket_np(m - (16 + half * NG + qf))].astype(np.float32)
    pp = np.arange(128)[:, None]
    Gx = np.zeros((8, 128, 2, 512), np.float32)
    Gx[:, :, 0, :] = tblT[:, t5_bucket_np((0 - half) * NG + 128 * (NGT - 1) + pp - qf)]
    Gx[:, :, 1, :] = tblT[:, t5_bucket_np((1 - half) * NG + pp - 512 * (NQ - 1) - qf)]
    cb = np.zeros((8, NQ + 1, NKC), np.float32)
    for qc in range(1, NQ + 1):
        qpos = 16 + half * NG + (qc - 1) * 512
        for kc in range(NKC):
            kpos = 0 if kc == 0 else 16 + (kc - 1) * 128
            cb[:, qc, kc] = tblT[:, t5_bucket_np(kpos - qpos)]
    cb = np.ascontiguousarray(np.broadcast_to(cb.reshape(8, 1, -1), (8, 128, (NQ + 1) * NKC)))
    kp = np.zeros((128, NKC), np.int64)
    kp[:, 0] = np.arange(128)
    for kc in range(1, NKC):
        kp[:, kc] = 16 + (kc - 1) * 128 + np.arange(128)
    q = np.arange(16)[None, None, :]
    Bm = tblT[:, t5_bucket_np(kp[:, :, None] - q)].astype(np.float32)
    return G, np.ascontiguousarray(Gm), cb, np.ascontiguousarray(Bm.reshape(8, 128, NKC * 16)), Gx.reshape(8, 128, 1024)


class Builder:
    def __init__(self, cfg, mode, seg):
        self.cfg = cfg
        self.mode = mode
        self.seg = seg
        self.nc = bass.Bass("TRN2", target_bir_lowering=False)
        self.dr = {}
        self.in_names = []
        self.out_names = []

    def dram(self, name, shape, dtype, kind):
        t = self.nc.dram_tensor(name, list(shape), dtype, kind=kind)
        self.dr[name] = t.ap()
        if kind == "ExternalInput":
            self.in_names.append(name)
        elif kind == "ExternalOutput":
            self.out_names.append(name)
        return self.dr[name]

    def build(self):
        cfg = self.cfg
        nc = self.nc
        DEPTH, DFF = cfg.DEPTH, cfg.DFF
        seg, fused = self.seg, self.mode == "fused"
        with ExitStack() as st:
            P = Prog(nc, st)
            self.P = P
            at = st.enter_context(nc.sbuf_tensor("arena", [128, 200 * 256], F32))
            self.A = Arena(at, 200 * 1024)
            self.ps = st.enter_context(nc.psum_tensor("ps", [128, 8, 512], F32))
            self.pb = [Buf(f"bank{i}", excl=True) for i in range(8)]
            EI = "ExternalInput"
            self.dram("norm_g", [DEPTH, 6, D], F32, EI)
            self.dram("ffn_w_gate", [DEPTH, 2, D, DFF], F32, EI)
            self.dram("ffn_w_up", [DEPTH, 2, D, DFF], F32, EI)
            self.dram("ffn_w_down", [DEPTH, 2, DFF, D], F32, EI)
            self.dram("na_w_qkv", [cfg.NLA, D, 3 * D], F32, EI)
            self.dram("na_b_qkv", [cfg.NLA, 3 * D], F32, EI)
            self.dram("na_w_o", [cfg.NLA, D, D], F32, EI)
            self.dram("na_b_o", [cfg.NLA, D], F32, EI)
            self.dram("na_meta_bias", [cfg.NLA, 16, 16], F32, EI)
            self.dram("na_mask", [cfg.NLA, cfg.NVAR, 128, 16 * 6 * 128], F32, EI)
            if cfg.NLB:
                self.dram("da_w_qkv", [cfg.NLB, D, 3 * D], F32, EI)
                self.dram("da_w_o", [cfg.NLB, D, D], F32, EI)
                self.dram("da_lambda", [cfg.NLB, 4 * 64], F32, EI)
                self.dram("da_subln_g", [cfg.NLB, 128], F32, EI)
                self.dram("da_G", [8, 128, 2 * 1152], F32, EI)
                self.dram("da_Gm", [8, 16, 512], F32, EI)
                self.dram("da_Gx", [8, 128, 1024], F32, EI)
                self.dram("da_cb", [8, 128, (cfg.NQ + 1) * cfg.NKC], F32, EI)
                self.dram("da_Bm", [8, 128, cfg.NKC * 16], F32, EI)
            TCP, NCH = cfg.TCP, cfg.NCH
            self.dram("h_in", [TCP, D], F32, EI)
            self.b_hin = [Buf("hin") for _ in range(NCH)]
            if fused:
                self.dram("h", [TCP, D], F32, "Internal")
                self.dram("out", [cfg.NG, D], F32, "ExternalOutput")
                for nm in ("QT", "KT", "AOT"):
                    self.dram(nm, [8 * 128 * TCP], BF16, "Internal")
                self.dram("V", [8 * 128 * TCP], BF16, "Internal")
                CH = 131072
                for nm, shp in (("KT_all", [8, 2, 128 * TCP]), ("V_all", [8, 2, 128 * TCP]),
                                ("KTlo_all", [2, 2 * CH]), ("KThi_all", [2, 2 * CH]),
                                ("Vlo_all", [2, 2 * CH]), ("Vhi_all", [2, 2 * CH])):
                    t = self.nc.dram_tensor(nm, shp, BF16, kind="Internal", addr_space="Local")
                    self.dr[nm] = t.ap()
            else:
                if seg > 0:
                    self.dram("QT", [8 * 128 * TCP], BF16, EI)
                    self.dram("KT", [8 * 128 * TCP], BF16, EI)
                    self.dram("V", [8 * 128 * TCP], BF16, EI)
                    if (seg - 1) % 2 == 1:
                        self.dram("KT_all", [8, 2, 128 * TCP], BF16, EI)
                        self.dram("V_all", [8, 2, 128 * TCP], BF16, EI)
                    else:
                        for nm in ("KTlo_all", "KThi_all", "Vlo_all", "Vhi_all"):
                            self.dram(nm, [2, 2 * 131072], BF16, EI)
                    import os
                    self.attn_only = "attnonly" in os.environ.get("KDBG1", "")
                    self.dram("AOT", [8 * 128 * TCP], BF16, "ExternalOutput" if self.attn_only else "Internal")
                    self.dram("h", [TCP, D], F32, "Internal")
                if seg > 0 and self.attn_only:
                    pass
                elif seg < DEPTH:
                    self.dram("h_out", [TCP, D], F32, "ExternalOutput")
                    self.dram("QT_o", [8 * 128 * TCP], BF16, "ExternalOutput")
                    self.dram("KT_o", [8 * 128 * TCP], BF16, "ExternalOutput")
                    self.dram("V_o", [8 * 128 * TCP], BF16, "ExternalOutput")
                else:
                    self.dram("out", [cfg.NG, D], F32, "ExternalOutput")
            self.b_h = [Buf("h") for _ in range(NCH)]
            self.b_hout = [Buf("hout") for _ in range(NCH)]
            self.b_out = [Buf("out") for _ in range(NCH)]
            self.b_QT, self.b_KT, self.b_V, self.b_AOT = Buf("QT"), Buf("KT"), Buf("V"), Buf("AOT")
            self.b_KTall, self.b_Vall = Buf("KTall"), Buf("Vall")

            self.consts()
            dr = self.dr
            if fused:
                hcur = ("h_in", self.b_hin)
                for i in range(DEPTH):
                    self.stage_ffn(i, 0, hcur, ("h", self.b_h))
                    hcur = ("h", self.b_h)
                    self.stage_qkv(i, hcur, "QT", "KT", "V")
                    self.stage_gather(i)
                    self.stage_attn(i)
                    self.stage_oproj(i, hcur, hcur)
                    last = (i == DEPTH - 1)
                    self.stage_ffn(i, 1, hcur, ("out", self.b_out) if last else hcur)
                P.barrier(self.b_out)
            else:
                hcur = ("h_in", self.b_hin)
                if seg > 0:
                    i = seg - 1
                    self.stage_attn(i)
                    if self.attn_only:
                        P.barrier([self.b_AOT])
                        P.emit()
                        self.ninst = P.ninst
                        return nc
                    self.stage_oproj(i, hcur, ("h", self.b_h))
                    hcur = ("h", self.b_h)
                    if seg == DEPTH:
                        self.stage_ffn(i, 1, hcur, ("out", self.b_out))
                    else:
                        self.stage_ffn(i, 1, hcur, hcur)
                if seg < DEPTH:
                    import os
                    dbg = os.environ.get("KDBG", "ffn,qkv")
                    if "ffn" in dbg:
                        self.stage_ffn(seg, 0, hcur, ("h_out", self.b_hout))
                    if "qkv" in dbg:
                        self.stage_qkv(seg, ("h_out", self.b_hout) if "ffn" in dbg else hcur, "QT_o", "KT_o", "V_o")
                P.barrier(self.b_out + self.b_hout + [self.b_QT, self.b_KT, self.b_V])
            P.emit()
            self.ninst = P.ninst
        return nc

    def consts(self):
        P, A = self.P, self.A
        self.b_c = P.buf("consts")
        self.identf = A.alloc([128], F32)
        self.ident = A.alloc([128], BF16)
        self.ones_f = A.alloc([128], F32)
        self.ones_b = A.alloc([128], BF16)
        self.eps = A.alloc([1], F32)
        b = self.b_c
        P.memset("pool", self.identf, 0.0, [b])
        P.add("pool", lambda e: e.affine_select(out=self.identf, in_=self.identf, pattern=[[-1, 128]],
                                                compare_op=ALU.not_equal, fill=1.0, base=0,
                                                channel_multiplier=1), [b], [b])
        P.cp("dve", self.ident, self.identf, [b], [b])
        P.memset("dve", self.ones_f, 1.0, [b])
        P.memset("dve", self.ones_b, 1.0, [b])
        P.memset("dve", self.eps, RMS_EPS, [b])
        A.mark()

    def load_w_cast(self, dst, src, rows_chunks, ncols, bufs):
        P = self.P
        step = ncols
        while step > 2048:
            step //= 2
        i = 0
        for k in range(rows_chunks):
            for c0 in range(0, ncols, step):
                P.dma("pool", dst[:, k, c0:c0 + step], src[k * 128:(k + 1) * 128, c0:c0 + step], (), [bufs[i]])
                i += 1
        return i

    def load_rep(self, dst, vec, b):
        self.P.dma("sp", dst, vec.partition_broadcast(128), (), [b])

    def rstd_from_ss(self, ss, rstd, np_, n, b_ss, b_rstd):
        P = self.P
        P.act(rstd[:np_], ss[:np_], AF.Sqrt, [b_ss, self.b_c], [b_rstd], bias=self.eps[:np_], scale=1.0 / n)
        P.recip(rstd[:np_], rstd[:np_], [b_rstd], [b_rstd])

    def norm_T(self, W, grp, hsrc, g_rep, b_g, gi):
        P, ps, pb = self.P, self.ps, self.pb
        hname, hbufs = hsrc
        hd = self.dr[hname]
        slot = gi % 2
        hb, b_hb = W["hb"][slot], W["b_hb"][slot]
        xnT, b_xnT = W["xnT"][slot], W["b_xnT"][slot]
        N = 0
        ptr = ps[:, 7, :].bitcast(BF16).rearrange("p (k n) -> p k n", n=128)
        for j, (t, np_) in enumerate(grp):
            P.dma("sp", hb[:np_, j, :], hd[t * 128:t * 128 + np_, :], [hbufs[t]], [b_hb[j]])
            sl = (gi * len(grp) + j) % 2
            ss, rstd, xn = W["ss"][sl], W["rstd"][sl], W["xn"][sl]
            b_ss, b_rstd, b_xn = W["b_ss"][sl], W["b_rstd"][sl], W["b_xn"][sl]
            P.act(W["junk"][:np_], hb[:np_, j, :], AF.Square, [b_hb[j]], [W["b_junk"], b_ss], accum_out=ss[:np_])
            self.rstd_from_ss(ss, rstd, np_, D, b_ss, b_rstd)
            P.stt(xn[:np_], hb[:np_, j, :], rstd[:np_], g_rep[:np_], ALU.mult, ALU.mult, [b_hb[j], b_rstd, b_g], [b_xn])
            for k in range(8):
                P.tr(ptr[:, k, :np_], xn[:np_, k * 128:(k + 1) * 128], self.ident[:np_, :np_], [b_xn, self.b_c], [pb[7]])
            P.cp("dve" if j % 2 else "act", xnT[:, :, N:N + np_], ptr[:, :, :np_], [pb[7]], [b_xnT])
            N += np_
        return N

    def work_common(self, gs=4, lean=False):
        P, A = self.P, self.A
        W = {}
        hb0 = A.alloc([gs, D], F32)
        W["hb"] = [hb0, hb0 if lean else A.alloc([gs, D], F32)]
        bh0 = P.bufs(gs, "hb")
        W["b_hb"] = [bh0, bh0 if lean else P.bufs(gs, "hb")]
        W["xnT"] = [A.alloc([8, gs * 128], BF16) for _ in range(2)]
        W["b_xnT"] = P.bufs(2, "xnT")
        W["xn"] = [A.alloc([D], BF16) for _ in range(2)]
        W["b_xn"] = P.bufs(2, "xn")
        W["ss"] = [A.alloc([1], F32) for _ in range(2)]
        W["b_ss"] = P.bufs(2, "ss")
        W["rstd"] = [A.alloc([1], F32) for _ in range(2)]
        W["b_rstd"] = P.bufs(2, "rstd")
        W["junk"] = A.alloc([D], BF16)
        W["b_junk"] = P.buf("junk")
        tmp0 = A.alloc([D], F32)
        W["tmp"] = [tmp0, tmp0 if lean else A.alloc([D], F32)]
        bt0 = P.buf("tmp")
        W["b_tmp"] = [bt0, bt0 if lean else P.buf("tmp")]
        W["ho"] = [A.alloc([D], F32) for _ in range(2)]
        W["b_ho"] = P.bufs(2, "ho")
        W["ss2"] = [A.alloc([1], F32) for _ in range(2)]
        W["b_ss2"] = P.bufs(2, "ss2")
        W["rstd2"] = [A.alloc([1], F32) for _ in range(2)]
        W["b_rstd2"] = P.bufs(2, "rstd2")
        return W

    def dst_rows(self, hdst, t, np_):
        name, bufs = hdst
        d = self.dr[name]
        if name == "out":
            if t == 0:
                return None, None
            return d[(t - 1) * 128:(t - 1) * 128 + np_, :], bufs[t]
        return d[t * 128:t * 128 + np_, :], bufs[t]

    def resid_out(self, W, cnt, np_, src, src_bufs, hb_j, b_hb_j, g_rep, b_g, scale, hdst, t):
        P = self.P
        sl = cnt % 2
        ss2, rstd2, tmp, ho = W["ss2"][sl], W["rstd2"][sl], W["tmp"][sl], W["ho"][sl]
        b_ss2, b_rstd2, b_tmp, b_ho = W["b_ss2"][sl], W["b_rstd2"][sl], W["b_tmp"][sl], W["b_ho"][sl]
        P.act(W["junk"][:np_], src, AF.Square, src_bufs, [W["b_junk"], b_ss2], accum_out=ss2[:np_])
        self.rstd_from_ss(ss2, rstd2, np_, D, b_ss2, b_rstd2)
        P.stt(tmp[:np_], src, rstd2[:np_], g_rep[:np_], ALU.mult, ALU.mult, list(src_bufs) + [b_rstd2, b_g], [b_tmp])
        P.stt(ho[:np_], tmp[:np_], float(scale), hb_j, ALU.mult, ALU.add, [b_tmp, b_hb_j], [b_ho])
        dst, bd = self.dst_rows(hdst, t, np_)
        if dst is not None:
            P.dma("sp", dst, ho[:np_], [b_ho], [bd])

    def stage_ffn(self, li, w, hsrc, hdst):
        cfg, P, A, ps, pb, dr = self.cfg, self.P, self.A, self.ps, self.pb, self.dr
        A.reset()
        FCH, DFF = cfg.FCH, cfg.DFF
        Wg = A.alloc([8, DFF], BF16)
        Wu = A.alloc([8, DFF], BF16)
        Wd = A.alloc([FCH, D], BF16)
        b_wg, b_wu, b_wd = P.bufs(32, "wg"), P.bufs(32, "wu"), P.bufs(FCH, "wd")
        n = self.load_w_cast(Wg, dr["ffn_w_gate"][li, w], 8, DFF, b_wg)
        b_wg = b_wg[:n]
        n = self.load_w_cast(Wu, dr["ffn_w_up"][li, w], 8, DFF, b_wu)
        b_wu = b_wu[:n]
        self.load_w_cast(Wd, dr["ffn_w_down"][li, w], FCH, D, b_wd)
        gin, gout = A.alloc([D], F32), A.alloc([D], F32)
        b_gin, b_gout = P.buf("gin"), P.buf("gout")
        self.load_rep(gin, dr["norm_g"][li, 4 * w, :], b_gin)
        self.load_rep(gout, dr["norm_g"][li, 4 * w + 1, :], b_gout)
        gs = 2 if DFF > 2048 else 4
        W = self.work_common(gs, lean=(gs == 2))
        HT = A.alloc([FCH, gs * 128], BF16)
        b_HT = P.buf("HT")
        sg = [A.alloc([gs * 128], F32) for _ in range(2)]
        b_sg = P.bufs(2, "sg")
        cnt = 0
        import os
        kparts = os.environ.get("KPARTS", "norm,gu,down")
        kgroups = int(os.environ.get("KGROUPS", "1000"))
        for gi, grp in enumerate(cfg.groups(gs)):
            if gi >= kgroups:
                break
            N = self.norm_T(W, grp, hsrc, gin, b_gin, gi)
            slot = gi % 2
            xnT, b_xnT = W["xnT"][slot], W["b_xnT"][slot]
            hb, b_hb = W["hb"][slot], W["b_hb"][slot]
            if "gu" not in kparts:
                continue
            for f in range(FCH):
                s2 = f % 2
                bg, bu = 2 * s2, 2 * s2 + 1
                for k in range(8):
                    P.mm(ps[:, bg, :N], Wg[:, k, f * 128:(f + 1) * 128], xnT[:, k, :N], k == 0, k == 7,
                         [b_xnT] + b_wg, [pb[bg]])
                for k in range(8):
                    P.mm(ps[:, bu, :N], Wu[:, k, f * 128:(f + 1) * 128], xnT[:, k, :N], k == 0, k == 7,
                         [b_xnT] + b_wu, [pb[bu]])
                P.act(sg[s2][:, :N], ps[:, bg, :N], AF.Silu, [pb[bg]], [b_sg[s2]])
                P.tt("dve", HT[:, f, :N], sg[s2][:, :N], ps[:, bu, :N], ALU.mult, [b_sg[s2], pb[bu]], [b_HT])
            if "down" not in kparts:
                continue
            for j, (t, np_) in enumerate(grp):
                pd0 = 4 if cnt % 2 == 0 else 2
                for half in range(2):
                    for f in range(FCH):
                        P.mm(ps[:np_, pd0 + half, :], HT[:, f, j * 128:j * 128 + np_], Wd[:, f, half * 512:(half + 1) * 512],
                             f == 0, f == FCH - 1, [b_HT, b_wd[f]], [pb[pd0 + half]])
                src = ps[:np_, pd0:pd0 + 2, :]
                self.resid_out(W, cnt, np_, src, [pb[pd0], pb[pd0 + 1]], hb[:np_, j, :], b_hb[j], gout, b_gout, 0.5, hdst, t)
                cnt += 1
        P.barrier()

    def stage_qkv(self, li, hsrc, nQ, nK, nV):
        cfg, P, A, ps, pb, dr = self.cfg, self.P, self.A, self.ps, self.pb, self.dr
        A.reset()
        is_na = (li % 2 == 0)
        jj = li // 2
        TCP, NCH = cfg.TCP, cfg.NCH
        Wq = A.alloc([8, 3 * D], BF16)
        b_w = P.bufs(16, "wqkv")
        self.load_w_cast(Wq, dr["na_w_qkv" if is_na else "da_w_qkv"][jj], 8, 3 * D, b_w)
        g2 = A.alloc([D], F32)
        b_g2 = P.buf("g2")
        self.load_rep(g2, dr["norm_g"][li, 2, :], b_g2)
        if is_na:
            bqk = A.alloc([16], F32)
            b_bqk = P.buf("bqk")
            for kq in range(16):
                P.dma("sp", bqk[:, kq:kq + 1], dr["na_b_qkv"][jj, kq * 128:(kq + 1) * 128].rearrange("(p o) -> p o", o=1),
                      (), [b_bqk])
            bv = A.alloc([D], F32)
            b_bv = P.buf("bv")
            self.load_rep(bv, dr["na_b_qkv"][jj, 2 * D:3 * D], b_bv)
        W = self.work_common()
        qk = [A.alloc([16, 512], BF16) for _ in range(2)]
        b_qk = P.bufs(2, "qk")
        vs = [A.alloc([D], BF16) for _ in range(2)]
        b_vs = P.bufs(2, "vs")
        if is_na:
            QTd = dr[nQ].rearrange("(c p k n) -> c p k n", p=128, k=8, n=128)
            KTd = dr[nK].rearrange("(c p k n) -> c p k n", p=128, k=8, n=128)
            Vd = dr[nV].rearrange("(c p f) -> c p f", p=128, f=D)
        else:
            QTd = dr[nQ].rearrange("(k p n) -> p k n", p=128, n=TCP)
            KTd = dr[nK].rearrange("(k p n) -> p k n", p=128, n=TCP)
            Vd = dr[nV].rearrange("(h p c d) -> p h c d", p=128, c=NCH, d=128)
        cnt = 0
        for gi, grp in enumerate(cfg.groups()):
            N = self.norm_T(W, grp, hsrc, g2, b_g2, gi)
            slot = gi % 2
            xnT, b_xnT = W["xnT"][slot], W["b_xnT"][slot]
            qs, b_qs = qk[slot], b_qk[slot]
            for kq in range(16):
                bk = kq % 4
                for k in range(8):
                    P.mm(ps[:, bk, :N], Wq[:, k, kq * 128:(kq + 1) * 128], xnT[:, k, :N], k == 0, k == 7,
                         [b_xnT] + b_w, [pb[bk]])
                if is_na:
                    P.act(qs[:, kq, :N], ps[:, bk, :N], AF.Identity, [pb[bk], b_bqk], [b_qs], bias=bqk[:, kq:kq + 1])
                else:
                    P.cp("act" if kq % 2 else "dve", qs[:, kq, :N], ps[:, bk, :N], [pb[bk]], [b_qs])
            t0 = grp[0][0]
            if is_na:
                for j, (t, np_) in enumerate(grp):
                    P.dma("sp", QTd[t][:, :, 0:np_], qs[:, 0:8, j * 128:j * 128 + np_], [b_qs], [self.b_QT])
                    P.dma("sp", KTd[t][:, :, 0:np_], qs[:, 8:16, j * 128:j * 128 + np_], [b_qs], [self.b_KT])
            else:
                P.dma("sp", QTd[:, :, t0 * 128:t0 * 128 + N], qs[:, 0:8, :N], [b_qs], [self.b_QT])
                P.dma("sp", KTd[:, :, t0 * 128:t0 * 128 + N], qs[:, 8:16, :N], [b_qs], [self.b_KT])
            for j, (t, np_) in enumerate(grp):
                sl = cnt % 2
                for half in range(2):
                    for k in range(8):
                        P.mm(ps[:np_, 4 + half, :], xnT[:, k, j * 128:j * 128 + np_],
                             Wq[:, k, 2 * D + half * 512:2 * D + (half + 1) * 512], k == 0, k == 7,
                             [b_xnT] + b_w, [pb[4 + half]])
                src = ps[:np_, 4:6, :]
                if is_na:
                    P.tt("dve", vs[sl][:np_], src, bv[:np_], ALU.add, [pb[4], pb[5], b_bv], [b_vs[sl]])
                else:
                    P.cp("dve", vs[sl][:np_], src, [pb[4], pb[5]], [b_vs[sl]])
                if is_na:
                    P.dma("sp", Vd[t][0:np_, :], vs[sl][:np_], [b_vs[sl]], [self.b_V])
                else:
                    P.dma("sp", Vd[0:np_, :, t, :], vs[sl][:np_].rearrange("p (h d) -> p h d", h=8), [b_vs[sl]], [self.b_V])
                cnt += 1
        P.barrier()

    def stage_gather(self, li):
        P, dr, cfg = self.P, self.dr, self.cfg
        groups = [[2 * i, 2 * i + 1] for i in range(cfg.B)]
        TCP, NGT = cfg.TCP, cfg.NGT
        CH = 131072

        def cc(src2d, dst2d, rb, wb):
            P.add("pool", lambda e: e.collective_compute("AllGather", ALU.bypass, replica_groups=groups,
                                                         ins=[src2d], outs=[dst2d]), [rb], [wb], dma=True, inc=1)

        if li % 2 == 0:
            for nm, bsrc, bdst in (("KT", self.b_KT, self.b_KTall), ("V", self.b_V, self.b_Vall)):
                lo = dr[nm][CH:3 * CH].rearrange("(a n) -> a n", n=1024)
                hi = dr[nm][(NGT - 1) * CH:(NGT + 1) * CH].rearrange("(a n) -> a n", n=1024)
                cc(lo, dr[nm + "lo_all"].rearrange("r (a n) -> (r a) n", n=1024), bsrc, Buf("x"))
                cc(hi, dr[nm + "hi_all"].rearrange("r (a n) -> (r a) n", n=1024), bsrc, Buf("x"))
        else:
            HS = 128 * TCP
            for nm, bsrc, bdst in (("KT", self.b_KT, self.b_KTall), ("V", self.b_V, self.b_Vall)):
                for h in range(8):
                    cc(dr[nm][h * HS:(h + 1) * HS].rearrange("(a n) -> a n", n=TCP),
                       dr[nm + "_all"][h].rearrange("r (a n) -> (r a) n", n=TCP), bsrc, Buf("x"))
        self.gather_bufs = None
        P.barrier([self.b_KT, self.b_V])

    def stage_attn(self, li):
        if li % 2 == 0:
            self.stage_na(li)
        else:
            self.stage_da(li)

    def stage_oproj(self, li, hsrc, hdst):
        cfg, P, A, ps, pb, dr = self.cfg, self.P, self.A, self.ps, self.pb, self.dr
        A.reset()
        is_na = (li % 2 == 0)
        jj = li // 2
        TCP = cfg.TCP
        Wo = A.alloc([8, D], BF16)
        b_w = P.bufs(8, "wo")
        self.load_w_cast(Wo, dr["na_w_o" if is_na else "da_w_o"][jj], 8, D, b_w)
        g3 = A.alloc([D], F32)
        b_g3 = P.buf("g3")
        self.load_rep(g3, dr["norm_g"][li, 3, :], b_g3)
        if is_na:
            bo = A.alloc([D], F32)
            b_bo = P.buf("bo")
            self.load_rep(bo, dr["na_b_o"][jj, :], b_bo)
        W = self.work_common()
        aoT = [A.alloc([8, 512], BF16) for _ in range(2)]
        b_ao = P.bufs(2, "aoT")
        msb = [A.alloc([D], F32) for _ in range(2)]
        b_msb = P.bufs(2, "msb")
        AOd = dr["AOT"].rearrange("(k p n) -> p k n", p=128, n=TCP)
        hname, hbufs = hsrc
        hd = dr[hname]
        cnt = 0
        for gi, grp in enumerate(cfg.groups()):
            slot = gi % 2
            hb, b_hb = W["hb"][slot], W["b_hb"][slot]
            N = sum(np_ for _, np_ in grp)
            t0 = grp[0][0]
            P.dma("sp", aoT[slot][:, :, :N], AOd[:, :, t0 * 128:t0 * 128 + N], [self.b_AOT], [b_ao[slot]])
            for j, (t, np_) in enumerate(grp):
                P.dma("sp", hb[:np_, j, :], hd[t * 128:t * 128 + np_, :], [hbufs[t]], [b_hb[j]])
            for j, (t, np_) in enumerate(grp):
                pd0 = 4 if cnt % 2 == 0 else 2
                for half in range(2):
                    for k in range(8):
                        P.mm(ps[:np_, pd0 + half, :], aoT[slot][:, k, j * 128:j * 128 + np_], Wo[:, k, half * 512:(half + 1) * 512],
                             k == 0, k == 7, [b_ao[slot]] + b_w, [pb[pd0 + half]])
                src = ps[:np_, pd0:pd0 + 2, :]
                srcb = [pb[pd0], pb[pd0 + 1]]
                if is_na:
                    sl = cnt % 2
                    P.tt("dve", msb[sl][:np_], src, bo[:np_], ALU.add, srcb + [b_bo], [b_msb[sl]])
                    src, srcb = msb[sl][:np_], [b_msb[sl]]
                self.resid_out(W, cnt, np_, src, srcb, hb[:np_, j, :], b_hb[j], g3, b_g3, 1.0, hdst, t)
                cnt += 1
        P.barrier()

    def stage_na(self, li):
        cfg, P, A, ps, pb, dr = self.cfg, self.P, self.A, self.ps, self.pb, self.dr
        A.reset()
        jj = li // 2
        TCP, NCH, NGT = cfg.TCP, cfg.NCH, cfg.NGT
        QTd = dr["QT"].rearrange("(c p k n) -> c p k n", p=128, k=8, n=128)
        KTo = dr["KT"].rearrange("(c p k n) -> c p k n", p=128, k=8, n=128)
        Vo = dr["V"].rearrange("(c p f) -> c p f", p=128, f=D)
        KTlo = dr["KTlo_all"].rearrange("r (c p k n) -> r c p k n", p=128, k=8, n=128)
        KThi = dr["KThi_all"].rearrange("r (c p k n) -> r c p k n", p=128, k=8, n=128)
        Vlo = dr["Vlo_all"].rearrange("r (c p f) -> r c p f", p=128, f=D)
        Vhi = dr["Vhi_all"].rearrange("r (c p f) -> r c p f", p=128, f=D)
        AOd = dr["AOT"].rearrange("(k p n) -> p k n", p=128, n=TCP)
        maskd = dr["na_mask"][jj]
        metab = A.alloc([16], F32)
        b_mb = P.buf("metab")
        P.dma("sp", metab[0:16, :], dr["na_meta_bias"][jj].rearrange("h m -> m h"), (), [b_mb], allow_slow_non_contiguous=True)
        KTm = A.alloc([8, 16], BF16)
        Vm = A.alloc([D], BF16)
        b_km = P.buf("kvm")
        P.dma("sp", KTm, KTo[0][:, :, 0:16], [self.b_KT], [b_km])
        P.dma("sp", Vm[0:16, :], Vo[0][0:16, :], [self.b_V], [b_km])
        mk = [A.alloc([16 * 6 * 128], F32) for _ in range(2)]
        b_mk = P.bufs(2, "mk")
        QTb = [A.alloc([8, 128], BF16) for _ in range(2)]
        b_q = P.bufs(2, "QTb")
        KTw = [A.alloc([6, 8 * 128], BF16) for _ in range(2)]
        Vw = [A.alloc([6, D], BF16) for _ in range(2)]
        b_kw = [P.bufs(6, "KTw") for _ in range(2)]
        b_vw = [P.bufs(6, "Vw") for _ in range(2)]
        sm = [A.alloc([6 * 128], F32) for _ in range(2)]
        b_sm = P.bufs(2, "sm")
        pt = [A.alloc([6 * 128], BF16) for _ in range(3)]
        b_pt = P.bufs(3, "pt")
        ptm = [A.alloc([128], BF16) for _ in range(3)]
        b_ptm = P.bufs(3, "ptm")
        rd = [A.alloc([128], F32) for _ in range(2)]
        b_rd = P.bufs(2, "rd")
        ao = [A.alloc([8, 128], BF16) for _ in range(2)]
        b_ao = P.bufs(2, "ao")

        blocks = [-1] + list(range(NGT))

        def issue_loads(bi):
            b = blocks[bi]
            sl = bi % 2
            chunk = 0 if b < 0 else b + 1
            nq = 16 if b < 0 else 128
            P.dma("sp", QTb[sl][:, :, :nq], QTd[chunk][:, :, 0:nq], [self.b_QT], [b_q[sl]])
            if b < 0:
                return
            offs = [-2, -1, 0, 1, 2] + ([3] if b == 0 else ([-3] if b == NGT - 1 else []))
            for s, of in enumerate(offs):
                l = b + of
                if l < 0:
                    ksrc, vsrc, rb = KThi[0][l + 2], Vhi[0][l + 2], [self.b_KTall, self.b_Vall]
                elif l >= NGT:
                    ksrc, vsrc, rb = KTlo[1][l - NGT], Vlo[1][l - NGT], [self.b_KTall, self.b_Vall]
                else:
                    ksrc, vsrc, rb = KTo[l + 1], Vo[l + 1], [self.b_KT, self.b_V]
                P.dma("sp", KTw[sl][:, s, :], ksrc.rearrange("p k n -> p (k n)"), rb, [b_kw[sl][s]])
                P.dma("pool", Vw[sl][:, s, :], vsrc, rb, [b_vw[sl][s]])

        cur_var = {"v": None, "slot": 0}

        def ensure_mask(b):
            v = cfg.var_of_block(b)
            if cur_var["v"] != v:
                cur_var["slot"] ^= 1
                cur_var["v"] = v
                P.dma("sp", mk[cur_var["slot"]], maskd[v], (), [b_mk[cur_var["slot"]]])
            return cur_var["slot"]

        issue_loads(0)
        hcount = 0
        for bi, b in enumerate(blocks):
            if bi + 1 < len(blocks):
                issue_loads(bi + 1)
            sl = bi % 2
            nq = 16 if b < 0 else 128
            nsl = 0 if b < 0 else (6 if b in (0, NGT - 1) else 5)
            if b >= 0:
                ms = ensure_mask(b)
                mkv = mk[ms].rearrange("p (h s q) -> p h s q", h=16, s=6)
            heads = []
            for hi in range(16):
                st_ = hcount % 2
                heads.append((hi, st_, hcount % 3))
                hcount += 1

            def emit_qk(hi, st_):
                k, hh = hi // 2, hi % 2
                r0 = 64 * hh
                bx, by = 2 * st_, 2 * st_ + 1
                for s in range(nsl):
                    outp = ps[:, bx, s * 128:s * 128 + nq] if s < 4 else ps[:, by, (s - 4) * 128:(s - 4) * 128 + nq]
                    P.mm(outp, KTw[sl][r0:r0 + 64, s, k * 128:(k + 1) * 128], QTb[sl][r0:r0 + 64, k, :nq], True, True,
                         [b_kw[sl][s], b_q[sl]], [pb[bx] if s < 4 else pb[by]])
                P.mm(ps[0:16, by, 256:256 + nq], KTm[r0:r0 + 64, k, :], QTb[sl][r0:r0 + 64, k, :nq], True, True,
                     [b_km, b_q[sl]], [pb[by]])

            emit_qk(heads[0][0], heads[0][1])
            for (hi, st_, pi) in heads:
                if hi + 1 < 16:
                    emit_qk(heads[hi + 1][0], heads[hi + 1][1])
                k, hh = hi // 2, hi % 2
                h = hi
                r0 = 64 * hh
                bx, by = 2 * st_, 2 * st_ + 1
                bo = 4 + k % 2
                if nsl:
                    smv = sm[st_]
                    P.stt(smv[:, 0:512], ps[:, bx, :], 0.125, mkv[:, h, 0:4, :].rearrange("p s q -> p (s q)"),
                          ALU.mult, ALU.add, [pb[bx], b_mk[ms]], [b_sm[st_]])
                    P.stt(smv[:, 512:nsl * 128], ps[:, by, 0:(nsl - 4) * 128], 0.125,
                          mkv[:, h, 4:nsl, :].rearrange("p s q -> p (s q)"), ALU.mult, ALU.add,
                          [pb[by], b_mk[ms]], [b_sm[st_]])
                    P.act(pt[pi][:, 0:nsl * 128], smv[:, 0:nsl * 128], AF.Exp, [b_sm[st_]], [b_pt[pi]])
                P.act(ptm[pi][0:16, :nq], ps[0:16, by, 256:256 + nq], AF.Exp, [pb[by], b_mb], [b_ptm[pi]],
                      bias=metab[0:16, h:h + 1], scale=0.125)
                for which in range(2):
                    col = 128 * which
                    for s in range(nsl):
                        lhs = Vw[sl][:, s, k * 128 + r0:k * 128 + r0 + 64] if which == 0 else self.ones_b[:, 0:64]
                        rds = [b_vw[sl][s], b_pt[pi]] if which == 0 else [self.b_c, b_pt[pi]]
                        P.mm(ps[r0:r0 + 64, bo, col:col + nq], lhs, pt[pi][:, s * 128:s * 128 + nq], s == 0, False, rds, [pb[bo]])
                    lhs = Vm[0:16, k * 128 + r0:k * 128 + r0 + 64] if which == 0 else self.ones_b[0:16, 0:64]
                    P.mm(ps[r0:r0 + 64, bo, col:col + nq], lhs, ptm[pi][0:16, :nq], nsl == 0, True,
                         [b_km, self.b_c, b_ptm[pi]], [pb[bo]])
                if hh == 1:
                    rs = k % 2
                    P.recip(rd[rs][:, :nq], ps[:, bo, 128:128 + nq], [pb[bo]], [b_rd[rs]])
                    P.tt("dve", ao[sl][:, k, :nq], ps[:, bo, 0:nq], rd[rs][:, :nq], ALU.mult, [pb[bo], b_rd[rs]], [b_ao[sl]])
            col0 = 0 if b < 0 else (b + 1) * 128
            P.dma("sp", AOd[:, :, col0:col0 + nq], ao[sl][:, :, :nq], [b_ao[sl]], [self.b_AOT])
        P.barrier()

    def stage_da(self, li):
        cfg, P, A, ps, pb, dr = self.cfg, self.P, self.A, self.ps, self.pb, self.dr
        A.reset()
        jj = li // 2
        TCP, NCH, NGT, NKC, NQ = cfg.TCP, cfg.NCH, cfg.NGT, cfg.NKC, cfg.NQ
        lam_init = 0.8 - 0.6 * math.exp(-0.3 * li)
        QTd = dr["QT"].rearrange("(k p n) -> k p n", p=128, n=TCP)
        KTa = dr["KT_all"].rearrange("k r (p n) -> k p r n", p=128)
        Va = dr["V_all"].rearrange("h r (p c d) -> h p r c d", p=128, c=NCH)
        AOd = dr["AOT"].rearrange("(k p n) -> k p n", p=128, n=TCP)
        lam = A.alloc([4, 64], F32)
        prod = A.alloc([2, 64], F32)
        sc = A.alloc([8], F32)
        b_l = P.buf("lam")
        P.dma("sp", lam, dr["da_lambda"][jj].rearrange("(a d) -> a d", a=4).partition_broadcast(128), (), [b_l])
        P.tt("dve", prod[:, 0, :], lam[:, 0, :], lam[:, 1, :], ALU.mult, [b_l], [b_l])
        P.tt("dve", prod[:, 1, :], lam[:, 2, :], lam[:, 3, :], ALU.mult, [b_l], [b_l])
        P.add("dve", lambda e: e.tensor_reduce(out=sc[:, 0:2], in_=prod, axis=mybir.AxisListType.X, op=ALU.add), [b_l], [b_l])
        P.act(sc[:, 2:4], sc[:, 0:2], AF.Exp, [b_l], [b_l])
        P.tt("dve", sc[:, 4:5], sc[:, 3:4], sc[:, 2:3], ALU.subtract, [b_l], [b_l])
        P.add("dve", lambda e: e.tensor_scalar(out=sc[:, 5:6], in0=sc[:, 4:5], scalar1=-lam_init, scalar2=None, op0=ALU.add), [b_l], [b_l])
        neglam = sc[:, 5:6]
        P.dma("sp", sc[:, 6:7], dr["da_subln_g"][jj].rearrange("(p o) -> p o", o=1), (), [b_l])
        P.add("dve", lambda e: e.tensor_scalar(out=sc[:, 7:8], in0=sc[:, 6:7], scalar1=1.0 - lam_init, scalar2=None, op0=ALU.mult), [b_l], [b_l])
        gsc = sc[:, 7:8]
        KTh = [A.alloc([2, TCP], BF16) for _ in range(2)]
        Vh = [A.alloc([2 * NCH, 128], BF16) for _ in range(2)]
        QTh = [A.alloc([TCP], BF16) for _ in range(2)]
        Gh = [A.alloc([2, 1152], F32) for _ in range(2)]
        Gmh = [A.alloc([512], F32) for _ in range(2)]
        Gxh = [A.alloc([2, 512], F32) for _ in range(2)]
        cbh = [A.alloc([(NQ + 1) * NKC], F32) for _ in range(2)]
        Bmh = [A.alloc([NKC, 16], F32) for _ in range(2)]
        b_hd = [P.bufs(8, "hd") for _ in range(2)]
        pt = [A.alloc([2, 512], BF16) for _ in range(3)]
        b_pt = P.bufs(3, "pt")
        tmpb = [A.alloc([2, 512], F32) for _ in range(2)]
        b_tmpb = P.bufs(2, "tmpb")
        rden = A.alloc([2, 512], F32)
        b_rden = P.bufs(2, "rden")
        o0 = A.alloc([512], F32)
        o1 = A.alloc([512], F32)
        sq = A.alloc([512], F32)
        rt = A.alloc([512], F32)
        b_o = P.bufs(4, "o")
        aob = [A.alloc([512], BF16) for _ in range(2)]
        b_aob = P.bufs(2, "aob")

        def load_head(h):
            sl = h % 2
            bh = b_hd[sl]
            P.dma("sp", KTh[sl], KTa[h], [self.b_KTall], [bh[0]])
            P.dma("pool", Vh[sl].rearrange("p (r c) d -> p r c d", r=2), Va[h], [self.b_Vall], [bh[1]])
            P.dma("sp", QTh[sl], QTd[h], [self.b_QT], [bh[2]])
            P.dma("sp", Gh[sl].rearrange("p r x -> p (r x)"), dr["da_G"][h], (), [bh[3]])
            P.dma("sp", Gmh[sl][0:16, :], dr["da_Gm"][h], (), [bh[4]])
            P.dma("sp", Gxh[sl].rearrange("p r x -> p (r x)"), dr["da_Gx"][h], (), [bh[7]])
            P.dma("sp", cbh[sl], dr["da_cb"][h], (), [bh[5]])
            P.dma("sp", Bmh[sl].rearrange("p c q -> p (c q)"), dr["da_Bm"][h], (), [bh[6]])

        load_head(0)
        it = 0
        oc = 0
        for h in range(8):
            if h + 1 < 8:
                load_head(h + 1)
            sl = h % 2
            bh = b_hd[sl]
            for qc in range(NQ + 1):
                N = 16 if qc == 0 else 512
                q0 = 0 if qc == 0 else 128 + (qc - 1) * 512

                def kinfo(kc):
                    if kc == 0:
                        return 16, 0, 0
                    r, jl = (kc - 1) // NGT, (kc - 1) % NGT
                    return 128, r, jl + 1

                def qk(kc, itn):
                    nk, r, ch = kinfo(kc)
                    st_ = itn % 2
                    for s in range(2):
                        bk = 2 * st_ + s
                        P.mm(ps[:nk, bk, :N], KTh[sl][64 * s:64 * s + 64, r, ch * 128:ch * 128 + nk],
                             QTh[sl][64 * s:64 * s + 64, q0:q0 + N], True, True, [bh[0], bh[2]], [pb[bk]])

                qk(0, it)
                for kc in range(NKC):
                    if kc + 1 < NKC:
                        qk(kc + 1, it + 1)
                    nk, r, ch = kinfo(kc)
                    st_ = it % 2
                    pi = it % 3
                    it += 1
                    table = None
                    if qc == 0:
                        table, tb = Bmh[sl][:nk, kc, :], bh[6]
                    elif kc == 0:
                        if qc == 1:
                            table, tb = Gmh[sl][0:16, :], bh[4]
                    else:
                        d = (ch - 1) - 4 * (qc - 1)
                        if -1 <= d <= 4:
                            table, tb = Gh[sl][:, r, 512 - 128 * d:1024 - 128 * d], bh[3]
                        elif qc == 1 and r == 0 and ch == NGT:
                            table, tb = Gxh[sl][:, 0, :], bh[7]
                        elif qc == NQ and r == 1 and ch == 1:
                            table, tb = Gxh[sl][:, 1, :], bh[7]
                    b0, b1 = 2 * st_, 2 * st_ + 1
                    tsl = it % 2
                    if table is not None:
                        for s in range(2):
                            P.stt(tmpb[tsl][:nk, s, :N], ps[:nk, 2 * st_ + s, :N], 0.125, table, ALU.mult, ALU.add,
                                  [pb[2 * st_ + s], tb], [b_tmpb[tsl]])
                        P.act(pt[pi][:nk, :, :N], tmpb[tsl][:nk, :, :N], AF.Exp, [b_tmpb[tsl]], [b_pt[pi]])
                    else:
                        ci = qc * NKC + kc
                        P.act(pt[pi][:nk, :, :N], ps[:nk, b0:b1 + 1, :N], AF.Exp, [pb[b0], pb[b1], bh[5]], [b_pt[pi]],
                              bias=cbh[sl][:nk, ci:ci + 1], scale=0.125)
                    for s in range(2):
                        P.mm(ps[:, 4 + s, :N], Vh[sl][:nk, r * NCH + ch, :], pt[pi][:nk, s, :N], kc == 0, kc == NKC - 1,
                             [bh[1], b_pt[pi]], [pb[4 + s]])
                    for s in range(2):
                        P.mm(ps[:, 6 + s, :N], self.ones_b[:nk, :], pt[pi][:nk, s, :N], kc == 0, kc == NKC - 1,
                             [self.b_c, b_pt[pi]], [pb[6 + s]])
                for s in range(2):
                    P.recip(rden[:, s, :N], ps[:, 6 + s, :N], [pb[6 + s]], [b_rden[s]])
                P.tt("dve", o0[:, :N], ps[:, 4, :N], rden[:, 0, :N], ALU.mult, [pb[4], b_rden[0]], [b_o[0]])
                P.stt(o1[:, :N], ps[:, 5, :N], neglam, rden[:, 1, :N], ALU.mult, ALU.mult, [pb[5], b_rden[1], b_l], [b_o[1]])
                P.tt("dve", o0[:, :N], o0[:, :N], o1[:, :N], ALU.add, [b_o[0], b_o[1]], [b_o[0]])
                P.act(sq[:, :N], o0[:, :N], AF.Square, [b_o[0]], [b_o[2]])
                P.mm(ps[:, 6, :N], self.ones_f, sq[:, :N], True, True, [self.b_c, b_o[2]], [pb[6]])
                P.act(rt[:, :N], ps[:, 6, :N], AF.Sqrt, [pb[6], self.b_c], [b_o[3]], bias=self.eps, scale=1.0 / 128)
                P.recip(rt[:, :N], rt[:, :N], [b_o[3]], [b_o[3]])
                ob = oc % 2
                oc += 1
                P.stt(aob[ob][:, :N], o0[:, :N], gsc, rt[:, :N], ALU.mult, ALU.mult, [b_o[0], b_o[3], b_l], [b_aob[ob]])
                P.dma("sp", AOd[h][:, q0:q0 + N], aob[ob][:, :N], [b_aob[ob]], [self.b_AOT])
        P.barrier()


PARAM_NAMES = ["norm_g", "ffn_w_gate", "ffn_w_up", "ffn_w_down", "na_w_qkv", "na_b_qkv", "na_w_o", "na_b_o",
               "na_meta_bias", "da_w_qkv", "da_w_o", "da_lambda", "da_subln_g"]

MODE = "fused"
_last_ninst = None


def run_forward(cfg, inputs, mode=None):
    mode = mode or MODE
    B, SEQ, DEPTH = cfg.B, cfg.SEQ, cfg.DEPTH
    ncores = 2 * B
    x = np.asarray(inputs["x"], np.float32)
    meta = np.asarray(inputs["meta_tokens"], np.float32)
    params = {}
    for nm in PARAM_NAMES:
        a = np.ascontiguousarray(np.asarray(inputs[nm], np.float32))
        if nm == "da_lambda":
            a = a.reshape(a.shape[0], 256)
        params[nm] = a
    rpb = np.asarray(inputs["na_rpb"], np.float32)
    tbl = np.asarray(inputs["t5_rel_bias"], np.float32)
    per_core = []
    for c in range(ncores):
        b, half = c // 2, c % 2
        d = dict(params)
        d["na_mask"] = build_na_mask(cfg, rpb, half).reshape(cfg.NLA, cfg.NVAR, 128, 16 * 6 * 128)
        if cfg.NLB:
            G, Gm, cb, Bm, Gx = build_da_tables(cfg, tbl, half)
            d["da_G"] = G.reshape(8, 128, 2 * 1152)
            d["da_Gm"], d["da_cb"], d["da_Bm"], d["da_Gx"] = Gm, cb, Bm, Gx
        else:
            for nm in ("da_w_qkv", "da_w_o", "da_lambda", "da_subln_g"):
                d.pop(nm, None)
        h0 = np.zeros((cfg.TCP, D), np.float32)
        h0[:NMETA] = meta
        h0[128:] = x[b, half * cfg.NG:(half + 1) * cfg.NG]
        d["h_in"] = h0
        per_core.append(d)
    global _last_ninst
    if mode == "fused":
        bld = Builder(cfg, "fused", None)
        nc = bld.build()
        _last_ninst = bld.ninst
        in_maps = [{k: per_core[c][k] for k in bld.in_names} for c in range(ncores)]
        res = run_bass_kernel_spmd(nc, in_maps, core_ids=list(range(ncores)))
        outs = [res.results[c]["out"] for c in range(ncores)]
    else:
        state = [dict() for _ in range(ncores)]
        for seg in range(DEPTH + 1):
            bld = Builder(cfg, "multi", seg)
            nc = bld.build()
            _last_ninst = bld.ninst
            in_maps = []
            for c in range(ncores):
                m = {}
                for k in bld.in_names:
                    m[k] = state[c][k] if k in state[c] else per_core[c][k]
                in_maps.append(m)
            res = run_bass_kernel_spmd(nc, in_maps, core_ids=list(range(ncores)))
            if seg < DEPTH:
                for c in range(ncores):
                    r = res.results[c]
                    state[c] = {"h_in": r["h_out"], "QT": r["QT_o"], "KT": r["KT_o"], "V": r["V_o"]}
                CH = 131072
                HS = 128 * cfg.TCP
                for b in range(B):
                    c0, c1 = state[2 * b], state[2 * b + 1]
                    g = {}
                    if seg % 2 == 1:
                        for nm in ("KT", "V"):
                            a0, a1 = np.asarray(c0[nm]).reshape(8, HS), np.asarray(c1[nm]).reshape(8, HS)
                            g[nm + "_all"] = np.ascontiguousarray(np.stack([a0, a1], axis=1))
                    else:
                        for nm in ("KT", "V"):
                            a0, a1 = np.asarray(c0[nm]), np.asarray(c1[nm])
                            g[nm + "lo_all"] = np.stack([a0[CH:3 * CH], a1[CH:3 * CH]])
                            g[nm + "hi_all"] = np.stack([a0[(cfg.NGT - 1) * CH:(cfg.NGT + 1) * CH],
                                                         a1[(cfg.NGT - 1) * CH:(cfg.NGT + 1) * CH]])
                    for c in (2 * b, 2 * b + 1):
                        state[c].update(g)
            else:
                outs = [res.results[c]["out"] for c in range(ncores)]
    out = np.zeros((B, SEQ, D), np.float32)
    for c in range(ncores):
        b, half = c // 2, c % 2
        out[b, half * cfg.NG:(half + 1) * cfg.NG] = outs[c]
    return out


def kernel(**inputs):
    cfg = Cfg()
    return run_forward(cfg, inputs)
```

```python
import math
from contextlib import ExitStack

import numpy as np
import concourse.bass as bass
import concourse.mybir as mybir
from concourse.bass_utils import run_bass_kernel_spmd

F32 = mybir.dt.float32
BF16 = mybir.dt.bfloat16
AF = mybir.ActivationFunctionType
ALU = mybir.AluOpType

D = 1024
NMETA = 16
GW = 64
RMS_EPS = 1e-6
NEG = -30000.0

ENGS = ("pe", "act", "dve", "pool", "sp")
EPOCH = 30000
NDMA_SEMS = {"sp": 24, "pool": 16, "act": 4, "pe": 2, "dve": 2}


class Buf:
    __slots__ = ("name", "last_w", "rd_eng", "rd_dma", "excl")

    def __init__(self, name, excl=False):
        self.name = name
        self.excl = excl
        self.last_w = None
        self.rd_eng = {}
        self.rd_dma = []


class Op:
    __slots__ = ("eng", "fn", "dma", "deps", "signal", "token", "idx", "prev_dma", "inc")

    def __init__(self, eng, fn, dma):
        self.eng = eng
        self.fn = fn
        self.dma = dma
        self.inc = 16
        self.deps = []
        self.signal = False
        self.token = None
        self.prev_dma = None


class Arena:
    def __init__(self, t, nbytes):
        self.t = t
        self.nbytes = nbytes
        self.off = 0
        self.base = 0

    def alloc(self, shape, dtype):
        n = 1
        for s in shape:
            n *= s
        esz = 4 if dtype == F32 else 2
        nb = (n * esz + 31) // 32 * 32
        assert self.off + nb <= self.nbytes, ("arena overflow", self.off, nb)
        v = self.t[:, self.off // 4:(self.off + nb) // 4]
        self.off += nb
        if dtype != F32:
            v = v.bitcast(dtype)
        v = v[:, 0:n]
        if len(shape) == 2:
            return v.rearrange("p (a b) -> p a b", b=shape[1])
        if len(shape) == 3:
            return v.rearrange("p (a b c) -> p a b c", b=shape[1], c=shape[2])
        return v

    def mark(self):
        self.base = self.off

    def reset(self):
        self.off = self.base


class Prog:
    def __init__(self, nc, stack):
        self.nc = nc
        self.stack = stack
        self.ops = []
        self.live = []

    def buf(self, name="b"):
        b = Buf(name)
        self.live.append(b)
        return b

    def bufs(self, n, name="b"):
        return [self.buf(name) for _ in range(n)]

    def add(self, eng, fn, reads=(), writes=(), dma=False, inc=16):
        op = Op(eng, fn, dma)
        op.inc = inc
        op.idx = len(self.ops)
        deps = {}
        xr = [b for b in reads if b.excl]
        if xr:
            writes = list(writes) + [b for b in xr if b not in writes]
        for b in reads:
            if b.last_w is not None:
                deps[b.last_w.idx] = (b.last_w, True)
        for b in writes:
            if b.last_w is not None and b.last_w.idx not in deps:
                deps[b.last_w.idx] = (b.last_w, False)
            for r in b.rd_eng.values():
                if r.idx not in deps:
                    deps[r.idx] = (r, False)
            for r in b.rd_dma:
                if r.idx not in deps:
                    deps[r.idx] = (r, False)
        for p, raw in deps.values():
            if p is op:
                continue
            same = (p.eng == op.eng) and (not p.dma) and (not op.dma)
            if same and (op.eng == "pe" or not raw):
                continue
            op.deps.append(p)
            p.signal = True
        if fn is not None:
            for b in writes:
                b.last_w = op
                b.rd_eng = {}
                b.rd_dma = []
            for b in reads:
                if b.last_w is not op:
                    if dma:
                        b.rd_dma.append(op)
                    else:
                        b.rd_eng[eng] = op
        self.ops.append(op)
        return op

    def dma(self, eng, out, in_, reads=(), writes=(), **kw):
        return self.add(eng, lambda e: e.dma_start(out=out, in_=in_, **kw), reads, writes, dma=True)

    def barrier(self, extra=()):
        bl = list(self.live) + list(extra)
        for e in ENGS:
            self.add(e, None, reads=bl, writes=bl)
        self.live = []

    def mm(self, out, lhsT, rhs, start, stop, reads, writes):
        return self.add("pe", lambda e: e.matmul(out=out, lhsT=lhsT, rhs=rhs, start=start, stop=stop), reads, writes)

    def tr(self, out, in_, ident, reads, writes):
        return self.add("pe", lambda e: e.transpose(out=out, in_=in_, identity=ident), reads, writes)

    def act(self, out, in_, func, reads, writes, **kw):
        return self.add("act", lambda e: e.activation(out=out, in_=in_, func=func, **kw), reads, writes)

    def stt(self, out, in0, scalar, in1, op0, op1, reads, writes):
        return self.add("dve", lambda e: e.scalar_tensor_tensor(out=out, in0=in0, scalar=scalar, in1=in1, op0=op0, op1=op1), reads, writes)

    def tt(self, eng, out, in0, in1, op, reads, writes):
        return self.add(eng, lambda e: e.tensor_tensor(out=out, in0=in0, in1=in1, op=op), reads, writes)

    def cp(self, eng, out, in_, reads, writes):
        if eng == "act":
            return self.add("act", lambda e: e.copy(out=out, in_=in_), reads, writes)
        return self.add(eng, lambda e: e.tensor_copy(out=out, in_=in_), reads, writes)

    def recip(self, out, in_, reads, writes):
        return self.add("dve", lambda e: e.reciprocal(out=out, in_=in_), reads, writes)

    def memset(self, eng, ap, val, writes):
        return self.add(eng, lambda e: e.memset(ap, val), (), writes)

    def emit(self):
        nc = self.nc
        st = self.stack
        cnt = {e: 0 for e in ENGS}
        epoch_sems = {e: [] for e in ENGS}
        dma_sems = {e: [] for e in ENGS}
        dma_use = {e: [] for e in ENGS}
        dma_last = {e: [] for e in ENGS}
        dma_rr = {e: 0 for e in ENGS}
        for op in self.ops:
            if op.fn is None:
                continue
            e = op.eng
            if op.dma and op.inc == 1:
                sem = st.enter_context(nc.semaphore(f"cc_{op.idx}"))
                op.token = (sem, 1)
                op.signal = True
            elif op.dma:
                if not dma_sems[e]:
                    for j in range(NDMA_SEMS[e]):
                        dma_sems[e].append(st.enter_context(nc.semaphore(f"d_{e}_{j}")))
                        dma_use[e].append(0)
                        dma_last[e].append(None)
                j = dma_rr[e] % len(dma_sems[e])
                dma_rr[e] += 1
                dma_use[e][j] += 1
                op.prev_dma = dma_last[e][j]
                op.token = (dma_sems[e][j], 16 * dma_use[e][j])
                dma_last[e][j] = op.token
                op.signal = True
            elif op.signal:
                k = cnt[e] // EPOCH
                if k >= len(epoch_sems[e]):
                    epoch_sems[e].append(st.enter_context(nc.semaphore(f"s_{e}_{k}")))
                cnt[e] += 1
                op.token = (epoch_sems[e][k], cnt[e] - k * EPOCH)
        per_eng = {e: [] for e in ENGS}
        for op in self.ops:
            per_eng[op.eng].append(op)
        ninst = {e: 0 for e in ENGS}

        def run(e, eo):
            waited = {}
            for op in per_eng[e]:
                toks = [p.token for p in op.deps]
                if op.prev_dma is not None:
                    toks.append(op.prev_dma)
                need = {}
                for (s, v) in toks:
                    key = id(s)
                    if waited.get(key, 0) >= v:
                        continue
                    if key not in need or need[key][1] < v:
                        need[key] = (s, v)
                for key, (s, v) in need.items():
                    eo.wait_ge(s, v)
                    waited[key] = v
                    ninst[e] += 1
                if op.fn is None:
                    continue
                inst = op.fn(eo)
                ninst[e] += 1
                if op.signal:
                    inst.then_inc(op.token[0], op.inc if op.dma else 1)

        with nc.Block() as block:
            @block.tensor
            def _(eo):
                run("pe", eo)

            @block.scalar
            def _(eo):
                run("act", eo)

            @block.vector
            def _(eo):
                run("dve", eo)

            @block.gpsimd
            def _(eo):
                run("pool", eo)

            @block.sync
            def _(eo):
                run("sp", eo)
        self.ninst = ninst


class Cfg:
    def __init__(self, SEQ=8192, DEPTH=4, DFF=2816, B=4):
        self.SEQ, self.DEPTH, self.DFF, self.B = SEQ, DEPTH, DFF, B
        self.ROWS = SEQ // GW
        self.NG = SEQ // 2
        self.NGT = self.NG // 128
        self.NCH = self.NGT + 1
        self.TCP = self.NCH * 128
        self.FCH = DFF // 128
        self.NQ = self.NGT // 4
        self.NKC = 1 + 2 * self.NGT
        self.NLA = (DEPTH + 1) // 2
        self.NLB = DEPTH // 2
        self.NVAR = 5
        assert self.NGT % 4 == 0 and self.NGT >= 8

    def groups(self, gs=4):
        g = [[(0, NMETA)]]
        for c in range(self.NGT // gs):
            g.append([(1 + gs * c + j, 128) for j in range(gs)])
        return g

    def var_of_block(self, b):
        if b == 0:
            return 0
        if b == 1:
            return 1
        if b == self.NGT - 2:
            return 3
        if b == self.NGT - 1:
            return 4
        return 2

    def rep_block(self, v):
        return [0, 1, 2, self.NGT - 2, self.NGT - 1][v]


def t5_bucket_np(rel):
    nb, me = 16, 8
    rel = np.asarray(rel, np.int64)
    ret = np.where(rel > 0, nb, 0)
    n = np.abs(rel)
    nf = np.maximum(n, 1).astype(np.float32)
    large = me + (np.log(nf / np.float32(me)) / np.float32(math.log(128 / me)) * np.float32(nb - me)).astype(np.int32)
    large = np.minimum(large, nb - 1)
    return ret + np.where(n < me, n, large)


def na_mask_index(cfg, half):
    NB = 2 * cfg.NGT
    rows = cfg.ROWS
    kh = min(8, rows)
    p = np.arange(128)
    out = np.full((cfg.NVAR, 6, 128, 128), 465, np.int64)
    for v in range(cfg.NVAR):
        i = half * cfg.NGT + cfg.rep_block(v)
        qr = 2 * i + p // 64
        qc = p % 64
        rs = np.clip(qr - kh // 2, 0, rows - kh)
        cs = np.clip(qc - 8, 0, GW - 16)
        offs = [-2, -1, 0, 1, 2, 3 if v == 0 else (-3 if v == 4 else None)]
        for s in range(6):
            if offs[s] is None:
                continue
            g = i + offs[s]
            if g < 0 or g >= NB:
                continue
            kr = 2 * g + p // 64
            kc = p % 64
            vis = ((kr[:, None] >= rs[None, :]) & (kr[:, None] < rs[None, :] + kh)
                   & (kc[:, None] >= cs[None, :]) & (kc[:, None] < cs[None, :] + 16))
            dy = kr[:, None] - qr[None, :] + 7
            dx = kc[:, None] - qc[None, :] + 15
            idx = dy * 31 + dx
            out[v, s] = np.where(vis, idx, 465)
    return out


def build_na_mask(cfg, rpb, half):
    idx = na_mask_index(cfg, half)
    ext = np.concatenate([rpb.reshape(rpb.shape[0], 16, 465),
                          np.full((rpb.shape[0], 16, 1), NEG, np.float32)], axis=2)
    m = ext[:, :, idx]
    return np.ascontiguousarray(m.transpose(0, 2, 4, 1, 3, 5)).astype(np.float32)


def build_da_tables(cfg, tbl, half):
    NG, NGT, NKC, NQ = cfg.NG, cfg.NGT, cfg.NKC, cfg.NQ
    p = np.arange(128)[:, None]
    x = np.arange(1152)[None, :]
    tblT = tbl.T
    G = np.zeros((8, 128, 2, 1152), np.float32)
    for r in range(2):
        rel = (r - half) * NG + p - x + 512
        G[:, :, r, :] = tblT[:, t5_bucket_np(rel)]
    m = np.arange(16)[:, None]
    qf = np.arange(512)[None, :]
    Gm = tblT[:, t5_bucket_np(m - (16 + half * NG + qf))].astype(np.float32)
    pp = np.arange(128)[:, None]
    Gx = np.zeros((8, 128, 2, 512), np.float32)
    Gx[:, :, 0, :] = tblT[:, t5_bucket_np((0 - half) * NG + 128 * (NGT - 1) + pp - qf)]
    Gx[:, :, 1, :] = tblT[:, t5_bucket_np((1 - half) * NG + pp - 512 * (NQ - 1) - qf)]
    cb = np.zeros((8, NQ + 1, NKC), np.float32)
    for qc in range(1, NQ + 1):
        qpos = 16 + half * NG + (qc - 1) * 512
        for kc in range(NKC):
            kpos = 0 if kc == 0 else 16 + (kc - 1) * 128
            cb[:, qc, kc] = tblT[:, t5_bucket_np(kpos - qpos)]
    cb = np.ascontiguousarray(np.broadcast_to(cb.reshape(8, 1, -1), (8, 128, (NQ + 1) * NKC)))
    kp = np.zeros((128, NKC), np.int64)
    kp[:, 0] = np.arange(128)
    for kc in range(1, NKC):
        kp[:, kc] = 16 + (kc - 1) * 128 + np.arange(128)
    q = np.arange(16)[None, None, :]
    Bm = tblT[:, t5_bucket_np(kp[:, :, None] - q)].astype(np.float32)
    return G, np.ascontiguousarray(Gm), cb, np.ascontiguousarray(Bm.reshape(8, 128, NKC * 16)), Gx.reshape(8, 128, 1024)


class Builder:
    def __init__(self, cfg, mode, seg):
        self.cfg = cfg
        self.mode = mode
        self.seg = seg
        self.nc = bass.Bass("TRN2", target_bir_lowering=False)
        self.dr = {}
        self.in_names = []
        self.out_names = []

    def dram(self, name, shape, dtype, kind):
        t = self.nc.dram_tensor(name, list(shape), dtype, kind=kind)
        self.dr[name] = t.ap()
        if kind == "ExternalInput":
            self.in_names.append(name)
        elif kind == "ExternalOutput":
            self.out_names.append(name)
        return self.dr[name]

    def build(self):
        cfg = self.cfg
        nc = self.nc
        DEPTH, DFF = cfg.DEPTH, cfg.DFF
        seg, fused = self.seg, self.mode == "fused"
        with ExitStack() as st:
            P = Prog(nc, st)
            self.P = P
            at = st.enter_context(nc.sbuf_tensor("arena", [128, 200 * 256], F32))
            self.A = Arena(at, 200 * 1024)
            self.ps = st.enter_context(nc.psum_tensor("ps", [128, 8, 512], F32))
            self.pb = [Buf(f"bank{i}", excl=True) for i in range(8)]
            EI = "ExternalInput"
            self.dram("norm_g", [DEPTH, 6, D], F32, EI)
            self.dram("ffn_w_gate", [DEPTH, 2, D, DFF], F32, EI)
            self.dram("ffn_w_up", [DEPTH, 2, D, DFF], F32, EI)
            self.dram("ffn_w_down", [DEPTH, 2, DFF, D], F32, EI)
            self.dram("na_w_qkv", [cfg.NLA, D, 3 * D], F32, EI)
            self.dram("na_b_qkv", [cfg.NLA, 3 * D], F32, EI)
            self.dram("na_w_o", [cfg.NLA, D, D], F32, EI)
            self.dram("na_b_o", [cfg.NLA, D], F32, EI)
            self.dram("na_meta_bias", [cfg.NLA, 16, 16], F32, EI)
            self.dram("na_mask", [cfg.NLA, cfg.NVAR, 128, 16 * 6 * 128], F32, EI)
            if cfg.NLB:
                self.dram("da_w_qkv", [cfg.NLB, D, 3 * D], F32, EI)
                self.dram("da_w_o", [cfg.NLB, D, D], F32, EI)
                self.dram("da_lambda", [cfg.NLB, 4 * 64], F32, EI)
                self.dram("da_subln_g", [cfg.NLB, 128], F32, EI)
                self.dram("da_G", [8, 128, 2 * 1152], F32, EI)
                self.dram("da_Gm", [8, 16, 512], F32, EI)
                self.dram("da_Gx", [8, 128, 1024], F32, EI)
                self.dram("da_cb", [8, 128, (cfg.NQ + 1) * cfg.NKC], F32, EI)
                self.dram("da_Bm", [8, 128, cfg.NKC * 16], F32, EI)
            TCP, NCH = cfg.TCP, cfg.NCH
            self.dram("h_in", [TCP, D], F32, EI)
            self.b_hin = [Buf("hin") for _ in range(NCH)]
            if fused:
                self.dram("h", [TCP, D], F32, "Internal")
                self.dram("out", [cfg.NG, D], F32, "ExternalOutput")
                for nm in ("QT", "KT", "AOT"):
                    self.dram(nm, [8 * 128 * TCP], BF16, "Internal")
                self.dram("V", [8 * 128 * TCP], BF16, "Internal")
                CH = 131072
                for nm, shp in (("KT_all", [8, 2, 128 * TCP]), ("V_all", [8, 2, 128 * TCP]),
                                ("KTlo_all", [2, 2 * CH]), ("KThi_all", [2, 2 * CH]),
                                ("Vlo_all", [2, 2 * CH]), ("Vhi_all", [2, 2 * CH])):
                    t = self.nc.dram_tensor(nm, shp, BF16, kind="Internal", addr_space="Local")
                    self.dr[nm] = t.ap()
            else:
                if seg > 0:
                    self.dram("QT", [8 * 128 * TCP], BF16, EI)
                    self.dram("KT", [8 * 128 * TCP], BF16, EI)
                    self.dram("V", [8 * 128 * TCP], BF16, EI)
                    if (seg - 1) % 2 == 1:
                        self.dram("KT_all", [8, 2, 128 * TCP], BF16, EI)
                        self.dram("V_all", [8, 2, 128 * TCP], BF16, EI)
                    else:
                        for nm in ("KTlo_all", "KThi_all", "Vlo_all", "Vhi_all"):
                            self.dram(nm, [2, 2 * 131072], BF16, EI)
                    import os
                    self.attn_only = "attnonly" in os.environ.get("KDBG1", "")
                    self.dram("AOT", [8 * 128 * TCP], BF16, "ExternalOutput" if self.attn_only else "Internal")
                    self.dram("h", [TCP, D], F32, "Internal")
                if seg > 0 and self.attn_only:
                    pass
                elif seg < DEPTH:
                    self.dram("h_out", [TCP, D], F32, "ExternalOutput")
                    self.dram("QT_o", [8 * 128 * TCP], BF16, "ExternalOutput")
                    self.dram("KT_o", [8 * 128 * TCP], BF16, "ExternalOutput")
                    self.dram("V_o", [8 * 128 * TCP], BF16, "ExternalOutput")
                else:
                    self.dram("out", [cfg.NG, D], F32, "ExternalOutput")
            self.b_h = [Buf("h") for _ in range(NCH)]
            self.b_hout = [Buf("hout") for _ in range(NCH)]
            self.b_out = [Buf("out") for _ in range(NCH)]
            self.b_QT, self.b_KT, self.b_V, self.b_AOT = Buf("QT"), Buf("KT"), Buf("V"), Buf("AOT")
            self.b_KTall, self.b_Vall = Buf("KTall"), Buf("Vall")

            self.consts()
            dr = self.dr
            if fused:
                hcur = ("h_in", self.b_hin)
                for i in range(DEPTH):
                    self.stage_ffn(i, 0, hcur, ("h", self.b_h))
                    hcur = ("h", self.b_h)
                    self.stage_qkv(i, hcur, "QT", "KT", "V")
                    self.stage_gather(i)
                    self.stage_attn(i)
                    self.stage_oproj(i, hcur, hcur)
                    last = (i == DEPTH - 1)
                    self.stage_ffn(i, 1, hcur, ("out", self.b_out) if last else hcur)
                P.barrier(self.b_out)
            else:
                hcur = ("h_in", self.b_hin)
                if seg > 0:
                    i = seg - 1
                    self.stage_attn(i)
                    if self.attn_only:
                        P.barrier([self.b_AOT])
                        P.emit()
                        self.ninst = P.ninst
                        return nc
                    self.stage_oproj(i, hcur, ("h", self.b_h))
                    hcur = ("h", self.b_h)
                    if seg == DEPTH:
                        self.stage_ffn(i, 1, hcur, ("out", self.b_out))
                    else:
                        self.stage_ffn(i, 1, hcur, hcur)
                if seg < DEPTH:
                    import os
                    dbg = os.environ.get("KDBG", "ffn,qkv")
                    if "ffn" in dbg:
                        self.stage_ffn(seg, 0, hcur, ("h_out", self.b_hout))
                    if "qkv" in dbg:
                        self.stage_qkv(seg, ("h_out", self.b_hout) if "ffn" in dbg else hcur, "QT_o", "KT_o", "V_o")
                P.barrier(self.b_out + self.b_hout + [self.b_QT, self.b_KT, self.b_V])
            P.emit()
            self.ninst = P.ninst
        return nc

    def consts(self):
        P, A = self.P, self.A
        self.b_c = P.buf("consts")
        self.identf = A.alloc([128], F32)
        self.ident = A.alloc([128], BF16)
        self.ones_f = A.alloc([128], F32)
        self.ones_b = A.alloc([128], BF16)
        self.eps = A.alloc([1], F32)
        b = self.b_c
        P.memset("pool", self.identf, 0.0, [b])
        P.add("pool", lambda e: e.affine_select(out=self.identf, in_=self.identf, pattern=[[-1, 128]],
                                                compare_op=ALU.not_equal, fill=1.0, base=0,
                                                channel_multiplier=1), [b], [b])
        P.cp("dve", self.ident, self.identf, [b], [b])
        P.memset("dve", self.ones_f, 1.0, [b])
        P.memset("dve", self.ones_b, 1.0, [b])
        P.memset("dve", self.eps, RMS_EPS, [b])
        self.sel = [A.alloc([128], F32) for _ in range(2)]
        for i in range(2):
            P.memset("dve", self.sel[i], 0.0, [b])
            P.memset("dve", self.sel[i][64 * i:64 * i + 64, :], 1.0 / 64, [b])
        A.mark()

    def load_w_cast(self, dst, src, rows_chunks, ncols, bufs):
        P = self.P
        step = ncols
        while step > 2048:
            step //= 2
        i = 0
        for k in range(rows_chunks):
            for c0 in range(0, ncols, step):
                P.dma("pool", dst[:, k, c0:c0 + step], src[k * 128:(k + 1) * 128, c0:c0 + step], (), [bufs[i]])
                i += 1
        return i

    def load_rep(self, dst, vec, b):
        self.P.dma("sp", dst, vec.partition_broadcast(128), (), [b])

    def rstd_from_ss(self, ss, rstd, np_, n, b_ss, b_rstd):
        P = self.P
        P.act(rstd[:np_], ss[:np_], AF.Sqrt, [b_ss, self.b_c], [b_rstd], bias=self.eps[:np_], scale=1.0 / n)
        P.recip(rstd[:np_], rstd[:np_], [b_rstd], [b_rstd])

    def norm_T(self, W, grp, hsrc, g_rep, b_g, gi):
        P, ps, pb = self.P, self.ps, self.pb
        hname, hbufs = hsrc
        hd = self.dr[hname]
        slot = gi % 2
        hb, b_hb = W["hb"][slot], W["b_hb"][slot]
        xnT, b_xnT = W["xnT"][slot], W["b_xnT"][slot]
        N = 0
        ptr = ps[:, 7, :].bitcast(BF16).rearrange("p (k n) -> p k n", n=128)
        for j, (t, np_) in enumerate(grp):
            P.dma("sp", hb[:np_, j, :], hd[t * 128:t * 128 + np_, :], [hbufs[t]], [b_hb[j]])
            sl = (gi * len(grp) + j) % 2
            ss, rstd, xn = W["ss"][sl], W["rstd"][sl], W["xn"][sl]
            b_ss, b_rstd, b_xn = W["b_ss"][sl], W["b_rstd"][sl], W["b_xn"][sl]
            P.act(W["junk"][:np_], hb[:np_, j, :], AF.Square, [b_hb[j]], [W["b_junk"], b_ss], accum_out=ss[:np_])
            self.rstd_from_ss(ss, rstd, np_, D, b_ss, b_rstd)
            P.stt(xn[:np_], hb[:np_, j, :], rstd[:np_], g_rep[:np_], ALU.mult, ALU.mult, [b_hb[j], b_rstd, b_g], [b_xn])
            for k in range(8):
                P.tr(ptr[:, k, :np_], xn[:np_, k * 128:(k + 1) * 128], self.ident[:np_, :np_], [b_xn, self.b_c], [pb[7]])
            P.cp("dve" if j % 2 else "act", xnT[:, :, N:N + np_], ptr[:, :, :np_], [pb[7]], [b_xnT])
            N += np_
        return N

    def work_common(self, gs=4, lean=False):
        P, A = self.P, self.A
        W = {}
        hb0 = A.alloc([gs, D], F32)
        W["hb"] = [hb0, hb0 if lean else A.alloc([gs, D], F32)]
        bh0 = P.bufs(gs, "hb")
        W["b_hb"] = [bh0, bh0 if lean else P.bufs(gs, "hb")]
        W["xnT"] = [A.alloc([8, gs * 128], BF16) for _ in range(2)]
        W["b_xnT"] = P.bufs(2, "xnT")
        W["xn"] = [A.alloc([D], BF16) for _ in range(2)]
        W["b_xn"] = P.bufs(2, "xn")
        W["ss"] = [A.alloc([1], F32) for _ in range(2)]
        W["b_ss"] = P.bufs(2, "ss")
        W["rstd"] = [A.alloc([1], F32) for _ in range(2)]
        W["b_rstd"] = P.bufs(2, "rstd")
        W["junk"] = A.alloc([D], BF16)
        W["b_junk"] = P.buf("junk")
        tmp0 = A.alloc([D], F32)
        W["tmp"] = [tmp0, tmp0 if lean else A.alloc([D], F32)]
        bt0 = P.buf("tmp")
        W["b_tmp"] = [bt0, bt0 if lean else P.buf("tmp")]
        W["ho"] = [A.alloc([D], F32) for _ in range(2)]
        W["b_ho"] = P.bufs(2, "ho")
        W["ss2"] = [A.alloc([1], F32) for _ in range(2)]
        W["b_ss2"] = P.bufs(2, "ss2")
        W["rstd2"] = [A.alloc([1], F32) for _ in range(2)]
        W["b_rstd2"] = P.bufs(2, "rstd2")
        return W

    def dst_rows(self, hdst, t, np_):
        name, bufs = hdst
        d = self.dr[name]
        if name == "out":
            if t == 0:
                return None, None
            return d[(t - 1) * 128:(t - 1) * 128 + np_, :], bufs[t]
        return d[t * 128:t * 128 + np_, :], bufs[t]

    def resid_out(self, W, cnt, np_, src, src_bufs, hb_j, b_hb_j, g_rep, b_g, scale, hdst, t):
        P = self.P
        sl = cnt % 2
        ss2, rstd2, tmp, ho = W["ss2"][sl], W["rstd2"][sl], W["tmp"][sl], W["ho"][sl]
        b_ss2, b_rstd2, b_tmp, b_ho = W["b_ss2"][sl], W["b_rstd2"][sl], W["b_tmp"][sl], W["b_ho"][sl]
        P.act(W["junk"][:np_], src, AF.Square, src_bufs, [W["b_junk"], b_ss2], accum_out=ss2[:np_])
        self.rstd_from_ss(ss2, rstd2, np_, D, b_ss2, b_rstd2)
        P.stt(tmp[:np_], src, rstd2[:np_], g_rep[:np_], ALU.mult, ALU.mult, list(src_bufs) + [b_rstd2, b_g], [b_tmp])
        P.stt(ho[:np_], tmp[:np_], float(scale), hb_j, ALU.mult, ALU.add, [b_tmp, b_hb_j], [b_ho])
        dst, bd = self.dst_rows(hdst, t, np_)
        if dst is not None:
            P.dma("sp", dst, ho[:np_], [b_ho], [bd])

    def stage_ffn(self, li, w, hsrc, hdst):
        cfg, P, A, ps, pb, dr = self.cfg, self.P, self.A, self.ps, self.pb, self.dr
        A.reset()
        FCH, DFF = cfg.FCH, cfg.DFF
        Wg = A.alloc([8, DFF], BF16)
        Wu = A.alloc([8, DFF], BF16)
        Wd = A.alloc([FCH, D], BF16)
        b_wg, b_wu, b_wd = P.bufs(32, "wg"), P.bufs(32, "wu"), P.bufs(FCH, "wd")
        n = self.load_w_cast(Wg, dr["ffn_w_gate"][li, w], 8, DFF, b_wg)
        b_wg = b_wg[:n]
        n = self.load_w_cast(Wu, dr["ffn_w_up"][li, w], 8, DFF, b_wu)
        b_wu = b_wu[:n]
        self.load_w_cast(Wd, dr["ffn_w_down"][li, w], FCH, D, b_wd)
        gin, gout = A.alloc([D], F32), A.alloc([D], F32)
        b_gin, b_gout = P.buf("gin"), P.buf("gout")
        self.load_rep(gin, dr["norm_g"][li, 4 * w, :], b_gin)
        self.load_rep(gout, dr["norm_g"][li, 4 * w + 1, :], b_gout)
        gs = 2 if DFF > 2048 else 4
        W = self.work_common(gs, lean=(gs == 2))
        HT = A.alloc([FCH, gs * 128], BF16)
        b_HT = P.buf("HT")
        sg = [A.alloc([gs * 128], F32) for _ in range(2)]
        b_sg = P.bufs(2, "sg")
        cnt = 0
        import os
        kparts = os.environ.get("KPARTS", "norm,gu,down")
        kgroups = int(os.environ.get("KGROUPS", "1000"))
        for gi, grp in enumerate(cfg.groups(gs)):
            if gi >= kgroups:
                break
            N = self.norm_T(W, grp, hsrc, gin, b_gin, gi)
            slot = gi % 2
            xnT, b_xnT = W["xnT"][slot], W["b_xnT"][slot]
            hb, b_hb = W["hb"][slot], W["b_hb"][slot]
            if "gu" not in kparts:
                continue
            for f in range(FCH):
                s2 = f % 2
                bg, bu = 2 * s2, 2 * s2 + 1
                for k in range(8):
                    P.mm(ps[:, bg, :N], Wg[:, k, f * 128:(f + 1) * 128], xnT[:, k, :N], k == 0, k == 7,
                         [b_xnT] + b_wg, [pb[bg]])
                for k in range(8):
                    P.mm(ps[:, bu, :N], Wu[:, k, f * 128:(f + 1) * 128], xnT[:, k, :N], k == 0, k == 7,
                         [b_xnT] + b_wu, [pb[bu]])
                P.act(sg[s2][:, :N], ps[:, bg, :N], AF.Silu, [pb[bg]], [b_sg[s2]])
                P.tt("dve", HT[:, f, :N], sg[s2][:, :N], ps[:, bu, :N], ALU.mult, [b_sg[s2], pb[bu]], [b_HT])
            if "down" not in kparts:
                continue
            for j, (t, np_) in enumerate(grp):
                pd0 = 4 if cnt % 2 == 0 else 2
                for half in range(2):
                    for f in range(FCH):
                        P.mm(ps[:np_, pd0 + half, :], HT[:, f, j * 128:j * 128 + np_], Wd[:, f, half * 512:(half + 1) * 512],
                             f == 0, f == FCH - 1, [b_HT, b_wd[f]], [pb[pd0 + half]])
                src = ps[:np_, pd0:pd0 + 2, :]
                self.resid_out(W, cnt, np_, src, [pb[pd0], pb[pd0 + 1]], hb[:np_, j, :], b_hb[j], gout, b_gout, 0.5, hdst, t)
                cnt += 1
        P.barrier()

    def stage_qkv(self, li, hsrc, nQ, nK, nV):
        cfg, P, A, ps, pb, dr = self.cfg, self.P, self.A, self.ps, self.pb, self.dr
        A.reset()
        is_na = (li % 2 == 0)
        jj = li // 2
        TCP, NCH = cfg.TCP, cfg.NCH
        Wq = A.alloc([8, 3 * D], BF16)
        b_w = P.bufs(16, "wqkv")
        self.load_w_cast(Wq, dr["na_w_qkv" if is_na else "da_w_qkv"][jj], 8, 3 * D, b_w)
        g2 = A.alloc([D], F32)
        b_g2 = P.buf("g2")
        self.load_rep(g2, dr["norm_g"][li, 2, :], b_g2)
        if is_na:
            bqk = A.alloc([16], F32)
            b_bqk = P.buf("bqk")
            for kq in range(16):
                P.dma("sp", bqk[:, kq:kq + 1], dr["na_b_qkv"][jj, kq * 128:(kq + 1) * 128].rearrange("(p o) -> p o", o=1),
                      (), [b_bqk])
            bv = A.alloc([D], F32)
            b_bv = P.buf("bv")
            self.load_rep(bv, dr["na_b_qkv"][jj, 2 * D:3 * D], b_bv)
        W = self.work_common()
        qk = [A.alloc([16, 512], BF16) for _ in range(2)]
        b_qk = P.bufs(2, "qk")
        vs = [A.alloc([D], BF16) for _ in range(2)]
        b_vs = P.bufs(2, "vs")
        if is_na:
            QTd = dr[nQ].rearrange("(c p k n) -> c p k n", p=128, k=8, n=128)
            KTd = dr[nK].rearrange("(c p k n) -> c p k n", p=128, k=8, n=128)
            Vd = dr[nV].rearrange("(c p f) -> c p f", p=128, f=D)
        else:
            QTd = dr[nQ].rearrange("(k p n) -> p k n", p=128, n=TCP)
            KTd = dr[nK].rearrange("(k p n) -> p k n", p=128, n=TCP)
            Vd = dr[nV].rearrange("(h p c d) -> p h c d", p=128, c=NCH, d=128)
        cnt = 0
        for gi, grp in enumerate(cfg.groups()):
            N = self.norm_T(W, grp, hsrc, g2, b_g2, gi)
            slot = gi % 2
            xnT, b_xnT = W["xnT"][slot], W["b_xnT"][slot]
            qs, b_qs = qk[slot], b_qk[slot]
            for kq in range(16):
                bk = kq % 4
                for k in range(8):
                    P.mm(ps[:, bk, :N], Wq[:, k, kq * 128:(kq + 1) * 128], xnT[:, k, :N], k == 0, k == 7,
                         [b_xnT] + b_w, [pb[bk]])
                if is_na:
                    P.act(qs[:, kq, :N], ps[:, bk, :N], AF.Identity, [pb[bk], b_bqk], [b_qs], bias=bqk[:, kq:kq + 1])
                else:
                    P.cp("act" if kq % 2 else "dve", qs[:, kq, :N], ps[:, bk, :N], [pb[bk]], [b_qs])
            t0 = grp[0][0]
            if is_na:
                for j, (t, np_) in enumerate(grp):
                    P.dma("sp", QTd[t][:, :, 0:np_], qs[:, 0:8, j * 128:j * 128 + np_], [b_qs], [self.b_QT])
                    P.dma("sp", KTd[t][:, :, 0:np_], qs[:, 8:16, j * 128:j * 128 + np_], [b_qs], [self.b_KT])
            else:
                P.dma("sp", QTd[:, :, t0 * 128:t0 * 128 + N], qs[:, 0:8, :N], [b_qs], [self.b_QT])
                P.dma("sp", KTd[:, :, t0 * 128:t0 * 128 + N], qs[:, 8:16, :N], [b_qs], [self.b_KT])
            for j, (t, np_) in enumerate(grp):
                sl = cnt % 2
                for half in range(2):
                    for k in range(8):
                        P.mm(ps[:np_, 4 + half, :], xnT[:, k, j * 128:j * 128 + np_],
                             Wq[:, k, 2 * D + half * 512:2 * D + (half + 1) * 512], k == 0, k == 7,
                             [b_xnT] + b_w, [pb[4 + half]])
                src = ps[:np_, 4:6, :]
                if is_na:
                    P.tt("dve", vs[sl][:np_], src, bv[:np_], ALU.add, [pb[4], pb[5], b_bv], [b_vs[sl]])
                else:
                    P.cp("dve", vs[sl][:np_], src, [pb[4], pb[5]], [b_vs[sl]])
                if is_na:
                    P.dma("sp", Vd[t][0:np_, :], vs[sl][:np_], [b_vs[sl]], [self.b_V])
                else:
                    P.dma("sp", Vd[0:np_, :, t, :], vs[sl][:np_].rearrange("p (h d) -> p h d", h=8), [b_vs[sl]], [self.b_V])
                cnt += 1
        P.barrier()

    def stage_gather(self, li):
        P, dr, cfg = self.P, self.dr, self.cfg
        groups = [[2 * i, 2 * i + 1] for i in range(cfg.B)]
        TCP, NGT = cfg.TCP, cfg.NGT
        CH = 131072

        def cc(src2d, dst2d, rb, wb):
            P.add("pool", lambda e: e.collective_compute("AllGather", ALU.bypass, replica_groups=groups,
                                                         ins=[src2d], outs=[dst2d]), [rb], [wb], dma=True, inc=1)

        if li % 2 == 0:
            for nm, bsrc, bdst in (("KT", self.b_KT, self.b_KTall), ("V", self.b_V, self.b_Vall)):
                lo = dr[nm][CH:3 * CH].rearrange("(a n) -> a n", n=1024)
                hi = dr[nm][(NGT - 1) * CH:(NGT + 1) * CH].rearrange("(a n) -> a n", n=1024)
                cc(lo, dr[nm + "lo_all"].rearrange("r (a n) -> (r a) n", n=1024), bsrc, Buf("x"))
                cc(hi, dr[nm + "hi_all"].rearrange("r (a n) -> (r a) n", n=1024), bsrc, Buf("x"))
        else:
            HS = 128 * TCP
            for nm, bsrc, bdst in (("KT", self.b_KT, self.b_KTall), ("V", self.b_V, self.b_Vall)):
                for h in range(8):
                    cc(dr[nm][h * HS:(h + 1) * HS].rearrange("(a n) -> a n", n=TCP),
                       dr[nm + "_all"][h].rearrange("r (a n) -> (r a) n", n=TCP), bsrc, Buf("x"))
        self.gather_bufs = None
        P.barrier([self.b_KT, self.b_V])

    def stage_attn(self, li):
        if li % 2 == 0:
            self.stage_na(li)
        else:
            self.stage_da(li)

    def stage_oproj(self, li, hsrc, hdst):
        cfg, P, A, ps, pb, dr = self.cfg, self.P, self.A, self.ps, self.pb, self.dr
        A.reset()
        is_na = (li % 2 == 0)
        jj = li // 2
        TCP = cfg.TCP
        Wo = A.alloc([8, D], BF16)
        b_w = P.bufs(8, "wo")
        self.load_w_cast(Wo, dr["na_w_o" if is_na else "da_w_o"][jj], 8, D, b_w)
        g3 = A.alloc([D], F32)
        b_g3 = P.buf("g3")
        self.load_rep(g3, dr["norm_g"][li, 3, :], b_g3)
        if is_na:
            bo = A.alloc([D], F32)
            b_bo = P.buf("bo")
            self.load_rep(bo, dr["na_b_o"][jj, :], b_bo)
        W = self.work_common()
        aoT = [A.alloc([8, 512], BF16) for _ in range(2)]
        b_ao = P.bufs(2, "aoT")
        msb = [A.alloc([D], F32) for _ in range(2)]
        b_msb = P.bufs(2, "msb")
        AOd = dr["AOT"].rearrange("(k p n) -> p k n", p=128, n=TCP)
        hname, hbufs = hsrc
        hd = dr[hname]
        cnt = 0
        for gi, grp in enumerate(cfg.groups()):
            slot = gi % 2
            hb, b_hb = W["hb"][slot], W["b_hb"][slot]
            N = sum(np_ for _, np_ in grp)
            t0 = grp[0][0]
            P.dma("sp", aoT[slot][:, :, :N], AOd[:, :, t0 * 128:t0 * 128 + N], [self.b_AOT], [b_ao[slot]])
            for j, (t, np_) in enumerate(grp):
                P.dma("sp", hb[:np_, j, :], hd[t * 128:t * 128 + np_, :], [hbufs[t]], [b_hb[j]])
            for j, (t, np_) in enumerate(grp):
                pd0 = 4 if cnt % 2 == 0 else 2
                for half in range(2):
                    for k in range(8):
                        P.mm(ps[:np_, pd0 + half, :], aoT[slot][:, k, j * 128:j * 128 + np_], Wo[:, k, half * 512:(half + 1) * 512],
                             k == 0, k == 7, [b_ao[slot]] + b_w, [pb[pd0 + half]])
                src = ps[:np_, pd0:pd0 + 2, :]
                srcb = [pb[pd0], pb[pd0 + 1]]
                if is_na:
                    sl = cnt % 2
                    P.tt("dve", msb[sl][:np_], src, bo[:np_], ALU.add, srcb + [b_bo], [b_msb[sl]])
                    src, srcb = msb[sl][:np_], [b_msb[sl]]
                self.resid_out(W, cnt, np_, src, srcb, hb[:np_, j, :], b_hb[j], g3, b_g3, 1.0, hdst, t)
                cnt += 1
        P.barrier()

    def stage_na(self, li):
        cfg, P, A, ps, pb, dr = self.cfg, self.P, self.A, self.ps, self.pb, self.dr
        A.reset()
        jj = li // 2
        TCP, NCH, NGT = cfg.TCP, cfg.NCH, cfg.NGT
        QTd = dr["QT"].rearrange("(c p k n) -> c p k n", p=128, k=8, n=128)
        KTo = dr["KT"].rearrange("(c p k n) -> c p k n", p=128, k=8, n=128)
        Vo = dr["V"].rearrange("(c p f) -> c p f", p=128, f=D)
        KTlo = dr["KTlo_all"].rearrange("r (c p k n) -> r c p k n", p=128, k=8, n=128)
        KThi = dr["KThi_all"].rearrange("r (c p k n) -> r c p k n", p=128, k=8, n=128)
        Vlo = dr["Vlo_all"].rearrange("r (c p f) -> r c p f", p=128, f=D)
        Vhi = dr["Vhi_all"].rearrange("r (c p f) -> r c p f", p=128, f=D)
        AOd = dr["AOT"].rearrange("(k p n) -> p k n", p=128, n=TCP)
        maskd = dr["na_mask"][jj]
        metab = A.alloc([16], F32)
        b_mb = P.buf("metab")
        P.dma("sp", metab[0:16, :], dr["na_meta_bias"][jj].rearrange("h m -> m h"), (), [b_mb], allow_slow_non_contiguous=True)
        KTm = A.alloc([8, 16], BF16)
        Vm = A.alloc([D], BF16)
        b_km = P.buf("kvm")
        P.dma("sp", KTm, KTo[0][:, :, 0:16], [self.b_KT], [b_km])
        P.dma("sp", Vm[0:16, :], Vo[0][0:16, :], [self.b_V], [b_km])
        mk = [A.alloc([16 * 6 * 128], F32) for _ in range(2)]
        b_mk = P.bufs(2, "mk")
        QTb = [A.alloc([8, 128], BF16) for _ in range(2)]
        b_q = P.bufs(2, "QTb")
        KTw = [A.alloc([6, 8 * 128], BF16) for _ in range(2)]
        Vw = [A.alloc([6, D], BF16) for _ in range(2)]
        b_kw = [P.bufs(6, "KTw") for _ in range(2)]
        b_vw = [P.bufs(6, "Vw") for _ in range(2)]
        sm = [A.alloc([6 * 128], F32) for _ in range(2)]
        b_sm = P.bufs(2, "sm")
        pt = [A.alloc([6 * 128], BF16) for _ in range(3)]
        b_pt = P.bufs(3, "pt")
        ptm = [A.alloc([128], BF16) for _ in range(3)]
        b_ptm = P.bufs(3, "ptm")
        rd = [A.alloc([128], F32) for _ in range(2)]
        b_rd = P.bufs(2, "rd")
        ao = [A.alloc([8, 128], BF16) for _ in range(2)]
        b_ao = P.bufs(2, "ao")

        blocks = [-1] + list(range(NGT))

        def issue_loads(bi):
            b = blocks[bi]
            sl = bi % 2
            chunk = 0 if b < 0 else b + 1
            nq = 16 if b < 0 else 128
            P.dma("sp", QTb[sl][:, :, :nq], QTd[chunk][:, :, 0:nq], [self.b_QT], [b_q[sl]])
            if b < 0:
                return
            offs = [-2, -1, 0, 1, 2] + ([3] if b == 0 else ([-3] if b == NGT - 1 else []))
            for s, of in enumerate(offs):
                l = b + of
                if l < 0:
                    ksrc, vsrc, rb = KThi[0][l + 2], Vhi[0][l + 2], [self.b_KTall, self.b_Vall]
                elif l >= NGT:
                    ksrc, vsrc, rb = KTlo[1][l - NGT], Vlo[1][l - NGT], [self.b_KTall, self.b_Vall]
                else:
                    ksrc, vsrc, rb = KTo[l + 1], Vo[l + 1], [self.b_KT, self.b_V]
                P.dma("sp", KTw[sl][:, s, :], ksrc.rearrange("p k n -> p (k n)"), rb, [b_kw[sl][s]])
                P.dma("pool", Vw[sl][:, s, :], vsrc, rb, [b_vw[sl][s]])

        cur_var = {"v": None, "slot": 0}

        def ensure_mask(b):
            v = cfg.var_of_block(b)
            if cur_var["v"] != v:
                cur_var["slot"] ^= 1
                cur_var["v"] = v
                P.dma("sp", mk[cur_var["slot"]], maskd[v], (), [b_mk[cur_var["slot"]]])
            return cur_var["slot"]

        issue_loads(0)
        hcount = 0
        for bi, b in enumerate(blocks):
            if bi + 1 < len(blocks):
                issue_loads(bi + 1)
            sl = bi % 2
            nq = 16 if b < 0 else 128
            nsl = 0 if b < 0 else (6 if b in (0, NGT - 1) else 5)
            if b >= 0:
                ms = ensure_mask(b)
                mkv = mk[ms].rearrange("p (h s q) -> p h s q", h=16, s=6)
            heads = []
            for hi in range(16):
                st_ = hcount % 2
                heads.append((hi, st_, hcount % 3))
                hcount += 1

            def emit_qk(hi, st_):
                k, hh = hi // 2, hi % 2
                r0 = 64 * hh
                bx, by = 2 * st_, 2 * st_ + 1
                for s in range(nsl):
                    outp = ps[:, bx, s * 128:s * 128 + nq] if s < 4 else ps[:, by, (s - 4) * 128:(s - 4) * 128 + nq]
                    P.mm(outp, KTw[sl][r0:r0 + 64, s, k * 128:(k + 1) * 128], QTb[sl][r0:r0 + 64, k, :nq], True, True,
                         [b_kw[sl][s], b_q[sl]], [pb[bx] if s < 4 else pb[by]])
                P.mm(ps[0:16, by, 256:256 + nq], KTm[r0:r0 + 64, k, :], QTb[sl][r0:r0 + 64, k, :nq], True, True,
                     [b_km, b_q[sl]], [pb[by]])

            emit_qk(heads[0][0], heads[0][1])
            for (hi, st_, pi) in heads:
                if hi + 1 < 16:
                    emit_qk(heads[hi + 1][0], heads[hi + 1][1])
                k, hh = hi // 2, hi % 2
                h = hi
                r0 = 64 * hh
                bx, by = 2 * st_, 2 * st_ + 1
                bo = 4 + k % 2
                if nsl:
                    smv = sm[st_]
                    P.stt(smv[:, 0:512], ps[:, bx, :], 0.125, mkv[:, h, 0:4, :].rearrange("p s q -> p (s q)"),
                          ALU.mult, ALU.add, [pb[bx], b_mk[ms]], [b_sm[st_]])
                    P.stt(smv[:, 512:nsl * 128], ps[:, by, 0:(nsl - 4) * 128], 0.125,
                          mkv[:, h, 4:nsl, :].rearrange("p s q -> p (s q)"), ALU.mult, ALU.add,
                          [pb[by], b_mk[ms]], [b_sm[st_]])
                    P.act(pt[pi][:, 0:nsl * 128], smv[:, 0:nsl * 128], AF.Exp, [b_sm[st_]], [b_pt[pi]])
                P.act(ptm[pi][0:16, :nq], ps[0:16, by, 256:256 + nq], AF.Exp, [pb[by], b_mb], [b_ptm[pi]],
                      bias=metab[0:16, h:h + 1], scale=0.125)
                for which in range(2):
                    col = 128 * which
                    for s in range(nsl):
                        lhs = Vw[sl][:, s, k * 128 + r0:k * 128 + r0 + 64] if which == 0 else self.ones_b[:, 0:64]
                        rds = [b_vw[sl][s], b_pt[pi]] if which == 0 else [self.b_c, b_pt[pi]]
                        P.mm(ps[r0:r0 + 64, bo, col:col + nq], lhs, pt[pi][:, s * 128:s * 128 + nq], s == 0, False, rds, [pb[bo]])
                    lhs = Vm[0:16, k * 128 + r0:k * 128 + r0 + 64] if which == 0 else self.ones_b[0:16, 0:64]
                    P.mm(ps[r0:r0 + 64, bo, col:col + nq], lhs, ptm[pi][0:16, :nq], nsl == 0, True,
                         [b_km, self.b_c, b_ptm[pi]], [pb[bo]])
                if hh == 1:
                    rs = k % 2
                    P.recip(rd[rs][:, :nq], ps[:, bo, 128:128 + nq], [pb[bo]], [b_rd[rs]])
                    P.tt("dve", ao[sl][:, k, :nq], ps[:, bo, 0:nq], rd[rs][:, :nq], ALU.mult, [pb[bo], b_rd[rs]], [b_ao[sl]])
            col0 = 0 if b < 0 else (b + 1) * 128
            P.dma("sp", AOd[:, :, col0:col0 + nq], ao[sl][:, :, :nq], [b_ao[sl]], [self.b_AOT])
        P.barrier()

    def stage_da(self, li):
        cfg, P, A, ps, pb, dr = self.cfg, self.P, self.A, self.ps, self.pb, self.dr
        A.reset()
        jj = li // 2
        TCP, NCH, NGT, NKC, NQ = cfg.TCP, cfg.NCH, cfg.NGT, cfg.NKC, cfg.NQ
        lam_init = 0.8 - 0.6 * math.exp(-0.3 * li)
        QTd = dr["QT"].rearrange("(k p n) -> k p n", p=128, n=TCP)
        KTa = dr["KT_all"].rearrange("k r (p n) -> k p r n", p=128)
        Va = dr["V_all"].rearrange("h r (p c d) -> h p r c d", p=128, c=NCH)
        AOd = dr["AOT"].rearrange("(k p n) -> k p n", p=128, n=TCP)
        lam = A.alloc([4, 64], F32)
        prod = A.alloc([2, 64], F32)
        sc = A.alloc([8], F32)
        b_l = P.buf("lam")
        P.dma("sp", lam, dr["da_lambda"][jj].rearrange("(a d) -> a d", a=4).partition_broadcast(128), (), [b_l])
        P.tt("dve", prod[:, 0, :], lam[:, 0, :], lam[:, 1, :], ALU.mult, [b_l], [b_l])
        P.tt("dve", prod[:, 1, :], lam[:, 2, :], lam[:, 3, :], ALU.mult, [b_l], [b_l])
        P.add("dve", lambda e: e.tensor_reduce(out=sc[:, 0:2], in_=prod, axis=mybir.AxisListType.X, op=ALU.add), [b_l], [b_l])
        P.act(sc[:, 2:4], sc[:, 0:2], AF.Exp, [b_l], [b_l])
        P.tt("dve", sc[:, 4:5], sc[:, 3:4], sc[:, 2:3], ALU.subtract, [b_l], [b_l])
        P.add("dve", lambda e: e.tensor_scalar(out=sc[:, 5:6], in0=sc[:, 4:5], scalar1=-lam_init, scalar2=None, op0=ALU.add), [b_l], [b_l])
        neglam = sc[:, 5:6]
        P.dma("sp", sc[:, 6:7], dr["da_subln_g"][jj].rearrange("(p o) -> p o", o=1), (), [b_l])
        P.add("dve", lambda e: e.tensor_scalar(out=sc[:, 7:8], in0=sc[:, 6:7], scalar1=1.0 - lam_init, scalar2=None, op0=ALU.mult), [b_l], [b_l])
        gsc = sc[:, 7:8]
        KTh = [A.alloc([2, TCP], BF16) for _ in range(2)]
        Vh = [A.alloc([2 * NCH, 128], BF16) for _ in range(2)]
        QTh = [A.alloc([TCP], BF16) for _ in range(2)]
        Gh = [A.alloc([2, 1152], F32) for _ in range(2)]
        Gmh = [A.alloc([512], F32) for _ in range(2)]
        Gxh = [A.alloc([2, 512], F32) for _ in range(2)]
        cbh = [A.alloc([(NQ + 1) * NKC], F32) for _ in range(2)]
        Bmh = [A.alloc([NKC, 16], F32) for _ in range(2)]
        b_hd = [P.bufs(8, "hd") for _ in range(2)]
        pt = [A.alloc([2, 512], BF16) for _ in range(3)]
        b_pt = P.bufs(3, "pt")
        tmpb = [A.alloc([2, 512], F32) for _ in range(2)]
        b_tmpb = P.bufs(2, "tmpb")
        o_raw = [A.alloc([2, 512], F32) for _ in range(2)]
        b_or = [P.bufs(2, "oraw") for _ in range(2)]
        r_raw = [A.alloc([512], F32) for _ in range(2)]
        b_rr = P.bufs(2, "rraw")
        pending = []

        def tick():
            for a in pending:
                a[0] -= 1
            while pending and pending[0][0] <= 0:
                pending.pop(0)[1]()

        def flush():
            while pending:
                pending.pop(0)[1]()
        o0 = A.alloc([512], F32)
        o1 = A.alloc([512], F32)
        sq = A.alloc([512], F32)
        rt = A.alloc([512], F32)
        b_o = P.bufs(4, "o")
        aob = [A.alloc([512], BF16) for _ in range(2)]
        b_aob = P.bufs(2, "aob")

        def load_head(h):
            sl = h % 2
            bh = b_hd[sl]
            P.dma("sp", KTh[sl], KTa[h], [self.b_KTall], [bh[0]])
            P.dma("pool", Vh[sl].rearrange("p (r c) d -> p r c d", r=2), Va[h], [self.b_Vall], [bh[1]])
            P.dma("sp", QTh[sl], QTd[h], [self.b_QT], [bh[2]])
            P.dma("sp", Gh[sl].rearrange("p r x -> p (r x)"), dr["da_G"][h], (), [bh[3]])
            P.dma("sp", Gmh[sl][0:16, :], dr["da_Gm"][h], (), [bh[4]])
            P.dma("sp", Gxh[sl].rearrange("p r x -> p (r x)"), dr["da_Gx"][h], (), [bh[7]])
            P.dma("sp", cbh[sl], dr["da_cb"][h], (), [bh[5]])
            P.dma("sp", Bmh[sl].rearrange("p c q -> p (c q)"), dr["da_Bm"][h], (), [bh[6]])

        load_head(0)
        it = 0
        oc = 0
        for h in range(8):
            if h + 1 < 8:
                load_head(h + 1)
            sl = h % 2
            bh = b_hd[sl]
            for qc in range(NQ + 1):
                N = 16 if qc == 0 else 512
                q0 = 0 if qc == 0 else 128 + (qc - 1) * 512

                def kinfo(kc):
                    if kc == 0:
                        return 16, 0, 0
                    r, jl = (kc - 1) // NGT, (kc - 1) % NGT
                    return 128, r, jl + 1

                def qk(kc, itn):
                    nk, r, ch = kinfo(kc)
                    st_ = itn % 2
                    for s in range(2):
                        bk = 2 * st_ + s
                        P.mm(ps[:nk, bk, :N], KTh[sl][64 * s:64 * s + 64, r, ch * 128:ch * 128 + nk],
                             QTh[sl][64 * s:64 * s + 64, q0:q0 + N], True, True, [bh[0], bh[2]], [pb[bk]])

                qk(0, it)
                for kc in range(NKC):
                    if kc + 1 < NKC:
                        qk(kc + 1, it + 1)
                    nk, r, ch = kinfo(kc)
                    st_ = it % 2
                    pi = it % 3
                    it += 1
                    table = None
                    if qc == 0:
                        table, tb = Bmh[sl][:nk, kc, :], bh[6]
                    elif kc == 0:
                        if qc == 1:
                            table, tb = Gmh[sl][0:16, :], bh[4]
                    else:
                        d = (ch - 1) - 4 * (qc - 1)
                        if -1 <= d <= 4:
                            table, tb = Gh[sl][:, r, 512 - 128 * d:1024 - 128 * d], bh[3]
                        elif qc == 1 and r == 0 and ch == NGT:
                            table, tb = Gxh[sl][:, 0, :], bh[7]
                        elif qc == NQ and r == 1 and ch == 1:
                            table, tb = Gxh[sl][:, 1, :], bh[7]
                    b0, b1 = 2 * st_, 2 * st_ + 1
                    tsl = it % 2
                    if table is not None:
                        for s in range(2):
                            P.stt(tmpb[tsl][:nk, s, :N], ps[:nk, 2 * st_ + s, :N], 0.125, table, ALU.mult, ALU.add,
                                  [pb[2 * st_ + s], tb], [b_tmpb[tsl]])
                        P.act(pt[pi][:nk, :, :N], tmpb[tsl][:nk, :, :N], AF.Exp, [b_tmpb[tsl]], [b_pt[pi]])
                    else:
                        ci = qc * NKC + kc
                        P.act(pt[pi][:nk, :, :N], ps[:nk, b0:b1 + 1, :N], AF.Exp, [pb[b0], pb[b1], bh[5]], [b_pt[pi]],
                              bias=cbh[sl][:nk, ci:ci + 1], scale=0.125)
                    for s in range(2):
                        P.mm(ps[:, 4 + s, :N], Vh[sl][:nk, r * NCH + ch, :], pt[pi][:nk, s, :N], kc == 0, kc == NKC - 1,
                             [bh[1], b_pt[pi]], [pb[4 + s]])
                    for s in range(2):
                        P.mm(ps[64 * s:64 * s + 64, 6, :N], self.ones_b[:nk, 0:64], pt[pi][:nk, s, :N], kc == 0, kc == NKC - 1,
                             [self.b_c, b_pt[pi]], [pb[6]])
                    tick()
                par = oc % 2
                oc += 1
                flush()
                P.cp("dve", r_raw[par][:, :N], ps[:, 6, :N], [pb[6]], [b_rr[par]])
                P.cp("dve", o_raw[par][:, 0, :N], ps[:, 4, :N], [pb[4]], [b_or[par][0]])
                P.cp("act", o_raw[par][:, 1, :N], ps[:, 5, :N], [pb[5]], [b_or[par][1]])
                P.recip(r_raw[par][:, :N], r_raw[par][:, :N], [b_rr[par]], [b_rr[par]])

                def stepA(par=par, N=N):
                    P.mm(ps[:, 7, :N], self.sel[0], r_raw[par][:, :N], True, True, [self.b_c, b_rr[par]], [pb[7]])
                    P.tt("dve", o0[:, :N], o_raw[par][:, 0, :N], ps[:, 7, :N], ALU.mult, [b_or[par][0], pb[7]], [b_o[0]])

                def stepB(par=par, N=N):
                    P.mm(ps[:, 7, :N], self.sel[1], r_raw[par][:, :N], True, True, [self.b_c, b_rr[par]], [pb[7]])
                    P.stt(o1[:, :N], o_raw[par][:, 1, :N], neglam, ps[:, 7, :N], ALU.mult, ALU.mult,
                          [b_or[par][1], pb[7], b_l], [b_o[1]])
                    P.tt("dve", o0[:, :N], o0[:, :N], o1[:, :N], ALU.add, [b_o[0], b_o[1]], [b_o[0]])
                    P.tt("dve", sq[:, :N], o0[:, :N], o0[:, :N], ALU.mult, [b_o[0]], [b_o[2]])

                def stepC(par=par, N=N):
                    P.mm(ps[:, 7, :N], self.ones_f, sq[:, :N], True, True, [self.b_c, b_o[2]], [pb[7]])

                def stepD(par=par, N=N, h=h, q0=q0):
                    P.act(rt[:, :N], ps[:, 7, :N], AF.Sqrt, [pb[7], self.b_c], [b_o[3]], bias=self.eps, scale=1.0 / 128)
                    P.recip(rt[:, :N], rt[:, :N], [b_o[3]], [b_o[3]])
                    P.stt(aob[par][:, :N], o0[:, :N], gsc, rt[:, :N], ALU.mult, ALU.mult, [b_o[0], b_o[3], b_l], [b_aob[par]])
                    P.dma("sp", AOd[h][:, q0:q0 + N], aob[par][:, :N], [b_aob[par]], [self.b_AOT])

                pending.extend([[3, stepA], [6, stepB], [11, stepC], [14, stepD]])
        flush()
        P.barrier()


PARAM_NAMES = ["norm_g", "ffn_w_gate", "ffn_w_up", "ffn_w_down", "na_w_qkv", "na_b_qkv", "na_w_o", "na_b_o",
               "na_meta_bias", "da_w_qkv", "da_w_o", "da_lambda", "da_subln_g"]

MODE = "fused"
_last_ninst = None


def run_forward(cfg, inputs, mode=None):
    mode = mode or MODE
    B, SEQ, DEPTH = cfg.B, cfg.SEQ, cfg.DEPTH
    ncores = 2 * B
    x = np.asarray(inputs["x"], np.float32)
    meta = np.asarray(inputs["meta_tokens"], np.float32)
    params = {}
    for nm in PARAM_NAMES:
        a = np.ascontiguousarray(np.asarray(inputs[nm], np.float32))
        if nm == "da_lambda":
            a = a.reshape(a.shape[0], 256)
        params[nm] = a
    rpb = np.asarray(inputs["na_rpb"], np.float32)
    tbl = np.asarray(inputs["t5_rel_bias"], np.float32)
    per_core = []
    for c in range(ncores):
        b, half = c // 2, c % 2
        d = dict(params)
        d["na_mask"] = build_na_mask(cfg, rpb, half).reshape(cfg.NLA, cfg.NVAR, 128, 16 * 6 * 128)
        if cfg.NLB:
            G, Gm, cb, Bm, Gx = build_da_tables(cfg, tbl, half)
            d["da_G"] = G.reshape(8, 128, 2 * 1152)
            d["da_Gm"], d["da_cb"], d["da_Bm"], d["da_Gx"] = Gm, cb, Bm, Gx
        else:
            for nm in ("da_w_qkv", "da_w_o", "da_lambda", "da_subln_g"):
                d.pop(nm, None)
        h0 = np.zeros((cfg.TCP, D), np.float32)
        h0[:NMETA] = meta
        h0[128:] = x[b, half * cfg.NG:(half + 1) * cfg.NG]
        d["h_in"] = h0
        per_core.append(d)
    global _last_ninst
    if mode == "fused":
        bld = Builder(cfg, "fused", None)
        nc = bld.build()
        _last_ninst = bld.ninst
        in_maps = [{k: per_core[c][k] for k in bld.in_names} for c in range(ncores)]
        res = run_bass_kernel_spmd(nc, in_maps, core_ids=list(range(ncores)))
        outs = [res.results[c]["out"] for c in range(ncores)]
    else:
        state = [dict() for _ in range(ncores)]
        for seg in range(DEPTH + 1):
            bld = Builder(cfg, "multi", seg)
            nc = bld.build()
            _last_ninst = bld.ninst
            in_maps = []
            for c in range(ncores):
                m = {}
                for k in bld.in_names:
                    m[k] = state[c][k] if k in state[c] else per_core[c][k]
                in_maps.append(m)
            res = run_bass_kernel_spmd(nc, in_maps, core_ids=list(range(ncores)))
            if seg < DEPTH:
                for c in range(ncores):
                    r = res.results[c]
                    state[c] = {"h_in": r["h_out"], "QT": r["QT_o"], "KT": r["KT_o"], "V": r["V_o"]}
                CH = 131072
                HS = 128 * cfg.TCP
                for b in range(B):
                    c0, c1 = state[2 * b], state[2 * b + 1]
                    g = {}
                    if seg % 2 == 1:
                        for nm in ("KT", "V"):
                            a0, a1 = np.asarray(c0[nm]).reshape(8, HS), np.asarray(c1[nm]).reshape(8, HS)
                            g[nm + "_all"] = np.ascontiguousarray(np.stack([a0, a1], axis=1))
                    else:
                        for nm in ("KT", "V"):
                            a0, a1 = np.asarray(c0[nm]), np.asarray(c1[nm])
                            g[nm + "lo_all"] = np.stack([a0[CH:3 * CH], a1[CH:3 * CH]])
                            g[nm + "hi_all"] = np.stack([a0[(cfg.NGT - 1) * CH:(cfg.NGT + 1) * CH],
                                                         a1[(cfg.NGT - 1) * CH:(cfg.NGT + 1) * CH]])
                    for c in (2 * b, 2 * b + 1):
                        state[c].update(g)
            else:
                outs = [res.results[c]["out"] for c in range(ncores)]
    out = np.zeros((B, SEQ, D), np.float32)
    for c in range(ncores):
        b, half = c // 2, c % 2
        out[b, half * cfg.NG:(half + 1) * cfg.NG] = outs[c]
    return out


def kernel(**inputs):
    cfg = Cfg()
    return run_forward(cfg, inputs)
```

```python
import math
from contextlib import ExitStack

import numpy as np
import concourse.bass as bass
import concourse.mybir as mybir
from concourse.bass_utils import run_bass_kernel_spmd

F32 = mybir.dt.float32
BF16 = mybir.dt.bfloat16
AF = mybir.ActivationFunctionType
ALU = mybir.AluOpType

D = 1024
NMETA = 16
GW = 64
RMS_EPS = 1e-6
NEG = -30000.0

ENGS = ("pe", "act", "dve", "pool", "sp")
EPOCH = 30000
NDMA_SEMS = {"sp": 24, "pool": 16, "act": 4, "pe": 2, "dve": 2}


class Buf:
    __slots__ = ("name", "last_w", "rd_eng", "rd_dma", "excl")

    def __init__(self, name, excl=False):
        self.name = name
        self.excl = excl
        self.last_w = None
        self.rd_eng = {}
        self.rd_dma = []


class Op:
    __slots__ = ("eng", "fn", "dma", "deps", "signal", "token", "idx", "prev_dma", "inc")

    def __init__(self, eng, fn, dma):
        self.eng = eng
        self.fn = fn
        self.dma = dma
        self.inc = 16
        self.deps = []
        self.signal = False
        self.token = None
        self.prev_dma = None


class Arena:
    def __init__(self, t, nbytes):
        self.t = t
        self.nbytes = nbytes
        self.off = 0
        self.base = 0

    def alloc(self, shape, dtype):
        n = 1
        for s in shape:
            n *= s
        esz = 4 if dtype == F32 else 2
        nb = (n * esz + 31) // 32 * 32
        assert self.off + nb <= self.nbytes, ("arena overflow", self.off, nb)
        v = self.t[:, self.off // 4:(self.off + nb) // 4]
        self.off += nb
        if dtype != F32:
            v = v.bitcast(dtype)
        v = v[:, 0:n]
        if len(shape) == 2:
            return v.rearrange("p (a b) -> p a b", b=shape[1])
        if len(shape) == 3:
            return v.rearrange("p (a b c) -> p a b c", b=shape[1], c=shape[2])
        return v

    def mark(self):
        self.base = self.off

    def reset(self):
        self.off = self.base


class Prog:
    def __init__(self, nc, stack):
        self.nc = nc
        self.stack = stack
        self.ops = []
        self.live = []

    def buf(self, name="b"):
        b = Buf(name)
        self.live.append(b)
        return b

    def bufs(self, n, name="b"):
        return [self.buf(name) for _ in range(n)]

    def add(self, eng, fn, reads=(), writes=(), dma=False, inc=16):
        op = Op(eng, fn, dma)
        op.inc = inc
        op.idx = len(self.ops)
        deps = {}
        xr = [b for b in reads if b.excl]
        if xr:
            writes = list(writes) + [b for b in xr if b not in writes]
        for b in reads:
            if b.last_w is not None:
                deps[b.last_w.idx] = (b.last_w, True)
        for b in writes:
            if b.last_w is not None and b.last_w.idx not in deps:
                deps[b.last_w.idx] = (b.last_w, False)
            for r in b.rd_eng.values():
                if r.idx not in deps:
                    deps[r.idx] = (r, False)
            for r in b.rd_dma:
                if r.idx not in deps:
                    deps[r.idx] = (r, False)
        for p, raw in deps.values():
            if p is op:
                continue
            same = (p.eng == op.eng) and (not p.dma) and (not op.dma)
            if same and (op.eng == "pe" or not raw):
                continue
            op.deps.append(p)
            p.signal = True
        if fn is not None:
            for b in writes:
                b.last_w = op
                b.rd_eng = {}
                b.rd_dma = []
            for b in reads:
                if b.last_w is not op:
                    if dma:
                        b.rd_dma.append(op)
                    else:
                        b.rd_eng[eng] = op
        self.ops.append(op)
        return op

    def dma(self, eng, out, in_, reads=(), writes=(), **kw):
        return self.add(eng, lambda e: e.dma_start(out=out, in_=in_, **kw), reads, writes, dma=True)

    def barrier(self, extra=()):
        bl = list(self.live) + list(extra)
        for e in ENGS:
            self.add(e, None, reads=bl, writes=bl)
        self.live = []

    def mm(self, out, lhsT, rhs, start, stop, reads, writes):
        return self.add("pe", lambda e: e.matmul(out=out, lhsT=lhsT, rhs=rhs, start=start, stop=stop), reads, writes)

    def tr(self, out, in_, ident, reads, writes):
        return self.add("pe", lambda e: e.transpose(out=out, in_=in_, identity=ident), reads, writes)

    def act(self, out, in_, func, reads, writes, **kw):
        return self.add("act", lambda e: e.activation(out=out, in_=in_, func=func, **kw), reads, writes)

    def stt(self, out, in0, scalar, in1, op0, op1, reads, writes):
        return self.add("dve", lambda e: e.scalar_tensor_tensor(out=out, in0=in0, scalar=scalar, in1=in1, op0=op0, op1=op1), reads, writes)

    def tt(self, eng, out, in0, in1, op, reads, writes):
        return self.add(eng, lambda e: e.tensor_tensor(out=out, in0=in0, in1=in1, op=op), reads, writes)

    def cp(self, eng, out, in_, reads, writes):
        if eng == "act":
            return self.add("act", lambda e: e.copy(out=out, in_=in_), reads, writes)
        return self.add(eng, lambda e: e.tensor_copy(out=out, in_=in_), reads, writes)

    def recip(self, out, in_, reads, writes):
        return self.add("dve", lambda e: e.reciprocal(out=out, in_=in_), reads, writes)

    def memset(self, eng, ap, val, writes):
        return self.add(eng, lambda e: e.memset(ap, val), (), writes)

    def emit(self):
        nc = self.nc
        st = self.stack
        cnt = {e: 0 for e in ENGS}
        epoch_sems = {e: [] for e in ENGS}
        dma_sems = {e: [] for e in ENGS}
        dma_use = {e: [] for e in ENGS}
        dma_last = {e: [] for e in ENGS}
        dma_rr = {e: 0 for e in ENGS}
        for op in self.ops:
            if op.fn is None:
                continue
            e = op.eng
            if op.dma and op.inc == 1:
                sem = st.enter_context(nc.semaphore(f"cc_{op.idx}"))
                op.token = (sem, 1)
                op.signal = True
            elif op.dma:
                if not dma_sems[e]:
                    for j in range(NDMA_SEMS[e]):
                        dma_sems[e].append(st.enter_context(nc.semaphore(f"d_{e}_{j}")))
                        dma_use[e].append(0)
                        dma_last[e].append(None)
                j = dma_rr[e] % len(dma_sems[e])
                dma_rr[e] += 1
                dma_use[e][j] += 1
                op.prev_dma = dma_last[e][j]
                op.token = (dma_sems[e][j], 16 * dma_use[e][j])
                dma_last[e][j] = op.token
                op.signal = True
            elif op.signal:
                k = cnt[e] // EPOCH
                if k >= len(epoch_sems[e]):
                    epoch_sems[e].append(st.enter_context(nc.semaphore(f"s_{e}_{k}")))
                cnt[e] += 1
                op.token = (epoch_sems[e][k], cnt[e] - k * EPOCH)
        per_eng = {e: [] for e in ENGS}
        for op in self.ops:
            per_eng[op.eng].append(op)
        ninst = {e: 0 for e in ENGS}

        def run(e, eo):
            waited = {}
            for op in per_eng[e]:
                toks = [p.token for p in op.deps]
                if op.prev_dma is not None:
                    toks.append(op.prev_dma)
                need = {}
                for (s, v) in toks:
                    key = id(s)
                    if waited.get(key, 0) >= v:
                        continue
                    if key not in need or need[key][1] < v:
                        need[key] = (s, v)
                for key, (s, v) in need.items():
                    eo.wait_ge(s, v)
                    waited[key] = v
                    ninst[e] += 1
                if op.fn is None:
                    continue
                inst = op.fn(eo)
                ninst[e] += 1
                if op.signal:
                    inst.then_inc(op.token[0], op.inc if op.dma else 1)

        with nc.Block() as block:
            @block.tensor
            def _(eo):
                run("pe", eo)

            @block.scalar
            def _(eo):
                run("act", eo)

            @block.vector
            def _(eo):
                run("dve", eo)

            @block.gpsimd
            def _(eo):
                run("pool", eo)

            @block.sync
            def _(eo):
                run("sp", eo)
        self.ninst = ninst


class Cfg:
    def __init__(self, SEQ=8192, DEPTH=4, DFF=2816, B=4):
        self.SEQ, self.DEPTH, self.DFF, self.B = SEQ, DEPTH, DFF, B
        self.ROWS = SEQ // GW
        self.NG = SEQ // 2
        self.NGT = self.NG // 128
        self.NCH = self.NGT + 1
        self.TCP = self.NCH * 128
        self.FCH = DFF // 128
        self.NQ = self.NGT // 4
        self.NKC = 1 + 2 * self.NGT
        self.NLA = (DEPTH + 1) // 2
        self.NLB = DEPTH // 2
        self.NVAR = 5
        assert self.NGT % 4 == 0 and self.NGT >= 8

    def groups(self, gs=4):
        g = [[(0, NMETA)]]
        for c in range(self.NGT // gs):
            g.append([(1 + gs * c + j, 128) for j in range(gs)])
        return g

    def var_of_block(self, b):
        if b == 0:
            return 0
        if b == 1:
            return 1
        if b == self.NGT - 2:
            return 3
        if b == self.NGT - 1:
            return 4
        return 2

    def rep_block(self, v):
        return [0, 1, 2, self.NGT - 2, self.NGT - 1][v]


def t5_bucket_np(rel):
    nb, me = 16, 8
    rel = np.asarray(rel, np.int64)
    ret = np.where(rel > 0, nb, 0)
    n = np.abs(rel)
    nf = np.maximum(n, 1).astype(np.float32)
    large = me + (np.log(nf / np.float32(me)) / np.float32(math.log(128 / me)) * np.float32(nb - me)).astype(np.int32)
    large = np.minimum(large, nb - 1)
    return ret + np.where(n < me, n, large)


def na_mask_index(cfg, half):
    NB = 2 * cfg.NGT
    rows = cfg.ROWS
    kh = min(8, rows)
    p = np.arange(128)
    out = np.full((cfg.NVAR, 6, 128, 128), 465, np.int64)
    for v in range(cfg.NVAR):
        i = half * cfg.NGT + cfg.rep_block(v)
        qr = 2 * i + p // 64
        qc = p % 64
        rs = np.clip(qr - kh // 2, 0, rows - kh)
        cs = np.clip(qc - 8, 0, GW - 16)
        offs = [-2, -1, 0, 1, 2, 3 if v == 0 else (-3 if v == 4 else None)]
        for s in range(6):
            if offs[s] is None:
                continue
            g = i + offs[s]
            if g < 0 or g >= NB:
                continue
            kr = 2 * g + p // 64
            kc = p % 64
            vis = ((kr[:, None] >= rs[None, :]) & (kr[:, None] < rs[None, :] + kh)
                   & (kc[:, None] >= cs[None, :]) & (kc[:, None] < cs[None, :] + 16))
            dy = kr[:, None] - qr[None, :] + 7
            dx = kc[:, None] - qc[None, :] + 15
            idx = dy * 31 + dx
            out[v, s] = np.where(vis, idx, 465)
    return out


def build_na_mask(cfg, rpb, half):
    idx = na_mask_index(cfg, half)
    ext = np.concatenate([rpb.reshape(rpb.shape[0], 16, 465),
                          np.full((rpb.shape[0], 16, 1), NEG, np.float32)], axis=2)
    m = ext[:, :, idx]
    return np.ascontiguousarray(m.transpose(0, 2, 4, 1, 3, 5)).astype(np.float32)


def build_da_tables(cfg, tbl, half):
    NG, NGT, NKC, NQ = cfg.NG, cfg.NGT, cfg.NKC, cfg.NQ
    p = np.arange(128)[:, None]
    x = np.arange(1152)[None, :]
    tblT = tbl.T
    G = np.zeros((8, 128, 2, 1152), np.float32)
    for r in range(2):
        rel = (r - half) * NG + p - x + 512
        G[:, :, r, :] = tblT[:, t5_bucket_np(rel)]
    m = np.arange(16)[:, None]
    qf = np.arange(512)[None, :]
    Gm = tblT[:, t5_bucket_np(m - (16 + half * NG + qf))].astype(np.float32)
    pp = np.arange(128)[:, None]
    Gx = np.zeros((8, 128, 2, 512), np.float32)
    Gx[:, :, 0, :] = tblT[:, t5_bucket_np((0 - half) * NG + 128 * (NGT - 1) + pp - qf)]
    Gx[:, :, 1, :] = tblT[:, t5_bucket_np((1 - half) * NG + pp - 512 * (NQ - 1) - qf)]
    cb = np.zeros((8, NQ + 1, NKC), np.float32)
    for qc in range(1, NQ + 1):
        qpos = 16 + half * NG + (qc - 1) * 512
        for kc in range(NKC):
            kpos = 0 if kc == 0 else 16 + (kc - 1) * 128
            cb[:, qc, kc] = tblT[:, t5_bucket_np(kpos - qpos)]
    cb = np.ascontiguousarray(np.broadcast_to(cb.reshape(8, 1, -1), (8, 128, (NQ + 1) * NKC)))
    kp = np.zeros((128, NKC), np.int64)
    kp[:, 0] = np.arange(128)
    for kc in range(1, NKC):
        kp[:, kc] = 16 + (kc - 1) * 128 + np.arange(128)
    q = np.arange(16)[None, None, :]
    Bm = tblT[:, t5_bucket_np(kp[:, :, None] - q)].astype(np.float32)
    return G, np.ascontiguousarray(Gm), cb, np.ascontiguousarray(Bm.reshape(8, 128, NKC * 16)), Gx.reshape(8, 128, 1024)


class Builder:
    def __init__(self, cfg, mode, seg):
        self.cfg = cfg
        self.mode = mode
        self.seg = seg
        self.nc = bass.Bass("TRN2", target_bir_lowering=False)
        self.dr = {}
        self.in_names = []
        self.out_names = []

    def dram(self, name, shape, dtype, kind):
        t = self.nc.dram_tensor(name, list(shape), dtype, kind=kind)
        self.dr[name] = t.ap()
        if kind == "ExternalInput":
            self.in_names.append(name)
        elif kind == "ExternalOutput":
            self.out_names.append(name)
        return self.dr[name]

    def build(self):
        cfg = self.cfg
        nc = self.nc
        DEPTH, DFF = cfg.DEPTH, cfg.DFF
        seg, fused = self.seg, self.mode == "fused"
        with ExitStack() as st:
            P = Prog(nc, st)
            self.P = P
            at = st.enter_context(nc.sbuf_tensor("arena", [128, 206 * 256], F32))
            self.A = Arena(at, 206 * 1024)
            self.ps = st.enter_context(nc.psum_tensor("ps", [128, 8, 512], F32))
            self.pb = [Buf(f"bank{i}", excl=True) for i in range(8)]
            EI = "ExternalInput"
            self.dram("norm_g", [DEPTH, 6, D], F32, EI)
            self.dram("ffn_w_gate", [DEPTH, 2, D, DFF], F32, EI)
            self.dram("ffn_w_up", [DEPTH, 2, D, DFF], F32, EI)
            self.dram("ffn_w_down", [DEPTH, 2, DFF, D], F32, EI)
            self.dram("na_w_qkv", [cfg.NLA, D, 3 * D], F32, EI)
            self.dram("na_b_qkv", [cfg.NLA, 3 * D], F32, EI)
            self.dram("na_w_o", [cfg.NLA, D, D], F32, EI)
            self.dram("na_b_o", [cfg.NLA, D], F32, EI)
            self.dram("na_meta_bias", [cfg.NLA, 16, 16], F32, EI)
            self.dram("na_mask", [cfg.NLA, cfg.NVAR, 128, 16 * 6 * 128], F32, EI)
            if cfg.NLB:
                self.dram("da_w_qkv", [cfg.NLB, D, 3 * D], F32, EI)
                self.dram("da_w_o", [cfg.NLB, D, D], F32, EI)
                self.dram("da_lambda", [cfg.NLB, 4 * 64], F32, EI)
                self.dram("da_subln_g", [cfg.NLB, 128], F32, EI)
                self.dram("da_G", [8, 128, 2 * 1152], F32, EI)
                self.dram("da_Gm", [8, 16, 512], F32, EI)
                self.dram("da_Gx", [8, 128, 1024], F32, EI)
                self.dram("da_cb", [8, 128, (cfg.NQ + 1) * cfg.NKC], F32, EI)
                self.dram("da_Bm", [8, 128, cfg.NKC * 16], F32, EI)
            TCP, NCH = cfg.TCP, cfg.NCH
            self.dram("h_in", [TCP, D], F32, EI)
            self.b_hin = [Buf("hin") for _ in range(NCH)]
            if fused:
                self.dram("h", [TCP, D], F32, "Internal")
                self.dram("out", [cfg.NG, D], F32, "ExternalOutput")
                for nm in ("QT", "KT", "AOT"):
                    self.dram(nm, [8 * 128 * TCP], BF16, "Internal")
                self.dram("V", [8 * 128 * TCP], BF16, "Internal")
                CH = 131072
                for nm, shp in (("KT_all", [8, 2, 128 * TCP]), ("V_all", [8, 2, 128 * TCP]),
                                ("KTlo_all", [2, 2 * CH]), ("KThi_all", [2, 2 * CH]),
                                ("Vlo_all", [2, 2 * CH]), ("Vhi_all", [2, 2 * CH])):
                    t = self.nc.dram_tensor(nm, shp, BF16, kind="Internal", addr_space="Local")
                    self.dr[nm] = t.ap()
            else:
                if seg > 0:
                    self.dram("QT", [8 * 128 * TCP], BF16, EI)
                    self.dram("KT", [8 * 128 * TCP], BF16, EI)
                    self.dram("V", [8 * 128 * TCP], BF16, EI)
                    if (seg - 1) % 2 == 1:
                        self.dram("KT_all", [8, 2, 128 * TCP], BF16, EI)
                        self.dram("V_all", [8, 2, 128 * TCP], BF16, EI)
                    else:
                        for nm in ("KTlo_all", "KThi_all", "Vlo_all", "Vhi_all"):
                            self.dram(nm, [2, 2 * 131072], BF16, EI)
                    import os
                    self.attn_only = "attnonly" in os.environ.get("KDBG1", "")
                    self.dram("AOT", [8 * 128 * TCP], BF16, "ExternalOutput" if self.attn_only else "Internal")
                    self.dram("h", [TCP, D], F32, "Internal")
                if seg > 0 and self.attn_only:
                    pass
                elif seg < DEPTH:
                    self.dram("h_out", [TCP, D], F32, "ExternalOutput")
                    self.dram("QT_o", [8 * 128 * TCP], BF16, "ExternalOutput")
                    self.dram("KT_o", [8 * 128 * TCP], BF16, "ExternalOutput")
                    self.dram("V_o", [8 * 128 * TCP], BF16, "ExternalOutput")
                else:
                    self.dram("out", [cfg.NG, D], F32, "ExternalOutput")
            self.b_h = [Buf("h") for _ in range(NCH)]
            self.b_hout = [Buf("hout") for _ in range(NCH)]
            self.b_out = [Buf("out") for _ in range(NCH)]
            self.b_QT, self.b_KT, self.b_V, self.b_AOT = Buf("QT"), Buf("KT"), Buf("V"), Buf("AOT")
            self.b_KTall, self.b_Vall = Buf("KTall"), Buf("Vall")

            self.consts()
            dr = self.dr
            if fused:
                hcur = ("h_in", self.b_hin)
                for i in range(DEPTH):
                    self.stage_ffn(i, 0, hcur, ("h", self.b_h))
                    hcur = ("h", self.b_h)
                    self.stage_qkv(i, hcur, "QT", "KT", "V")
                    self.stage_gather(i)
                    self.stage_attn(i)
                    self.stage_oproj(i, hcur, hcur)
                    last = (i == DEPTH - 1)
                    self.stage_ffn(i, 1, hcur, ("out", self.b_out) if last else hcur)
                P.barrier(self.b_out)
            else:
                hcur = ("h_in", self.b_hin)
                if seg > 0:
                    i = seg - 1
                    self.stage_attn(i)
                    if self.attn_only:
                        P.barrier([self.b_AOT])
                        P.emit()
                        self.ninst = P.ninst
                        return nc
                    self.stage_oproj(i, hcur, ("h", self.b_h))
                    hcur = ("h", self.b_h)
                    if seg == DEPTH:
                        self.stage_ffn(i, 1, hcur, ("out", self.b_out))
                    else:
                        self.stage_ffn(i, 1, hcur, hcur)
                if seg < DEPTH:
                    import os
                    dbg = os.environ.get("KDBG", "ffn,qkv")
                    if "ffn" in dbg:
                        self.stage_ffn(seg, 0, hcur, ("h_out", self.b_hout))
                    if "qkv" in dbg:
                        self.stage_qkv(seg, ("h_out", self.b_hout) if "ffn" in dbg else hcur, "QT_o", "KT_o", "V_o")
                P.barrier(self.b_out + self.b_hout + [self.b_QT, self.b_KT, self.b_V])
            P.emit()
            self.ninst = P.ninst
        return nc

    def consts(self):
        P, A = self.P, self.A
        self.b_c = P.buf("consts")
        self.identf = A.alloc([128], F32)
        self.ident = A.alloc([128], BF16)
        self.ones_f = A.alloc([128], F32)
        self.ones_b = A.alloc([128], BF16)
        self.eps = A.alloc([1], F32)
        b = self.b_c
        P.memset("pool", self.identf, 0.0, [b])
        P.add("pool", lambda e: e.affine_select(out=self.identf, in_=self.identf, pattern=[[-1, 128]],
                                                compare_op=ALU.not_equal, fill=1.0, base=0,
                                                channel_multiplier=1), [b], [b])
        P.cp("dve", self.ident, self.identf, [b], [b])
        P.memset("dve", self.ones_f, 1.0, [b])
        P.memset("dve", self.ones_b, 1.0, [b])
        P.memset("dve", self.eps, RMS_EPS, [b])
        self.sel = [A.alloc([128], F32) for _ in range(2)]
        for i in range(2):
            P.memset("dve", self.sel[i], 0.0, [b])
            P.memset("dve", self.sel[i][64 * i:64 * i + 64, :], 1.0 / 64, [b])
        A.mark()

    def load_w_cast(self, dst, src, rows_chunks, ncols, bufs):
        P = self.P
        step = ncols
        while step > 2048:
            step //= 2
        i = 0
        for k in range(rows_chunks):
            for c0 in range(0, ncols, step):
                P.dma("pool", dst[:, k, c0:c0 + step], src[k * 128:(k + 1) * 128, c0:c0 + step], (), [bufs[i]])
                i += 1
        return i

    def load_rep(self, dst, vec, b):
        self.P.dma("sp", dst, vec.partition_broadcast(128), (), [b])

    def rstd_from_ss(self, ss, rstd, np_, n, b_ss, b_rstd):
        P = self.P
        P.act(rstd[:np_], ss[:np_], AF.Sqrt, [b_ss, self.b_c], [b_rstd], bias=self.eps[:np_], scale=1.0 / n)
        P.recip(rstd[:np_], rstd[:np_], [b_rstd], [b_rstd])

    def norm_T(self, W, grp, hsrc, g_rep, b_g, gi):
        P, ps, pb = self.P, self.ps, self.pb
        hname, hbufs = hsrc
        hd = self.dr[hname]
        slot = gi % 2
        hb, b_hb = W["hb"][slot], W["b_hb"][slot]
        xnT, b_xnT = W["xnT"][slot], W["b_xnT"][slot]
        N = 0
        ptr = ps[:, 7, :].bitcast(BF16).rearrange("p (k n) -> p k n", n=128)
        for j, (t, np_) in enumerate(grp):
            P.dma("sp", hb[:np_, j, :], hd[t * 128:t * 128 + np_, :], [hbufs[t]], [b_hb[j]])
            sl = (gi * len(grp) + j) % 2
            ss, rstd, xn = W["ss"][sl], W["rstd"][sl], W["xn"][sl]
            b_ss, b_rstd, b_xn = W["b_ss"][sl], W["b_rstd"][sl], W["b_xn"][sl]
            P.act(W["junk"][:np_], hb[:np_, j, :], AF.Square, [b_hb[j]], [W["b_junk"], b_ss], accum_out=ss[:np_])
            self.rstd_from_ss(ss, rstd, np_, D, b_ss, b_rstd)
            P.stt(xn[:np_], hb[:np_, j, :], rstd[:np_], g_rep[:np_], ALU.mult, ALU.mult, [b_hb[j], b_rstd, b_g], [b_xn])
            for k in range(8):
                P.tr(ptr[:, k, :np_], xn[:np_, k * 128:(k + 1) * 128], self.ident[:np_, :np_], [b_xn, self.b_c], [pb[7]])
            P.cp("dve" if j % 2 else "act", xnT[:, :, N:N + np_], ptr[:, :, :np_], [pb[7]], [b_xnT])
            N += np_
        return N

    def work_common(self, gs=4, lean=False):
        P, A = self.P, self.A
        W = {}
        hb0 = A.alloc([gs, D], F32)
        W["hb"] = [hb0, A.alloc([gs, D], F32)]
        bh0 = P.bufs(gs, "hb")
        W["b_hb"] = [bh0, P.bufs(gs, "hb")]
        W["xnT"] = [A.alloc([8, gs * 128], BF16) for _ in range(2)]
        W["b_xnT"] = P.bufs(2, "xnT")
        W["xn"] = [A.alloc([D], BF16) for _ in range(2)]
        W["b_xn"] = P.bufs(2, "xn")
        W["ss"] = [A.alloc([1], F32) for _ in range(2)]
        W["b_ss"] = P.bufs(2, "ss")
        W["rstd"] = [A.alloc([1], F32) for _ in range(2)]
        W["b_rstd"] = P.bufs(2, "rstd")
        W["junk"] = A.alloc([D], BF16)
        W["b_junk"] = P.buf("junk")
        tmp0 = A.alloc([D], F32)
        W["tmp"] = [tmp0, tmp0 if lean else A.alloc([D], F32)]
        bt0 = P.buf("tmp")
        W["b_tmp"] = [bt0, bt0 if lean else P.buf("tmp")]
        W["ho"] = [A.alloc([D], F32) for _ in range(2)]
        W["b_ho"] = P.bufs(2, "ho")
        W["ss2"] = [A.alloc([1], F32) for _ in range(2)]
        W["b_ss2"] = P.bufs(2, "ss2")
        W["rstd2"] = [A.alloc([1], F32) for _ in range(2)]
        W["b_rstd2"] = P.bufs(2, "rstd2")
        return W

    def dst_rows(self, hdst, t, np_):
        name, bufs = hdst
        d = self.dr[name]
        if name == "out":
            if t == 0:
                return None, None
            return d[(t - 1) * 128:(t - 1) * 128 + np_, :], bufs[t]
        return d[t * 128:t * 128 + np_, :], bufs[t]

    def resid_out(self, W, cnt, np_, src, src_bufs, hb_j, b_hb_j, g_rep, b_g, scale, hdst, t):
        P = self.P
        sl = cnt % 2
        ss2, rstd2, tmp, ho = W["ss2"][sl], W["rstd2"][sl], W["tmp"][sl], W["ho"][sl]
        b_ss2, b_rstd2, b_tmp, b_ho = W["b_ss2"][sl], W["b_rstd2"][sl], W["b_tmp"][sl], W["b_ho"][sl]
        P.act(W["junk"][:np_], src, AF.Square, src_bufs, [W["b_junk"], b_ss2], accum_out=ss2[:np_])
        self.rstd_from_ss(ss2, rstd2, np_, D, b_ss2, b_rstd2)
        P.stt(tmp[:np_], src, rstd2[:np_], g_rep[:np_], ALU.mult, ALU.mult, list(src_bufs) + [b_rstd2, b_g], [b_tmp])
        P.stt(ho[:np_], tmp[:np_], float(scale), hb_j, ALU.mult, ALU.add, [b_tmp, b_hb_j], [b_ho])
        dst, bd = self.dst_rows(hdst, t, np_)
        if dst is not None:
            P.dma("pool", dst, ho[:np_], [b_ho], [bd])

    def stage_ffn(self, li, w, hsrc, hdst):
        cfg, P, A, ps, pb, dr = self.cfg, self.P, self.A, self.ps, self.pb, self.dr
        A.reset()
        FCH, DFF = cfg.FCH, cfg.DFF
        Wg = A.alloc([8, DFF], BF16)
        Wu = A.alloc([8, DFF], BF16)
        Wd = A.alloc([FCH, D], BF16)
        b_wg, b_wu, b_wd = P.bufs(32, "wg"), P.bufs(32, "wu"), P.bufs(FCH, "wd")
        n = self.load_w_cast(Wg, dr["ffn_w_gate"][li, w], 8, DFF, b_wg)
        b_wg = b_wg[:n]
        n = self.load_w_cast(Wu, dr["ffn_w_up"][li, w], 8, DFF, b_wu)
        b_wu = b_wu[:n]
        self.load_w_cast(Wd, dr["ffn_w_down"][li, w], FCH, D, b_wd)
        gin, gout = A.alloc([D], F32), A.alloc([D], F32)
        b_gin, b_gout = P.buf("gin"), P.buf("gout")
        self.load_rep(gin, dr["norm_g"][li, 4 * w, :], b_gin)
        self.load_rep(gout, dr["norm_g"][li, 4 * w + 1, :], b_gout)
        gs = 2 if DFF > 2048 else 4
        W = self.work_common(gs, lean=(gs == 2))
        HT = A.alloc([FCH, gs * 128], BF16)
        b_HT = P.buf("HT")
        sg = [A.alloc([gs * 128], F32) for _ in range(2)]
        b_sg = P.bufs(2, "sg")
        cnt = 0
        import os
        kparts = os.environ.get("KPARTS", "norm,gu,down")
        kgroups = int(os.environ.get("KGROUPS", "1000"))
        for gi, grp in enumerate(cfg.groups(gs)):
            if gi >= kgroups:
                break
            N = self.norm_T(W, grp, hsrc, gin, b_gin, gi)
            slot = gi % 2
            xnT, b_xnT = W["xnT"][slot], W["b_xnT"][slot]
            hb, b_hb = W["hb"][slot], W["b_hb"][slot]
            if "gu" not in kparts:
                continue
            for f in range(FCH):
                s2 = f % 2
                bg, bu = 2 * s2, 2 * s2 + 1
                for k in range(8):
                    P.mm(ps[:, bg, :N], Wg[:, k, f * 128:(f + 1) * 128], xnT[:, k, :N], k == 0, k == 7,
                         [b_xnT] + b_wg, [pb[bg]])
                for k in range(8):
                    P.mm(ps[:, bu, :N], Wu[:, k, f * 128:(f + 1) * 128], xnT[:, k, :N], k == 0, k == 7,
                         [b_xnT] + b_wu, [pb[bu]])
                P.act(sg[s2][:, :N], ps[:, bg, :N], AF.Silu, [pb[bg]], [b_sg[s2]])
                P.tt("dve", HT[:, f, :N], sg[s2][:, :N], ps[:, bu, :N], ALU.mult, [b_sg[s2], pb[bu]], [b_HT])
            if "down" not in kparts:
                continue
            for j, (t, np_) in enumerate(grp):
                pd0 = 4 if cnt % 2 == 0 else 2
                for half in range(2):
                    for f in range(FCH):
                        P.mm(ps[:np_, pd0 + half, :], HT[:, f, j * 128:j * 128 + np_], Wd[:, f, half * 512:(half + 1) * 512],
                             f == 0, f == FCH - 1, [b_HT, b_wd[f]], [pb[pd0 + half]])
                src = ps[:np_, pd0:pd0 + 2, :]
                self.resid_out(W, cnt, np_, src, [pb[pd0], pb[pd0 + 1]], hb[:np_, j, :], b_hb[j], gout, b_gout, 0.5, hdst, t)
                cnt += 1
        P.barrier()

    def stage_qkv(self, li, hsrc, nQ, nK, nV):
        cfg, P, A, ps, pb, dr = self.cfg, self.P, self.A, self.ps, self.pb, self.dr
        A.reset()
        is_na = (li % 2 == 0)
        jj = li // 2
        TCP, NCH = cfg.TCP, cfg.NCH
        Wq = A.alloc([8, 3 * D], BF16)
        b_w = P.bufs(16, "wqkv")
        self.load_w_cast(Wq, dr["na_w_qkv" if is_na else "da_w_qkv"][jj], 8, 3 * D, b_w)
        g2 = A.alloc([D], F32)
        b_g2 = P.buf("g2")
        self.load_rep(g2, dr["norm_g"][li, 2, :], b_g2)
        if is_na:
            bqk = A.alloc([16], F32)
            b_bqk = P.buf("bqk")
            for kq in range(16):
                P.dma("sp", bqk[:, kq:kq + 1], dr["na_b_qkv"][jj, kq * 128:(kq + 1) * 128].rearrange("(p o) -> p o", o=1),
                      (), [b_bqk])
            bv = A.alloc([D], F32)
            b_bv = P.buf("bv")
            self.load_rep(bv, dr["na_b_qkv"][jj, 2 * D:3 * D], b_bv)
        W = self.work_common()
        qk = [A.alloc([16, 512], BF16) for _ in range(2)]
        b_qk = P.bufs(2, "qk")
        vs = [A.alloc([D], BF16) for _ in range(2)]
        b_vs = P.bufs(2, "vs")
        if is_na:
            QTd = dr[nQ].rearrange("(c p k n) -> c p k n", p=128, k=8, n=128)
            KTd = dr[nK].rearrange("(c p k n) -> c p k n", p=128, k=8, n=128)
            Vd = dr[nV].rearrange("(c p f) -> c p f", p=128, f=D)
        else:
            QTd = dr[nQ].rearrange("(k p n) -> p k n", p=128, n=TCP)
            KTd = dr[nK].rearrange("(k p n) -> p k n", p=128, n=TCP)
            Vd = dr[nV].rearrange("(h p c d) -> p h c d", p=128, c=NCH, d=128)
        cnt = 0
        for gi, grp in enumerate(cfg.groups()):
            N = self.norm_T(W, grp, hsrc, g2, b_g2, gi)
            slot = gi % 2
            xnT, b_xnT = W["xnT"][slot], W["b_xnT"][slot]
            qs, b_qs = qk[slot], b_qk[slot]
            for kq in range(16):
                bk = kq % 4
                for k in range(8):
                    P.mm(ps[:, bk, :N], Wq[:, k, kq * 128:(kq + 1) * 128], xnT[:, k, :N], k == 0, k == 7,
                         [b_xnT] + b_w, [pb[bk]])
                if is_na:
                    P.act(qs[:, kq, :N], ps[:, bk, :N], AF.Identity, [pb[bk], b_bqk], [b_qs], bias=bqk[:, kq:kq + 1])
                else:
                    P.cp("act" if kq % 2 else "dve", qs[:, kq, :N], ps[:, bk, :N], [pb[bk]], [b_qs])
            t0 = grp[0][0]
            if is_na:
                for j, (t, np_) in enumerate(grp):
                    P.dma("pool", QTd[t][:, :, 0:np_], qs[:, 0:8, j * 128:j * 128 + np_], [b_qs], [self.b_QT])
                    P.dma("pool", KTd[t][:, :, 0:np_], qs[:, 8:16, j * 128:j * 128 + np_], [b_qs], [self.b_KT])
            else:
                P.dma("pool", QTd[:, :, t0 * 128:t0 * 128 + N], qs[:, 0:8, :N], [b_qs], [self.b_QT])
                P.dma("pool", KTd[:, :, t0 * 128:t0 * 128 + N], qs[:, 8:16, :N], [b_qs], [self.b_KT])
            for j, (t, np_) in enumerate(grp):
                sl = cnt % 2
                for half in range(2):
                    for k in range(8):
                        P.mm(ps[:np_, 4 + half, :], xnT[:, k, j * 128:j * 128 + np_],
                             Wq[:, k, 2 * D + half * 512:2 * D + (half + 1) * 512], k == 0, k == 7,
                             [b_xnT] + b_w, [pb[4 + half]])
                src = ps[:np_, 4:6, :]
                if is_na:
                    P.tt("dve", vs[sl][:np_], src, bv[:np_], ALU.add, [pb[4], pb[5], b_bv], [b_vs[sl]])
                else:
                    P.cp("dve", vs[sl][:np_], src, [pb[4], pb[5]], [b_vs[sl]])
                if is_na:
                    P.dma("pool", Vd[t][0:np_, :], vs[sl][:np_], [b_vs[sl]], [self.b_V])
                else:
                    P.dma("pool", Vd[0:np_, :, t, :], vs[sl][:np_].rearrange("p (h d) -> p h d", h=8), [b_vs[sl]], [self.b_V])
                cnt += 1
        P.barrier()

    def stage_gather(self, li):
        P, dr, cfg = self.P, self.dr, self.cfg
        groups = [[2 * i, 2 * i + 1] for i in range(cfg.B)]
        TCP, NGT = cfg.TCP, cfg.NGT
        CH = 131072

        def cc(src2d, dst2d, rb, wb):
            P.add("pool", lambda e: e.collective_compute("AllGather", ALU.bypass, replica_groups=groups,
                                                         ins=[src2d], outs=[dst2d]), [rb], [wb], dma=True, inc=1)

        if li % 2 == 0:
            for nm, bsrc, bdst in (("KT", self.b_KT, self.b_KTall), ("V", self.b_V, self.b_Vall)):
                lo = dr[nm][CH:3 * CH].rearrange("(a n) -> a n", n=1024)
                hi = dr[nm][(NGT - 1) * CH:(NGT + 1) * CH].rearrange("(a n) -> a n", n=1024)
                cc(lo, dr[nm + "lo_all"].rearrange("r (a n) -> (r a) n", n=1024), bsrc, Buf("x"))
                cc(hi, dr[nm + "hi_all"].rearrange("r (a n) -> (r a) n", n=1024), bsrc, Buf("x"))
        else:
            HS = 128 * TCP
            self.b_gh = [[Buf("gk"), Buf("gv")] for _ in range(8)]
            for h in range(8):
                for i, (nm, bsrc) in enumerate((("KT", self.b_KT), ("V", self.b_V))):
                    cc(dr[nm][h * HS:(h + 1) * HS].rearrange("(a n) -> a n", n=TCP),
                       dr[nm + "_all"][h].rearrange("r (a n) -> (r a) n", n=TCP), bsrc, self.b_gh[h][i])
            return
        P.barrier([self.b_KT, self.b_V])

    def stage_attn(self, li):
        if li % 2 == 0:
            self.stage_na(li)
        else:
            self.stage_da(li)

    def stage_oproj(self, li, hsrc, hdst):
        cfg, P, A, ps, pb, dr = self.cfg, self.P, self.A, self.ps, self.pb, self.dr
        A.reset()
        is_na = (li % 2 == 0)
        jj = li // 2
        TCP = cfg.TCP
        Wo = A.alloc([8, D], BF16)
        b_w = P.bufs(8, "wo")
        self.load_w_cast(Wo, dr["na_w_o" if is_na else "da_w_o"][jj], 8, D, b_w)
        g3 = A.alloc([D], F32)
        b_g3 = P.buf("g3")
        self.load_rep(g3, dr["norm_g"][li, 3, :], b_g3)
        if is_na:
            bo = A.alloc([D], F32)
            b_bo = P.buf("bo")
            self.load_rep(bo, dr["na_b_o"][jj, :], b_bo)
        W = self.work_common()
        aoT = [A.alloc([8, 512], BF16) for _ in range(2)]
        b_ao = P.bufs(2, "aoT")
        msb = [A.alloc([D], F32) for _ in range(2)]
        b_msb = P.bufs(2, "msb")
        AOd = dr["AOT"].rearrange("(k p n) -> p k n", p=128, n=TCP)
        hname, hbufs = hsrc
        hd = dr[hname]
        cnt = 0
        for gi, grp in enumerate(cfg.groups()):
            slot = gi % 2
            hb, b_hb = W["hb"][slot], W["b_hb"][slot]
            N = sum(np_ for _, np_ in grp)
            t0 = grp[0][0]
            P.dma("sp", aoT[slot][:, :, :N], AOd[:, :, t0 * 128:t0 * 128 + N], [self.b_AOT], [b_ao[slot]])
            for j, (t, np_) in enumerate(grp):
                P.dma("sp", hb[:np_, j, :], hd[t * 128:t * 128 + np_, :], [hbufs[t]], [b_hb[j]])
            for j, (t, np_) in enumerate(grp):
                pd0 = 4 if cnt % 2 == 0 else 2
                for half in range(2):
                    for k in range(8):
                        P.mm(ps[:np_, pd0 + half, :], aoT[slot][:, k, j * 128:j * 128 + np_], Wo[:, k, half * 512:(half + 1) * 512],
                             k == 0, k == 7, [b_ao[slot]] + b_w, [pb[pd0 + half]])
                src = ps[:np_, pd0:pd0 + 2, :]
                srcb = [pb[pd0], pb[pd0 + 1]]
                if is_na:
                    sl = cnt % 2
                    P.tt("dve", msb[sl][:np_], src, bo[:np_], ALU.add, srcb + [b_bo], [b_msb[sl]])
                    src, srcb = msb[sl][:np_], [b_msb[sl]]
                self.resid_out(W, cnt, np_, src, srcb, hb[:np_, j, :], b_hb[j], g3, b_g3, 1.0, hdst, t)
                cnt += 1
        P.barrier()

    def stage_na(self, li):
        cfg, P, A, ps, pb, dr = self.cfg, self.P, self.A, self.ps, self.pb, self.dr
        A.reset()
        jj = li // 2
        TCP, NCH, NGT = cfg.TCP, cfg.NCH, cfg.NGT
        QTd = dr["QT"].rearrange("(c p k n) -> c p k n", p=128, k=8, n=128)
        KTo = dr["KT"].rearrange("(c p k n) -> c p k n", p=128, k=8, n=128)
        Vo = dr["V"].rearrange("(c p f) -> c p f", p=128, f=D)
        KTlo = dr["KTlo_all"].rearrange("r (c p k n) -> r c p k n", p=128, k=8, n=128)
        KThi = dr["KThi_all"].rearrange("r (c p k n) -> r c p k n", p=128, k=8, n=128)
        Vlo = dr["Vlo_all"].rearrange("r (c p f) -> r c p f", p=128, f=D)
        Vhi = dr["Vhi_all"].rearrange("r (c p f) -> r c p f", p=128, f=D)
        AOd = dr["AOT"].rearrange("(k p n) -> p k n", p=128, n=TCP)
        maskd = dr["na_mask"][jj]
        metab = A.alloc([16], F32)
        b_mb = P.buf("metab")
        P.dma("sp", metab[0:16, :], dr["na_meta_bias"][jj].rearrange("h m -> m h"), (), [b_mb], allow_slow_non_contiguous=True)
        KTm = A.alloc([8, 16], BF16)
        Vm = A.alloc([D], BF16)
        b_km = P.buf("kvm")
        P.dma("sp", KTm, KTo[0][:, :, 0:16], [self.b_KT], [b_km])
        P.dma("sp", Vm[0:16, :], Vo[0][0:16, :], [self.b_V], [b_km])
        mk = [A.alloc([16 * 6 * 128], F32) for _ in range(2)]
        b_mk = P.bufs(2, "mk")
        QTb = [A.alloc([8, 128], BF16) for _ in range(2)]
        b_q = P.bufs(2, "QTb")
        KTw = [A.alloc([6, 8 * 128], BF16) for _ in range(2)]
        Vw = [A.alloc([6, D], BF16) for _ in range(2)]
        b_kw = [P.bufs(6, "KTw") for _ in range(2)]
        b_vw = [P.bufs(6, "Vw") for _ in range(2)]
        sm = [A.alloc([6 * 128], F32) for _ in range(2)]
        b_sm = P.bufs(2, "sm")
        pt = [A.alloc([6 * 128], BF16) for _ in range(3)]
        b_pt = P.bufs(3, "pt")
        ptm = [A.alloc([128], BF16) for _ in range(3)]
        b_ptm = P.bufs(3, "ptm")
        rd = [A.alloc([128], F32) for _ in range(2)]
        b_rd = P.bufs(2, "rd")
        ao = [A.alloc([8, 128], BF16) for _ in range(2)]
        b_ao = P.bufs(2, "ao")

        blocks = [-1] + list(range(NGT))

        def issue_loads(bi):
            b = blocks[bi]
            sl = bi % 2
            chunk = 0 if b < 0 else b + 1
            nq = 16 if b < 0 else 128
            P.dma("sp", QTb[sl][:, :, :nq], QTd[chunk][:, :, 0:nq], [self.b_QT], [b_q[sl]])
            if b < 0:
                return
            offs = [-2, -1, 0, 1, 2] + ([3] if b == 0 else ([-3] if b == NGT - 1 else []))
            for s, of in enumerate(offs):
                l = b + of
                if l < 0:
                    ksrc, vsrc, rb = KThi[0][l + 2], Vhi[0][l + 2], [self.b_KTall, self.b_Vall]
                elif l >= NGT:
                    ksrc, vsrc, rb = KTlo[1][l - NGT], Vlo[1][l - NGT], [self.b_KTall, self.b_Vall]
                else:
                    ksrc, vsrc, rb = KTo[l + 1], Vo[l + 1], [self.b_KT, self.b_V]
                P.dma("sp", KTw[sl][:, s, :], ksrc.rearrange("p k n -> p (k n)"), rb, [b_kw[sl][s]])
                P.dma("pool", Vw[sl][:, s, :], vsrc, rb, [b_vw[sl][s]])

        cur_var = {"v": None, "slot": 0}

        def ensure_mask(b):
            v = cfg.var_of_block(b)
            if cur_var["v"] != v:
                cur_var["slot"] ^= 1
                cur_var["v"] = v
                P.dma("sp", mk[cur_var["slot"]], maskd[v], (), [b_mk[cur_var["slot"]]])
            return cur_var["slot"]

        issue_loads(0)
        hcount = 0
        for bi, b in enumerate(blocks):
            if bi + 1 < len(blocks):
                issue_loads(bi + 1)
            sl = bi % 2
            nq = 16 if b < 0 else 128
            nsl = 0 if b < 0 else (6 if b in (0, NGT - 1) else 5)
            if b >= 0:
                ms = ensure_mask(b)
                mkv = mk[ms].rearrange("p (h s q) -> p h s q", h=16, s=6)
            heads = []
            for hi in range(16):
                st_ = hcount % 2
                heads.append((hi, st_, hcount % 3))
                hcount += 1

            def emit_qk(hi, st_):
                k, hh = hi // 2, hi % 2
                r0 = 64 * hh
                bx, by = 2 * st_, 2 * st_ + 1
                for s in range(nsl):
                    outp = ps[:, bx, s * 128:s * 128 + nq] if s < 4 else ps[:, by, (s - 4) * 128:(s - 4) * 128 + nq]
                    P.mm(outp, KTw[sl][r0:r0 + 64, s, k * 128:(k + 1) * 128], QTb[sl][r0:r0 + 64, k, :nq], True, True,
                         [b_kw[sl][s], b_q[sl]], [pb[bx] if s < 4 else pb[by]])
                P.mm(ps[0:16, by, 256:256 + nq], KTm[r0:r0 + 64, k, :], QTb[sl][r0:r0 + 64, k, :nq], True, True,
                     [b_km, b_q[sl]], [pb[by]])

            emit_qk(heads[0][0], heads[0][1])
            for (hi, st_, pi) in heads:
                if hi + 1 < 16:
                    emit_qk(heads[hi + 1][0], heads[hi + 1][1])
                k, hh = hi // 2, hi % 2
                h = hi
                r0 = 64 * hh
                bx, by = 2 * st_, 2 * st_ + 1
                bo = 4 + k % 2
                if nsl:
                    smv = sm[st_]
                    P.stt(smv[:, 0:512], ps[:, bx, :], 0.125, mkv[:, h, 0:4, :].rearrange("p s q -> p (s q)"),
                          ALU.mult, ALU.add, [pb[bx], b_mk[ms]], [b_sm[st_]])
                    P.stt(smv[:, 512:nsl * 128], ps[:, by, 0:(nsl - 4) * 128], 0.125,
                          mkv[:, h, 4:nsl, :].rearrange("p s q -> p (s q)"), ALU.mult, ALU.add,
                          [pb[by], b_mk[ms]], [b_sm[st_]])
                    P.act(pt[pi][:, 0:nsl * 128], smv[:, 0:nsl * 128], AF.Exp, [b_sm[st_]], [b_pt[pi]])
                P.act(ptm[pi][0:16, :nq], ps[0:16, by, 256:256 + nq], AF.Exp, [pb[by], b_mb], [b_ptm[pi]],
                      bias=metab[0:16, h:h + 1], scale=0.125)
                for which in range(2):
                    col = 128 * which
                    for s in range(nsl):
                        lhs = Vw[sl][:, s, k * 128 + r0:k * 128 + r0 + 64] if which == 0 else self.ones_b[:, 0:64]
                        rds = [b_vw[sl][s], b_pt[pi]] if which == 0 else [self.b_c, b_pt[pi]]
                        P.mm(ps[r0:r0 + 64, bo, col:col + nq], lhs, pt[pi][:, s * 128:s * 128 + nq], s == 0, False, rds, [pb[bo]])
                    lhs = Vm[0:16, k * 128 + r0:k * 128 + r0 + 64] if which == 0 else self.ones_b[0:16, 0:64]
                    P.mm(ps[r0:r0 + 64, bo, col:col + nq], lhs, ptm[pi][0:16, :nq], nsl == 0, True,
                         [b_km, self.b_c, b_ptm[pi]], [pb[bo]])
                if hh == 1:
                    rs = k % 2
                    P.recip(rd[rs][:, :nq], ps[:, bo, 128:128 + nq], [pb[bo]], [b_rd[rs]])
                    P.tt("dve", ao[sl][:, k, :nq], ps[:, bo, 0:nq], rd[rs][:, :nq], ALU.mult, [pb[bo], b_rd[rs]], [b_ao[sl]])
            col0 = 0 if b < 0 else (b + 1) * 128
            P.dma("sp", AOd[:, :, col0:col0 + nq], ao[sl][:, :, :nq], [b_ao[sl]], [self.b_AOT])
        P.barrier()

    def stage_da(self, li):
        cfg, P, A, ps, pb, dr = self.cfg, self.P, self.A, self.ps, self.pb, self.dr
        A.reset()
        jj = li // 2
        TCP, NCH, NGT, NKC, NQ = cfg.TCP, cfg.NCH, cfg.NGT, cfg.NKC, cfg.NQ
        lam_init = 0.8 - 0.6 * math.exp(-0.3 * li)
        QTd = dr["QT"].rearrange("(k p n) -> k p n", p=128, n=TCP)
        KTa = dr["KT_all"].rearrange("k r (p n) -> k p r n", p=128)
        Va = dr["V_all"].rearrange("h r (p c d) -> h p r c d", p=128, c=NCH)
        AOd = dr["AOT"].rearrange("(k p n) -> k p n", p=128, n=TCP)
        lam = A.alloc([4, 64], F32)
        prod = A.alloc([2, 64], F32)
        sc = A.alloc([8], F32)
        b_l = P.buf("lam")
        P.dma("sp", lam, dr["da_lambda"][jj].rearrange("(a d) -> a d", a=4).partition_broadcast(128), (), [b_l])
        P.tt("dve", prod[:, 0, :], lam[:, 0, :], lam[:, 1, :], ALU.mult, [b_l], [b_l])
        P.tt("dve", prod[:, 1, :], lam[:, 2, :], lam[:, 3, :], ALU.mult, [b_l], [b_l])
        P.add("dve", lambda e: e.tensor_reduce(out=sc[:, 0:2], in_=prod, axis=mybir.AxisListType.X, op=ALU.add), [b_l], [b_l])
        P.act(sc[:, 2:4], sc[:, 0:2], AF.Exp, [b_l], [b_l])
        P.tt("dve", sc[:, 4:5], sc[:, 3:4], sc[:, 2:3], ALU.subtract, [b_l], [b_l])
        P.add("dve", lambda e: e.tensor_scalar(out=sc[:, 5:6], in0=sc[:, 4:5], scalar1=-lam_init, scalar2=None, op0=ALU.add), [b_l], [b_l])
        neglam = sc[:, 5:6]
        P.dma("sp", sc[:, 6:7], dr["da_subln_g"][jj].rearrange("(p o) -> p o", o=1), (), [b_l])
        P.add("dve", lambda e: e.tensor_scalar(out=sc[:, 7:8], in0=sc[:, 6:7], scalar1=1.0 - lam_init, scalar2=None, op0=ALU.mult), [b_l], [b_l])
        gsc = sc[:, 7:8]
        KTh = [A.alloc([2, TCP], BF16) for _ in range(2)]
        Vh = [A.alloc([2 * NCH, 128], BF16) for _ in range(2)]
        QTh = [A.alloc([TCP], BF16) for _ in range(2)]
        Gh = [A.alloc([2, 1152], F32) for _ in range(2)]
        Gmh = [A.alloc([512], F32) for _ in range(2)]
        Gxh = [A.alloc([2, 512], F32) for _ in range(2)]
        cbh = [A.alloc([(NQ + 1) * NKC], F32) for _ in range(2)]
        Bmh = [A.alloc([NKC, 16], F32) for _ in range(2)]
        b_hd = [P.bufs(8, "hd") for _ in range(2)]
        pt = [A.alloc([2, 512], BF16) for _ in range(3)]
        b_pt = P.bufs(3, "pt")
        tmpb = [A.alloc([2, 512], F32) for _ in range(2)]
        b_tmpb = P.bufs(2, "tmpb")
        o_raw = [A.alloc([2, 512], F32) for _ in range(2)]
        b_or = [P.bufs(2, "oraw") for _ in range(2)]
        r_raw = [A.alloc([512], F32) for _ in range(2)]
        b_rr = P.bufs(2, "rraw")
        pending = []

        def tick():
            for a in pending:
                a[0] -= 1
            while pending and pending[0][0] <= 0:
                pending.pop(0)[1]()

        def flush():
            while pending:
                pending.pop(0)[1]()
        o0 = A.alloc([512], F32)
        o1 = A.alloc([512], F32)
        sq = A.alloc([512], F32)
        rt = A.alloc([512], F32)
        b_o = P.bufs(4, "o")
        aob = [A.alloc([512], BF16) for _ in range(2)]
        b_aob = P.bufs(2, "aob")

        def load_head(h):
            sl = h % 2
            bh = b_hd[sl]
            gk, gv = (self.b_gh[h] if getattr(self, "b_gh", None) else (self.b_KTall, self.b_Vall))
            P.dma("sp", KTh[sl], KTa[h], [gk], [bh[0]])
            P.dma("pool", Vh[sl].rearrange("p (r c) d -> p r c d", r=2), Va[h], [gv], [bh[1]])
            P.dma("sp", QTh[sl], QTd[h], [self.b_QT], [bh[2]])
            P.dma("sp", Gh[sl].rearrange("p r x -> p (r x)"), dr["da_G"][h], (), [bh[3]])
            P.dma("sp", Gmh[sl][0:16, :], dr["da_Gm"][h], (), [bh[4]])
            P.dma("sp", Gxh[sl].rearrange("p r x -> p (r x)"), dr["da_Gx"][h], (), [bh[7]])
            P.dma("sp", cbh[sl], dr["da_cb"][h], (), [bh[5]])
            P.dma("sp", Bmh[sl].rearrange("p c q -> p (c q)"), dr["da_Bm"][h], (), [bh[6]])

        load_head(0)
        it = 0
        oc = 0
        for h in range(8):
            if h + 1 < 8:
                load_head(h + 1)
            sl = h % 2
            bh = b_hd[sl]
            for qc in range(NQ + 1):
                N = 16 if qc == 0 else 512
                q0 = 0 if qc == 0 else 128 + (qc - 1) * 512

                def kinfo(kc):
                    if kc == 0:
                        return 16, 0, 0
                    r, jl = (kc - 1) // NGT, (kc - 1) % NGT
                    return 128, r, jl + 1

                def qk(kc, itn):
                    nk, r, ch = kinfo(kc)
                    st_ = itn % 2
                    for s in range(2):
                        bk = 2 * st_ + s
                        P.mm(ps[:nk, bk, :N], KTh[sl][64 * s:64 * s + 64, r, ch * 128:ch * 128 + nk],
                             QTh[sl][64 * s:64 * s + 64, q0:q0 + N], True, True, [bh[0], bh[2]], [pb[bk]])

                qk(0, it)
                for kc in range(NKC):
                    if kc + 1 < NKC:
                        qk(kc + 1, it + 1)
                    nk, r, ch = kinfo(kc)
                    st_ = it % 2
                    pi = it % 3
                    it += 1
                    table = None
                    if qc == 0:
                        table, tb = Bmh[sl][:nk, kc, :], bh[6]
                    elif kc == 0:
                        if qc == 1:
                            table, tb = Gmh[sl][0:16, :], bh[4]
                    else:
                        d = (ch - 1) - 4 * (qc - 1)
                        if -1 <= d <= 4:
                            table, tb = Gh[sl][:, r, 512 - 128 * d:1024 - 128 * d], bh[3]
                        elif qc == 1 and r == 0 and ch == NGT:
                            table, tb = Gxh[sl][:, 0, :], bh[7]
                        elif qc == NQ and r == 1 and ch == 1:
                            table, tb = Gxh[sl][:, 1, :], bh[7]
                    b0, b1 = 2 * st_, 2 * st_ + 1
                    tsl = it % 2
                    if table is not None:
                        for s in range(2):
                            P.stt(tmpb[tsl][:nk, s, :N], ps[:nk, 2 * st_ + s, :N], 0.125, table, ALU.mult, ALU.add,
                                  [pb[2 * st_ + s], tb], [b_tmpb[tsl]])
                        P.act(pt[pi][:nk, :, :N], tmpb[tsl][:nk, :, :N], AF.Exp, [b_tmpb[tsl]], [b_pt[pi]])
                    else:
                        ci = qc * NKC + kc
                        P.act(pt[pi][:nk, :, :N], ps[:nk, b0:b1 + 1, :N], AF.Exp, [pb[b0], pb[b1], bh[5]], [b_pt[pi]],
                              bias=cbh[sl][:nk, ci:ci + 1], scale=0.125)
                    for s in range(2):
                        P.mm(ps[:, 4 + s, :N], Vh[sl][:nk, r * NCH + ch, :], pt[pi][:nk, s, :N], kc == 0, kc == NKC - 1,
                             [bh[1], b_pt[pi]], [pb[4 + s]])
                    for s in range(2):
                        P.mm(ps[64 * s:64 * s + 64, 6, :N], self.ones_b[:nk, 0:64], pt[pi][:nk, s, :N], kc == 0, kc == NKC - 1,
                             [self.b_c, b_pt[pi]], [pb[6]])
                    tick()
                par = oc % 2
                oc += 1
                flush()
                P.cp("dve", r_raw[par][:, :N], ps[:, 6, :N], [pb[6]], [b_rr[par]])
                P.cp("dve", o_raw[par][:, 0, :N], ps[:, 4, :N], [pb[4]], [b_or[par][0]])
                P.cp("act", o_raw[par][:, 1, :N], ps[:, 5, :N], [pb[5]], [b_or[par][1]])
                P.recip(r_raw[par][:, :N], r_raw[par][:, :N], [b_rr[par]], [b_rr[par]])

                def stepA(par=par, N=N):
                    P.mm(ps[:, 7, :N], self.sel[0], r_raw[par][:, :N], True, True, [self.b_c, b_rr[par]], [pb[7]])
                    P.tt("dve", o0[:, :N], o_raw[par][:, 0, :N], ps[:, 7, :N], ALU.mult, [b_or[par][0], pb[7]], [b_o[0]])

                def stepB(par=par, N=N):
                    P.mm(ps[:, 7, :N], self.sel[1], r_raw[par][:, :N], True, True, [self.b_c, b_rr[par]], [pb[7]])
                    P.stt(o1[:, :N], o_raw[par][:, 1, :N], neglam, ps[:, 7, :N], ALU.mult, ALU.mult,
                          [b_or[par][1], pb[7], b_l], [b_o[1]])
                    P.tt("dve", o0[:, :N], o0[:, :N], o1[:, :N], ALU.add, [b_o[0], b_o[1]], [b_o[0]])
                    P.tt("dve", sq[:, :N], o0[:, :N], o0[:, :N], ALU.mult, [b_o[0]], [b_o[2]])

                def stepC(par=par, N=N):
                    P.mm(ps[:, 7, :N], self.ones_f, sq[:, :N], True, True, [self.b_c, b_o[2]], [pb[7]])

                def stepD(par=par, N=N, h=h, q0=q0):
                    P.act(rt[:, :N], ps[:, 7, :N], AF.Sqrt, [pb[7], self.b_c], [b_o[3]], bias=self.eps, scale=1.0 / 128)
                    P.recip(rt[:, :N], rt[:, :N], [b_o[3]], [b_o[3]])
                    P.stt(aob[par][:, :N], o0[:, :N], gsc, rt[:, :N], ALU.mult, ALU.mult, [b_o[0], b_o[3], b_l], [b_aob[par]])
                    P.dma("sp", AOd[h][:, q0:q0 + N], aob[par][:, :N], [b_aob[par]], [self.b_AOT])

                pending.extend([[3, stepA], [6, stepB], [11, stepC], [14, stepD]])
        flush()
        extra = [b for pair in (getattr(self, "b_gh", None) or []) for b in pair]
        P.barrier(extra + [self.b_KT, self.b_V])
        self.b_gh = None


PARAM_NAMES = ["norm_g", "ffn_w_gate", "ffn_w_up", "ffn_w_down", "na_w_qkv", "na_b_qkv", "na_w_o", "na_b_o",
               "na_meta_bias", "da_w_qkv", "da_w_o", "da_lambda", "da_subln_g"]

MODE = "fused"
_last_ninst = None


def run_forward(cfg, inputs, mode=None):
    mode = mode or MODE
    B, SEQ, DEPTH = cfg.B, cfg.SEQ, cfg.DEPTH
    ncores = 2 * B
    x = np.asarray(inputs["x"], np.float32)
    meta = np.asarray(inputs["meta_tokens"], np.float32)
    params = {}
    for nm in PARAM_NAMES:
        a = np.ascontiguousarray(np.asarray(inputs[nm], np.float32))
        if nm == "da_lambda":
            a = a.reshape(a.shape[0], 256)
        params[nm] = a
    rpb = np.asarray(inputs["na_rpb"], np.float32)
    tbl = np.asarray(inputs["t5_rel_bias"], np.float32)
    per_core = []
    for c in range(ncores):
        b, half = c // 2, c % 2
        d = dict(params)
        d["na_mask"] = build_na_mask(cfg, rpb, half).reshape(cfg.NLA, cfg.NVAR, 128, 16 * 6 * 128)
        if cfg.NLB:
            G, Gm, cb, Bm, Gx = build_da_tables(cfg, tbl, half)
            d["da_G"] = G.reshape(8, 128, 2 * 1152)
            d["da_Gm"], d["da_cb"], d["da_Bm"], d["da_Gx"] = Gm, cb, Bm, Gx
        else:
            for nm in ("da_w_qkv", "da_w_o", "da_lambda", "da_subln_g"):
                d.pop(nm, None)
        h0 = np.zeros((cfg.TCP, D), np.float32)
        h0[:NMETA] = meta
        h0[128:] = x[b, half * cfg.NG:(half + 1) * cfg.NG]
        d["h_in"] = h0
        per_core.append(d)
    global _last_ninst
    if mode == "fused":
        bld = Builder(cfg, "fused", None)
        nc = bld.build()
        _last_ninst = bld.ninst
        in_maps = [{k: per_core[c][k] for k in bld.in_names} for c in range(ncores)]
        res = run_bass_kernel_spmd(nc, in_maps, core_ids=list(range(ncores)))
        outs = [res.results[c]["out"] for c in range(ncores)]
    else:
        state = [dict() for _ in range(ncores)]
        for seg in range(DEPTH + 1):
            bld = Builder(cfg, "multi", seg)
            nc = bld.build()
            _last_ninst = bld.ninst
            in_maps = []
            for c in range(ncores):
                m = {}
                for k in bld.in_names:
                    m[k] = state[c][k] if k in state[c] else per_core[c][k]
                in_maps.append(m)
            res = run_bass_kernel_spmd(nc, in_maps, core_ids=list(range(ncores)))
            if seg < DEPTH:
                for c in range(ncores):
                    r = res.results[c]
                    state[c] = {"h_in": r["h_out"], "QT": r["QT_o"], "KT": r["KT_o"], "V": r["V_o"]}
                CH = 131072
                HS = 128 * cfg.TCP
                for b in range(B):
                    c0, c1 = state[2 * b], state[2 * b + 1]
                    g = {}
                    if seg % 2 == 1:
                        for nm in ("KT", "V"):
                            a0, a1 = np.asarray(c0[nm]).reshape(8, HS), np.asarray(c1[nm]).reshape(8, HS)
                            g[nm + "_all"] = np.ascontiguousarray(np.stack([a0, a1], axis=1))
                    else:
                        for nm in ("KT", "V"):
                            a0, a1 = np.asarray(c0[nm]), np.asarray(c1[nm])
                            g[nm + "lo_all"] = np.stack([a0[CH:3 * CH], a1[CH:3 * CH]])
                            g[nm + "hi_all"] = np.stack([a0[(cfg.NGT - 1) * CH:(cfg.NGT + 1) * CH],
                                                         a1[(cfg.NGT - 1) * CH:(cfg.NGT + 1) * CH]])
                    for c in (2 * b, 2 * b + 1):
                        state[c].update(g)
            else:
                outs = [res.results[c]["out"] for c in range(ncores)]
    out = np.zeros((B, SEQ, D), np.float32)
    for c in range(ncores):
        b, half = c // 2, c % 2
        out[b, half * cfg.NG:(half + 1) * cfg.NG] = outs[c]
    return out


def kernel(**inputs):
    cfg = Cfg()
    return run_forward(cfg, inputs)
```

```python
import math
from contextlib import ExitStack

import numpy as np
import concourse.bass as bass
import concourse.mybir as mybir
from concourse.bass_utils import run_bass_kernel_spmd

F32 = mybir.dt.float32
BF16 = mybir.dt.bfloat16
AF = mybir.ActivationFunctionType
ALU = mybir.AluOpType

D = 1024
NMETA = 16
GW = 64
RMS_EPS = 1e-6
NEG = -30000.0

ENGS = ("pe", "act", "dve", "pool", "sp")
EPOCH = 30000
NDMA_SEMS = {"sp": 24, "pool": 16, "act": 4, "pe": 2, "dve": 2}


class Buf:
    __slots__ = ("name", "last_w", "rd_eng", "rd_dma", "excl")

    def __init__(self, name, excl=False):
        self.name = name
        self.excl = excl
        self.last_w = None
        self.rd_eng = {}
        self.rd_dma = []


class Op:
    __slots__ = ("eng", "fn", "dma", "deps", "signal", "token", "idx", "prev_dma", "inc")

    def __init__(self, eng, fn, dma):
        self.eng = eng
        self.fn = fn
        self.dma = dma
        self.inc = 16
        self.deps = []
        self.signal = False
        self.token = None
        self.prev_dma = None


class Arena:
    def __init__(self, t, nbytes):
        self.t = t
        self.nbytes = nbytes
        self.off = 0
        self.base = 0

    def alloc(self, shape, dtype):
        n = 1
        for s in shape:
            n *= s
        esz = 4 if dtype == F32 else 2
        nb = (n * esz + 31) // 32 * 32
        assert self.off + nb <= self.nbytes, ("arena overflow", self.off, nb)
        v = self.t[:, self.off // 4:(self.off + nb) // 4]
        self.off += nb
        if dtype != F32:
            v = v.bitcast(dtype)
        v = v[:, 0:n]
        if len(shape) == 2:
            return v.rearrange("p (a b) -> p a b", b=shape[1])
        if len(shape) == 3:
            return v.rearrange("p (a b c) -> p a b c", b=shape[1], c=shape[2])
        return v

    def mark(self):
        self.base = self.off

    def reset(self):
        self.off = self.base


class Prog:
    def __init__(self, nc, stack):
        self.nc = nc
        self.stack = stack
        self.ops = []
        self.live = []

    def buf(self, name="b"):
        b = Buf(name)
        self.live.append(b)
        return b

    def bufs(self, n, name="b"):
        return [self.buf(name) for _ in range(n)]

    def add(self, eng, fn, reads=(), writes=(), dma=False, inc=16):
        op = Op(eng, fn, dma)
        op.inc = inc
        op.idx = len(self.ops)
        deps = {}
        xr = [b for b in reads if b.excl]
        if xr:
            writes = list(writes) + [b for b in xr if b not in writes]
        for b in reads:
            if b.last_w is not None:
                deps[b.last_w.idx] = (b.last_w, True)
        for b in writes:
            if b.last_w is not None and b.last_w.idx not in deps:
                deps[b.last_w.idx] = (b.last_w, False)
            for r in b.rd_eng.values():
                if r.idx not in deps:
                    deps[r.idx] = (r, False)
            for r in b.rd_dma:
                if r.idx not in deps:
                    deps[r.idx] = (r, False)
        for p, raw in deps.values():
            if p is op:
                continue
            same = (p.eng == op.eng) and (not p.dma) and (not op.dma)
            if same and (op.eng == "pe" or not raw):
                continue
            op.deps.append(p)
            p.signal = True
        if fn is not None:
            for b in writes:
                b.last_w = op
                b.rd_eng = {}
                b.rd_dma = []
            for b in reads:
                if b.last_w is not op:
                    if dma:
                        b.rd_dma.append(op)
                    else:
                        b.rd_eng[eng] = op
        self.ops.append(op)
        return op

    def dma(self, eng, out, in_, reads=(), writes=(), **kw):
        return self.add(eng, lambda e: e.dma_start(out=out, in_=in_, **kw), reads, writes, dma=True)

    def barrier(self, extra=()):
        bl = list(self.live) + list(extra)
        for e in ENGS:
            self.add(e, None, reads=bl, writes=bl)
        self.live = []

    def mm(self, out, lhsT, rhs, start, stop, reads, writes):
        return self.add("pe", lambda e: e.matmul(out=out, lhsT=lhsT, rhs=rhs, start=start, stop=stop), reads, writes)

    def tr(self, out, in_, ident, reads, writes):
        return self.add("pe", lambda e: e.transpose(out=out, in_=in_, identity=ident), reads, writes)

    def act(self, out, in_, func, reads, writes, **kw):
        return self.add("act", lambda e: e.activation(out=out, in_=in_, func=func, **kw), reads, writes)

    def stt(self, out, in0, scalar, in1, op0, op1, reads, writes):
        return self.add("dve", lambda e: e.scalar_tensor_tensor(out=out, in0=in0, scalar=scalar, in1=in1, op0=op0, op1=op1), reads, writes)

    def tt(self, eng, out, in0, in1, op, reads, writes):
        return self.add(eng, lambda e: e.tensor_tensor(out=out, in0=in0, in1=in1, op=op), reads, writes)

    def cp(self, eng, out, in_, reads, writes):
        if eng == "act":
            return self.add("act", lambda e: e.copy(out=out, in_=in_), reads, writes)
        return self.add(eng, lambda e: e.tensor_copy(out=out, in_=in_), reads, writes)

    def recip(self, out, in_, reads, writes):
        return self.add("dve", lambda e: e.reciprocal(out=out, in_=in_), reads, writes)

    def memset(self, eng, ap, val, writes):
        return self.add(eng, lambda e: e.memset(ap, val), (), writes)

    def emit(self):
        nc = self.nc
        st = self.stack
        cnt = {e: 0 for e in ENGS}
        epoch_sems = {e: [] for e in ENGS}
        dma_sems = {e: [] for e in ENGS}
        dma_use = {e: [] for e in ENGS}
        dma_last = {e: [] for e in ENGS}
        dma_rr = {e: 0 for e in ENGS}
        for op in self.ops:
            if op.fn is None:
                continue
            e = op.eng
            if op.dma and op.inc == 1:
                sem = st.enter_context(nc.semaphore(f"cc_{op.idx}"))
                op.token = (sem, 1)
                op.signal = True
            elif op.dma:
                if not dma_sems[e]:
                    for j in range(NDMA_SEMS[e]):
                        dma_sems[e].append(st.enter_context(nc.semaphore(f"d_{e}_{j}")))
                        dma_use[e].append(0)
                        dma_last[e].append(None)
                j = dma_rr[e] % len(dma_sems[e])
                dma_rr[e] += 1
                dma_use[e][j] += 1
                op.prev_dma = dma_last[e][j]
                op.token = (dma_sems[e][j], 16 * dma_use[e][j])
                dma_last[e][j] = op.token
                op.signal = True
            elif op.signal:
                k = cnt[e] // EPOCH
                if k >= len(epoch_sems[e]):
                    epoch_sems[e].append(st.enter_context(nc.semaphore(f"s_{e}_{k}")))
                cnt[e] += 1
                op.token = (epoch_sems[e][k], cnt[e] - k * EPOCH)
        per_eng = {e: [] for e in ENGS}
        for op in self.ops:
            per_eng[op.eng].append(op)
        ninst = {e: 0 for e in ENGS}

        def run(e, eo):
            waited = {}
            for op in per_eng[e]:
                toks = [p.token for p in op.deps]
                if op.prev_dma is not None:
                    toks.append(op.prev_dma)
                need = {}
                for (s, v) in toks:
                    key = id(s)
                    if waited.get(key, 0) >= v:
                        continue
                    if key not in need or need[key][1] < v:
                        need[key] = (s, v)
                for key, (s, v) in need.items():
                    eo.wait_ge(s, v)
                    waited[key] = v
                    ninst[e] += 1
                if op.fn is None:
                    continue
                inst = op.fn(eo)
                ninst[e] += 1
                if op.signal:
                    inst.then_inc(op.token[0], op.inc if op.dma else 1)

        with nc.Block() as block:
            @block.tensor
            def _(eo):
                run("pe", eo)

            @block.scalar
            def _(eo):
                run("act", eo)

            @block.vector
            def _(eo):
                run("dve", eo)

            @block.gpsimd
            def _(eo):
                run("pool", eo)

            @block.sync
            def _(eo):
                run("sp", eo)
        self.ninst = ninst


class Cfg:
    def __init__(self, SEQ=8192, DEPTH=4, DFF=2816, B=4):
        self.SEQ, self.DEPTH, self.DFF, self.B = SEQ, DEPTH, DFF, B
        self.ROWS = SEQ // GW
        self.NG = SEQ // 2
        self.NGT = self.NG // 128
        self.NCH = self.NGT + 1
        self.TCP = self.NCH * 128
        self.FCH = DFF // 128
        self.NQ = self.NGT // 4
        self.NKC = 1 + 2 * self.NGT
        self.NLA = (DEPTH + 1) // 2
        self.NLB = DEPTH // 2
        self.NVAR = 5
        assert self.NGT % 4 == 0 and self.NGT >= 8

    def groups(self, gs=4):
        g = [[(0, NMETA)]]
        for c in range(self.NGT // gs):
            g.append([(1 + gs * c + j, 128) for j in range(gs)])
        return g

    def var_of_block(self, b):
        if b == 0:
            return 0
        if b == 1:
            return 1
        if b == self.NGT - 2:
            return 3
        if b == self.NGT - 1:
            return 4
        return 2

    def rep_block(self, v):
        return [0, 1, 2, self.NGT - 2, self.NGT - 1][v]


def t5_bucket_np(rel):
    nb, me = 16, 8
    rel = np.asarray(rel, np.int64)
    ret = np.where(rel > 0, nb, 0)
    n = np.abs(rel)
    nf = np.maximum(n, 1).astype(np.float32)
    large = me + (np.log(nf / np.float32(me)) / np.float32(math.log(128 / me)) * np.float32(nb - me)).astype(np.int32)
    large = np.minimum(large, nb - 1)
    return ret + np.where(n < me, n, large)


def na_mask_index(cfg, half):
    NB = 2 * cfg.NGT
    rows = cfg.ROWS
    kh = min(8, rows)
    p = np.arange(128)
    out = np.full((cfg.NVAR, 6, 128, 128), 465, np.int64)
    for v in range(cfg.NVAR):
        i = half * cfg.NGT + cfg.rep_block(v)
        qr = 2 * i + p // 64
        qc = p % 64
        rs = np.clip(qr - kh // 2, 0, rows - kh)
        cs = np.clip(qc - 8, 0, GW - 16)
        offs = [-2, -1, 0, 1, 2, 3 if v == 0 else (-3 if v == 4 else None)]
        for s in range(6):
            if offs[s] is None:
                continue
            g = i + offs[s]
            if g < 0 or g >= NB:
                continue
            kr = 2 * g + p // 64
            kc = p % 64
            vis = ((kr[:, None] >= rs[None, :]) & (kr[:, None] < rs[None, :] + kh)
                   & (kc[:, None] >= cs[None, :]) & (kc[:, None] < cs[None, :] + 16))
            dy = kr[:, None] - qr[None, :] + 7
            dx = kc[:, None] - qc[None, :] + 15
            idx = dy * 31 + dx
            out[v, s] = np.where(vis, idx, 465)
    return out


def build_na_mask(cfg, rpb, half):
    idx = na_mask_index(cfg, half)
    ext = np.concatenate([rpb.reshape(rpb.shape[0], 16, 465),
                          np.full((rpb.shape[0], 16, 1), NEG, np.float32)], axis=2)
    m = ext[:, :, idx]
    return np.ascontiguousarray(m.transpose(0, 2, 4, 1, 3, 5)).astype(np.float32)


def build_da_tables(cfg, tbl, half):
    NG, NGT, NKC, NQ = cfg.NG, cfg.NGT, cfg.NKC, cfg.NQ
    p = np.arange(128)[:, None]
    x = np.arange(1152)[None, :]
    tblT = tbl.T
    G = np.zeros((8, 128, 2, 1152), np.float32)
    for r in range(2):
        rel = (r - half) * NG + p - x + 512
        G[:, :, r, :] = tblT[:, t5_bucket_np(rel)]
    m = np.arange(16)[:, None]
    qf = np.arange(512)[None, :]
    Gm = tblT[:, t5_bucket_np(m - (16 + half * NG + qf))].astype(np.float32)
    pp = np.arange(128)[:, None]
    Gx = np.zeros((8, 128, 2, 512), np.float32)
    Gx[:, :, 0, :] = tblT[:, t5_bucket_np((0 - half) * NG + 128 * (NGT - 1) + pp - qf)]
    Gx[:, :, 1, :] = tblT[:, t5_bucket_np((1 - half) * NG + pp - 512 * (NQ - 1) - qf)]
    cb = np.zeros((8, NQ + 1, NKC), np.float32)
    for qc in range(1, NQ + 1):
        qpos = 16 + half * NG + (qc - 1) * 512
        for kc in range(NKC):
            kpos = 0 if kc == 0 else 16 + (kc - 1) * 128
            cb[:, qc, kc] = tblT[:, t5_bucket_np(kpos - qpos)]
    cb = np.ascontiguousarray(np.broadcast_to(cb.reshape(8, 1, -1), (8, 128, (NQ + 1) * NKC)))
    kp = np.zeros((128, NKC), np.int64)
    kp[:, 0] = np.arange(128)
    for kc in range(1, NKC):
        kp[:, kc] = 16 + (kc - 1) * 128 + np.arange(128)
    q = np.arange(16)[None, None, :]
    Bm = tblT[:, t5_bucket_np(kp[:, :, None] - q)].astype(np.float32)
    return G, np.ascontiguousarray(Gm), cb, np.ascontiguousarray(Bm.reshape(8, 128, NKC * 16)), Gx.reshape(8, 128, 1024)


class Builder:
    def __init__(self, cfg, mode, seg):
        self.cfg = cfg
        self.mode = mode
        self.seg = seg
        self.nc = bass.Bass("TRN2", target_bir_lowering=False)
        self.dr = {}
        self.in_names = []
        self.out_names = []

    def dram(self, name, shape, dtype, kind):
        t = self.nc.dram_tensor(name, list(shape), dtype, kind=kind)
        self.dr[name] = t.ap()
        if kind == "ExternalInput":
            self.in_names.append(name)
        elif kind == "ExternalOutput":
            self.out_names.append(name)
        return self.dr[name]

    def build(self):
        cfg = self.cfg
        nc = self.nc
        DEPTH, DFF = cfg.DEPTH, cfg.DFF
        seg, fused = self.seg, self.mode == "fused"
        with ExitStack() as st:
            P = Prog(nc, st)
            self.P = P
            at = st.enter_context(nc.sbuf_tensor("arena", [128, 206 * 256], F32))
            self.A = Arena(at, 206 * 1024)
            self.ps = st.enter_context(nc.psum_tensor("ps", [128, 8, 512], F32))
            self.pb = [Buf(f"bank{i}", excl=True) for i in range(8)]
            EI = "ExternalInput"
            self.dram("norm_g", [DEPTH, 6, D], F32, EI)
            self.dram("ffn_w_gate", [DEPTH, 2, D, DFF], F32, EI)
            self.dram("ffn_w_up", [DEPTH, 2, D, DFF], F32, EI)
            self.dram("ffn_w_down", [DEPTH, 2, DFF, D], F32, EI)
            self.dram("na_w_qkv", [cfg.NLA, D, 3 * D], F32, EI)
            self.dram("na_b_qkv", [cfg.NLA, 3 * D], F32, EI)
            self.dram("na_w_o", [cfg.NLA, D, D], F32, EI)
            self.dram("na_b_o", [cfg.NLA, D], F32, EI)
            self.dram("na_meta_bias", [cfg.NLA, 16, 16], F32, EI)
            self.dram("na_mask", [cfg.NLA, cfg.NVAR, 128, 16 * 6 * 128], F32, EI)
            if cfg.NLB:
                self.dram("da_w_qkv", [cfg.NLB, D, 3 * D], F32, EI)
                self.dram("da_w_o", [cfg.NLB, D, D], F32, EI)
                self.dram("da_lambda", [cfg.NLB, 4 * 64], F32, EI)
                self.dram("da_subln_g", [cfg.NLB, 128], F32, EI)
                self.dram("da_G", [8, 128, 2 * 1152], F32, EI)
                self.dram("da_Gm", [8, 16, 512], F32, EI)
                self.dram("da_Gx", [8, 128, 1024], F32, EI)
                self.dram("da_cb", [8, 128, (cfg.NQ + 1) * cfg.NKC], F32, EI)
                self.dram("da_Bm", [8, 128, cfg.NKC * 16], F32, EI)
            TCP, NCH = cfg.TCP, cfg.NCH
            self.dram("h_in", [TCP, D], F32, EI)
            self.b_hin = [Buf("hin") for _ in range(NCH)]
            if fused:
                self.dram("h", [TCP, D], F32, "Internal")
                self.dram("out", [cfg.NG, D], F32, "ExternalOutput")
                for nm in ("QT", "KT", "AOT"):
                    self.dram(nm, [8 * 128 * TCP], BF16, "Internal")
                self.dram("V", [8 * 128 * TCP], BF16, "Internal")
                CH = 131072
                for nm, shp in (("KT_all", [8, 2, 128 * TCP]), ("V_all", [8, 2, 128 * TCP]),
                                ("KTlo_all", [2, 2 * CH]), ("KThi_all", [2, 2 * CH]),
                                ("Vlo_all", [2, 2 * CH]), ("Vhi_all", [2, 2 * CH])):
                    t = self.nc.dram_tensor(nm, shp, BF16, kind="Internal", addr_space="Local")
                    self.dr[nm] = t.ap()
            else:
                if seg > 0:
                    self.dram("QT", [8 * 128 * TCP], BF16, EI)
                    self.dram("KT", [8 * 128 * TCP], BF16, EI)
                    self.dram("V", [8 * 128 * TCP], BF16, EI)
                    if (seg - 1) % 2 == 1:
                        self.dram("KT_all", [8, 2, 128 * TCP], BF16, EI)
                        self.dram("V_all", [8, 2, 128 * TCP], BF16, EI)
                    else:
                        for nm in ("KTlo_all", "KThi_all", "Vlo_all", "Vhi_all"):
                            self.dram(nm, [2, 2 * 131072], BF16, EI)
                    import os
                    self.attn_only = "attnonly" in os.environ.get("KDBG1", "")
                    self.dram("AOT", [8 * 128 * TCP], BF16, "ExternalOutput" if self.attn_only else "Internal")
                    self.dram("h", [TCP, D], F32, "Internal")
                if seg > 0 and self.attn_only:
                    pass
                elif seg < DEPTH:
                    self.dram("h_out", [TCP, D], F32, "ExternalOutput")
                    self.dram("QT_o", [8 * 128 * TCP], BF16, "ExternalOutput")
                    self.dram("KT_o", [8 * 128 * TCP], BF16, "ExternalOutput")
                    self.dram("V_o", [8 * 128 * TCP], BF16, "ExternalOutput")
                else:
                    self.dram("out", [cfg.NG, D], F32, "ExternalOutput")
            self.b_h = [Buf("h") for _ in range(NCH)]
            self.b_hout = [Buf("hout") for _ in range(NCH)]
            self.b_out = [Buf("out") for _ in range(NCH)]
            self.b_QT, self.b_KT, self.b_V, self.b_AOT = Buf("QT"), Buf("KT"), Buf("V"), Buf("AOT")
            self.b_KTall, self.b_Vall = Buf("KTall"), Buf("Vall")

            self.consts()
            dr = self.dr
            if fused:
                hcur = ("h_in", self.b_hin)
                for i in range(DEPTH):
                    self.stage_ffn(i, 0, hcur, ("h", self.b_h))
                    hcur = ("h", self.b_h)
                    self.stage_qkv(i, hcur, "QT", "KT", "V")
                    self.stage_gather(i)
                    self.stage_attn(i)
                    self.stage_oproj(i, hcur, hcur)
                    last = (i == DEPTH - 1)
                    self.stage_ffn(i, 1, hcur, ("out", self.b_out) if last else hcur)
                P.barrier(self.b_out)
            else:
                hcur = ("h_in", self.b_hin)
                if seg > 0:
                    i = seg - 1
                    self.stage_attn(i)
                    if self.attn_only:
                        P.barrier([self.b_AOT])
                        P.emit()
                        self.ninst = P.ninst
                        return nc
                    self.stage_oproj(i, hcur, ("h", self.b_h))
                    hcur = ("h", self.b_h)
                    if seg == DEPTH:
                        self.stage_ffn(i, 1, hcur, ("out", self.b_out))
                    else:
                        self.stage_ffn(i, 1, hcur, hcur)
                if seg < DEPTH:
                    import os
                    dbg = os.environ.get("KDBG", "ffn,qkv")
                    if "ffn" in dbg:
                        self.stage_ffn(seg, 0, hcur, ("h_out", self.b_hout))
                    if "qkv" in dbg:
                        self.stage_qkv(seg, ("h_out", self.b_hout) if "ffn" in dbg else hcur, "QT_o", "KT_o", "V_o")
                P.barrier(self.b_out + self.b_hout + [self.b_QT, self.b_KT, self.b_V])
            P.emit()
            self.ninst = P.ninst
        return nc

    def consts(self):
        P, A = self.P, self.A
        self.b_c = P.buf("consts")
        self.identf = A.alloc([128], F32)
        self.ident = A.alloc([128], BF16)
        self.ones_f = A.alloc([128], F32)
        self.ones_b = A.alloc([128], BF16)
        self.eps = A.alloc([1], F32)
        b = self.b_c
        P.memset("pool", self.identf, 0.0, [b])
        P.add("pool", lambda e: e.affine_select(out=self.identf, in_=self.identf, pattern=[[-1, 128]],
                                                compare_op=ALU.not_equal, fill=1.0, base=0,
                                                channel_multiplier=1), [b], [b])
        P.cp("dve", self.ident, self.identf, [b], [b])
        P.memset("dve", self.ones_f, 1.0, [b])
        P.memset("dve", self.ones_b, 1.0, [b])
        P.memset("dve", self.eps, RMS_EPS, [b])
        self.sel = [A.alloc([128], F32) for _ in range(2)]
        for i in range(2):
            P.memset("dve", self.sel[i], 0.0, [b])
            P.memset("dve", self.sel[i][64 * i:64 * i + 64, :], 1.0 / 64, [b])
        A.mark()

    def load_w_cast(self, dst, src, rows_chunks, ncols, bufs):
        P = self.P
        step = ncols
        while step > 2048:
            step //= 2
        i = 0
        for k in range(rows_chunks):
            for c0 in range(0, ncols, step):
                P.dma("pool", dst[:, k, c0:c0 + step], src[k * 128:(k + 1) * 128, c0:c0 + step], (), [bufs[i]])
                i += 1
        return i

    def load_rep(self, dst, vec, b):
        self.P.dma("sp", dst, vec.partition_broadcast(128), (), [b])

    def rstd_from_ss(self, ss, rstd, np_, n, b_ss, b_rstd):
        P = self.P
        P.act(rstd[:np_], ss[:np_], AF.Sqrt, [b_ss, self.b_c], [b_rstd], bias=self.eps[:np_], scale=1.0 / n)
        P.recip(rstd[:np_], rstd[:np_], [b_rstd], [b_rstd])

    def norm_A(self, W, grp, hsrc, g_rep, b_g, gi):
        P = self.P
        hname, hbufs = hsrc
        hd = self.dr[hname]
        slot = gi % 2
        hb, b_hb = W["hb"][slot], W["b_hb"][slot]
        for j, (t, np_) in enumerate(grp):
            P.dma("sp", hb[:np_, j, :], hd[t * 128:t * 128 + np_, :], [hbufs[t]], [b_hb[j]])
            sl = (gi * len(grp) + j) % 2
            ss, rstd, xn = W["ss"][sl], W["rstd"][sl], W["xn"][sl]
            b_ss, b_rstd, b_xn = W["b_ss"][sl], W["b_rstd"][sl], W["b_xn"][sl]
            P.act(W["junk"][:np_], hb[:np_, j, :], AF.Square, [b_hb[j]], [W["b_junk"], b_ss], accum_out=ss[:np_])
            self.rstd_from_ss(ss, rstd, np_, D, b_ss, b_rstd)
            P.stt(xn[:np_], hb[:np_, j, :], rstd[:np_], g_rep[:np_], ALU.mult, ALU.mult, [b_hb[j], b_rstd, b_g], [b_xn])
            if not self._split:
                self.norm_B1(W, grp, gi, j)

    def norm_B1(self, W, grp, gi, j):
        P, ps, pb = self.P, self.ps, self.pb
        slot = gi % 2
        xnT, b_xnT = W["xnT"][slot], W["b_xnT"][slot]
        ptr = ps[:, 7, :].bitcast(BF16).rearrange("p (k n) -> p k n", n=128)
        t, np_ = grp[j]
        N0 = sum(x[1] for x in grp[:j])
        sl = (gi * len(grp) + j) % 2
        xn, b_xn = W["xn"][sl], W["b_xn"][sl]
        for k in range(8):
            P.tr(ptr[:, k, :np_], xn[:np_, k * 128:(k + 1) * 128], self.ident[:np_, :np_], [b_xn, self.b_c], [pb[7]])
        P.cp("dve" if j % 2 else "act", xnT[:, :, N0:N0 + np_], ptr[:, :, :np_], [pb[7]], [b_xnT])

    def norm_B(self, W, grp, gi):
        for j in range(len(grp)):
            self.norm_B1(W, grp, gi, j)

    _split = False

    def norm_T(self, W, grp, hsrc, g_rep, b_g, gi):
        self._split = False
        self.norm_A(W, grp, hsrc, g_rep, b_g, gi)
        return sum(x[1] for x in grp)

    def work_common(self, gs=4, lean=False):
        P, A = self.P, self.A
        W = {}
        hb0 = A.alloc([gs, D], F32)
        W["hb"] = [hb0, A.alloc([gs, D], F32)]
        bh0 = P.bufs(gs, "hb")
        W["b_hb"] = [bh0, P.bufs(gs, "hb")]
        W["xnT"] = [A.alloc([8, gs * 128], BF16) for _ in range(2)]
        W["b_xnT"] = P.bufs(2, "xnT")
        W["xn"] = [A.alloc([D], BF16) for _ in range(2)]
        W["b_xn"] = P.bufs(2, "xn")
        W["ss"] = [A.alloc([1], F32) for _ in range(2)]
        W["b_ss"] = P.bufs(2, "ss")
        W["rstd"] = [A.alloc([1], F32) for _ in range(2)]
        W["b_rstd"] = P.bufs(2, "rstd")
        W["junk"] = A.alloc([D], BF16)
        W["b_junk"] = P.buf("junk")
        tmp0 = A.alloc([D], F32)
        W["tmp"] = [tmp0, tmp0 if lean else A.alloc([D], F32)]
        bt0 = P.buf("tmp")
        W["b_tmp"] = [bt0, bt0 if lean else P.buf("tmp")]
        W["ho"] = [A.alloc([D], F32) for _ in range(2)]
        W["b_ho"] = P.bufs(2, "ho")
        W["ss2"] = [A.alloc([1], F32) for _ in range(2)]
        W["b_ss2"] = P.bufs(2, "ss2")
        W["rstd2"] = [A.alloc([1], F32) for _ in range(2)]
        W["b_rstd2"] = P.bufs(2, "rstd2")
        return W

    def dst_rows(self, hdst, t, np_):
        name, bufs = hdst
        d = self.dr[name]
        if name == "out":
            if t == 0:
                return None, None
            return d[(t - 1) * 128:(t - 1) * 128 + np_, :], bufs[t]
        return d[t * 128:t * 128 + np_, :], bufs[t]

    def resid_out(self, W, cnt, np_, src, src_bufs, hb_j, b_hb_j, g_rep, b_g, scale, hdst, t):
        P = self.P
        sl = cnt % 2
        ss2, rstd2, tmp, ho = W["ss2"][sl], W["rstd2"][sl], W["tmp"][sl], W["ho"][sl]
        b_ss2, b_rstd2, b_tmp, b_ho = W["b_ss2"][sl], W["b_rstd2"][sl], W["b_tmp"][sl], W["b_ho"][sl]
        P.act(W["junk"][:np_], src, AF.Square, src_bufs, [W["b_junk"], b_ss2], accum_out=ss2[:np_])
        self.rstd_from_ss(ss2, rstd2, np_, D, b_ss2, b_rstd2)
        P.stt(tmp[:np_], src, rstd2[:np_], g_rep[:np_], ALU.mult, ALU.mult, list(src_bufs) + [b_rstd2, b_g], [b_tmp])
        P.stt(ho[:np_], tmp[:np_], float(scale), hb_j, ALU.mult, ALU.add, [b_tmp, b_hb_j], [b_ho])
        dst, bd = self.dst_rows(hdst, t, np_)
        if dst is not None:
            P.dma("pool", dst, ho[:np_], [b_ho], [bd])

    def stage_ffn(self, li, w, hsrc, hdst):
        cfg, P, A, ps, pb, dr = self.cfg, self.P, self.A, self.ps, self.pb, self.dr
        A.reset()
        FCH, DFF = cfg.FCH, cfg.DFF
        Wg = A.alloc([8, DFF], BF16)
        Wu = A.alloc([8, DFF], BF16)
        Wd = A.alloc([FCH, D], BF16)
        b_wg, b_wu, b_wd = P.bufs(32, "wg"), P.bufs(32, "wu"), P.bufs(FCH, "wd")
        n = self.load_w_cast(Wg, dr["ffn_w_gate"][li, w], 8, DFF, b_wg)
        b_wg = b_wg[:n]
        n = self.load_w_cast(Wu, dr["ffn_w_up"][li, w], 8, DFF, b_wu)
        b_wu = b_wu[:n]
        self.load_w_cast(Wd, dr["ffn_w_down"][li, w], FCH, D, b_wd)
        gin, gout = A.alloc([D], F32), A.alloc([D], F32)
        b_gin, b_gout = P.buf("gin"), P.buf("gout")
        self.load_rep(gin, dr["norm_g"][li, 4 * w, :], b_gin)
        self.load_rep(gout, dr["norm_g"][li, 4 * w + 1, :], b_gout)
        gs = 2 if DFF > 2048 else 4
        W = self.work_common(gs, lean=(gs == 2))
        HT = A.alloc([FCH, gs * 128], BF16)
        b_HT = P.buf("HT")
        sg = [A.alloc([gs * 128], F32) for _ in range(2)]
        b_sg = P.bufs(2, "sg")
        cnt = 0
        import os
        kparts = os.environ.get("KPARTS", "norm,gu,down")
        kgroups = int(os.environ.get("KGROUPS", "1000"))
        groups = cfg.groups(gs)
        pipe = (gs == 2)
        if pipe:
            self._split = True
            self.norm_A(W, groups[0], hsrc, gin, b_gin, 0)
            self.norm_B(W, groups[0], 0)
        for gi, grp in enumerate(groups):
            if gi >= kgroups:
                break
            if pipe:
                N = sum(x[1] for x in grp)
            else:
                N = self.norm_T(W, grp, hsrc, gin, b_gin, gi)
            slot = gi % 2
            xnT, b_xnT = W["xnT"][slot], W["b_xnT"][slot]
            hb, b_hb = W["hb"][slot], W["b_hb"][slot]
            if "gu" not in kparts:
                continue
            for f in range(FCH):
                if pipe and gi + 1 < len(groups):
                    if f == 1:
                        self._split = True
                        self.norm_A(W, groups[gi + 1], hsrc, gin, b_gin, gi + 1)
                    if f == max(2, FCH - 3):
                        self.norm_B(W, groups[gi + 1], gi + 1)
                s2 = f % 2
                bg, bu = 2 * s2, 2 * s2 + 1
                for k in range(8):
                    P.mm(ps[:, bg, :N], Wg[:, k, f * 128:(f + 1) * 128], xnT[:, k, :N], k == 0, k == 7,
                         [b_xnT] + b_wg, [pb[bg]])
                for k in range(8):
                    P.mm(ps[:, bu, :N], Wu[:, k, f * 128:(f + 1) * 128], xnT[:, k, :N], k == 0, k == 7,
                         [b_xnT] + b_wu, [pb[bu]])
                P.act(sg[s2][:, :N], ps[:, bg, :N], AF.Silu, [pb[bg]], [b_sg[s2]])
                P.tt("dve", HT[:, f, :N], sg[s2][:, :N], ps[:, bu, :N], ALU.mult, [b_sg[s2], pb[bu]], [b_HT])
            if "down" not in kparts:
                continue
            for j, (t, np_) in enumerate(grp):
                pd0 = 4 if cnt % 2 == 0 else 2
                for half in range(2):
                    for f in range(FCH):
                        P.mm(ps[:np_, pd0 + half, :], HT[:, f, j * 128:j * 128 + np_], Wd[:, f, half * 512:(half + 1) * 512],
                             f == 0, f == FCH - 1, [b_HT, b_wd[f]], [pb[pd0 + half]])
                src = ps[:np_, pd0:pd0 + 2, :]
                self.resid_out(W, cnt, np_, src, [pb[pd0], pb[pd0 + 1]], hb[:np_, j, :], b_hb[j], gout, b_gout, 0.5, hdst, t)
                cnt += 1
        P.barrier()

    def stage_qkv(self, li, hsrc, nQ, nK, nV):
        cfg, P, A, ps, pb, dr = self.cfg, self.P, self.A, self.ps, self.pb, self.dr
        A.reset()
        is_na = (li % 2 == 0)
        jj = li // 2
        TCP, NCH = cfg.TCP, cfg.NCH
        Wq = A.alloc([8, 3 * D], BF16)
        b_w = P.bufs(16, "wqkv")
        self.load_w_cast(Wq, dr["na_w_qkv" if is_na else "da_w_qkv"][jj], 8, 3 * D, b_w)
        g2 = A.alloc([D], F32)
        b_g2 = P.buf("g2")
        self.load_rep(g2, dr["norm_g"][li, 2, :], b_g2)
        if is_na:
            bqk = A.alloc([16], F32)
            b_bqk = P.buf("bqk")
            for kq in range(16):
                P.dma("sp", bqk[:, kq:kq + 1], dr["na_b_qkv"][jj, kq * 128:(kq + 1) * 128].rearrange("(p o) -> p o", o=1),
                      (), [b_bqk])
            bv = A.alloc([D], F32)
            b_bv = P.buf("bv")
            self.load_rep(bv, dr["na_b_qkv"][jj, 2 * D:3 * D], b_bv)
        W = self.work_common()
        qk = [A.alloc([16, 512], BF16) for _ in range(2)]
        b_qk = P.bufs(2, "qk")
        vs = [A.alloc([D], BF16) for _ in range(2)]
        b_vs = P.bufs(2, "vs")
        if is_na:
            QTd = dr[nQ].rearrange("(c p k n) -> c p k n", p=128, k=8, n=128)
            KTd = dr[nK].rearrange("(c p k n) -> c p k n", p=128, k=8, n=128)
            Vd = dr[nV].rearrange("(c p f) -> c p f", p=128, f=D)
        else:
            QTd = dr[nQ].rearrange("(k p n) -> p k n", p=128, n=TCP)
            KTd = dr[nK].rearrange("(k p n) -> p k n", p=128, n=TCP)
            Vd = dr[nV].rearrange("(h p c d) -> p h c d", p=128, c=NCH, d=128)
        cnt = 0
        for gi, grp in enumerate(cfg.groups()):
            N = self.norm_T(W, grp, hsrc, g2, b_g2, gi)
            slot = gi % 2
            xnT, b_xnT = W["xnT"][slot], W["b_xnT"][slot]
            qs, b_qs = qk[slot], b_qk[slot]
            for kq in range(16):
                bk = kq % 4
                for k in range(8):
                    P.mm(ps[:, bk, :N], Wq[:, k, kq * 128:(kq + 1) * 128], xnT[:, k, :N], k == 0, k == 7,
                         [b_xnT] + b_w, [pb[bk]])
                if is_na:
                    P.act(qs[:, kq, :N], ps[:, bk, :N], AF.Identity, [pb[bk], b_bqk], [b_qs], bias=bqk[:, kq:kq + 1])
                else:
                    P.cp("act" if kq % 2 else "dve", qs[:, kq, :N], ps[:, bk, :N], [pb[bk]], [b_qs])
            t0 = grp[0][0]
            if is_na:
                for j, (t, np_) in enumerate(grp):
                    P.dma("pool", QTd[t][:, :, 0:np_], qs[:, 0:8, j * 128:j * 128 + np_], [b_qs], [self.b_QT])
                    P.dma("pool", KTd[t][:, :, 0:np_], qs[:, 8:16, j * 128:j * 128 + np_], [b_qs], [self.b_KT])
            else:
                P.dma("pool", QTd[:, :, t0 * 128:t0 * 128 + N], qs[:, 0:8, :N], [b_qs], [self.b_QT])
                P.dma("pool", KTd[:, :, t0 * 128:t0 * 128 + N], qs[:, 8:16, :N], [b_qs], [self.b_KT])
            for j, (t, np_) in enumerate(grp):
                sl = cnt % 2
                for half in range(2):
                    for k in range(8):
                        P.mm(ps[:np_, 4 + half, :], xnT[:, k, j * 128:j * 128 + np_],
                             Wq[:, k, 2 * D + half * 512:2 * D + (half + 1) * 512], k == 0, k == 7,
                             [b_xnT] + b_w, [pb[4 + half]])
                src = ps[:np_, 4:6, :]
                if is_na:
                    P.tt("dve", vs[sl][:np_], src, bv[:np_], ALU.add, [pb[4], pb[5], b_bv], [b_vs[sl]])
                else:
                    P.cp("dve", vs[sl][:np_], src, [pb[4], pb[5]], [b_vs[sl]])
                if is_na:
                    P.dma("pool", Vd[t][0:np_, :], vs[sl][:np_], [b_vs[sl]], [self.b_V])
                else:
                    P.dma("pool", Vd[0:np_, :, t, :], vs[sl][:np_].rearrange("p (h d) -> p h d", h=8), [b_vs[sl]], [self.b_V])
                cnt += 1
        P.barrier()

    def stage_gather(self, li):
        P, dr, cfg = self.P, self.dr, self.cfg
        groups = [[2 * i, 2 * i + 1] for i in range(cfg.B)]
        TCP, NGT = cfg.TCP, cfg.NGT
        CH = 131072

        def cc(src2d, dst2d, rb, wb):
            P.add("pool", lambda e: e.collective_compute("AllGather", ALU.bypass, replica_groups=groups,
                                                         ins=[src2d], outs=[dst2d]), [rb], [wb], dma=True, inc=1)

        if li % 2 == 0:
            for nm, bsrc, bdst in (("KT", self.b_KT, self.b_KTall), ("V", self.b_V, self.b_Vall)):
                lo = dr[nm][CH:3 * CH].rearrange("(a n) -> a n", n=1024)
                hi = dr[nm][(NGT - 1) * CH:(NGT + 1) * CH].rearrange("(a n) -> a n", n=1024)
                cc(lo, dr[nm + "lo_all"].rearrange("r (a n) -> (r a) n", n=1024), bsrc, Buf("x"))
                cc(hi, dr[nm + "hi_all"].rearrange("r (a n) -> (r a) n", n=1024), bsrc, Buf("x"))
        else:
            HS = 128 * TCP
            self.b_gh = [[Buf("gk"), Buf("gv")] for _ in range(8)]
            for h in range(8):
                for i, (nm, bsrc) in enumerate((("KT", self.b_KT), ("V", self.b_V))):
                    cc(dr[nm][h * HS:(h + 1) * HS].rearrange("(a n) -> a n", n=TCP),
                       dr[nm + "_all"][h].rearrange("r (a n) -> (r a) n", n=TCP), bsrc, self.b_gh[h][i])
            return
        P.barrier([self.b_KT, self.b_V])

    def stage_attn(self, li):
        if li % 2 == 0:
            self.stage_na(li)
        else:
            self.stage_da(li)

    def stage_oproj(self, li, hsrc, hdst):
        cfg, P, A, ps, pb, dr = self.cfg, self.P, self.A, self.ps, self.pb, self.dr
        A.reset()
        is_na = (li % 2 == 0)
        jj = li // 2
        TCP = cfg.TCP
        Wo = A.alloc([8, D], BF16)
        b_w = P.bufs(8, "wo")
        self.load_w_cast(Wo, dr["na_w_o" if is_na else "da_w_o"][jj], 8, D, b_w)
        g3 = A.alloc([D], F32)
        b_g3 = P.buf("g3")
        self.load_rep(g3, dr["norm_g"][li, 3, :], b_g3)
        if is_na:
            bo = A.alloc([D], F32)
            b_bo = P.buf("bo")
            self.load_rep(bo, dr["na_b_o"][jj, :], b_bo)
        W = self.work_common()
        aoT = [A.alloc([8, 512], BF16) for _ in range(2)]
        b_ao = P.bufs(2, "aoT")
        msb = [A.alloc([D], F32) for _ in range(2)]
        b_msb = P.bufs(2, "msb")
        AOd = dr["AOT"].rearrange("(k p n) -> p k n", p=128, n=TCP)
        hname, hbufs = hsrc
        hd = dr[hname]
        cnt = 0
        for gi, grp in enumerate(cfg.groups()):
            slot = gi % 2
            hb, b_hb = W["hb"][slot], W["b_hb"][slot]
            N = sum(np_ for _, np_ in grp)
            t0 = grp[0][0]
            P.dma("sp", aoT[slot][:, :, :N], AOd[:, :, t0 * 128:t0 * 128 + N], [self.b_AOT], [b_ao[slot]])
            for j, (t, np_) in enumerate(grp):
                P.dma("sp", hb[:np_, j, :], hd[t * 128:t * 128 + np_, :], [hbufs[t]], [b_hb[j]])
            for j, (t, np_) in enumerate(grp):
                pd0 = 4 if cnt % 2 == 0 else 2
                for half in range(2):
                    for k in range(8):
                        P.mm(ps[:np_, pd0 + half, :], aoT[slot][:, k, j * 128:j * 128 + np_], Wo[:, k, half * 512:(half + 1) * 512],
                             k == 0, k == 7, [b_ao[slot]] + b_w, [pb[pd0 + half]])
                src = ps[:np_, pd0:pd0 + 2, :]
                srcb = [pb[pd0], pb[pd0 + 1]]
                if is_na:
                    sl = cnt % 2
                    P.tt("dve", msb[sl][:np_], src, bo[:np_], ALU.add, srcb + [b_bo], [b_msb[sl]])
                    src, srcb = msb[sl][:np_], [b_msb[sl]]
                self.resid_out(W, cnt, np_, src, srcb, hb[:np_, j, :], b_hb[j], g3, b_g3, 1.0, hdst, t)
                cnt += 1
        P.barrier()

    def stage_na(self, li):
        cfg, P, A, ps, pb, dr = self.cfg, self.P, self.A, self.ps, self.pb, self.dr
        A.reset()
        jj = li // 2
        TCP, NCH, NGT = cfg.TCP, cfg.NCH, cfg.NGT
        QTd = dr["QT"].rearrange("(c p k n) -> c p k n", p=128, k=8, n=128)
        KTo = dr["KT"].rearrange("(c p k n) -> c p k n", p=128, k=8, n=128)
        Vo = dr["V"].rearrange("(c p f) -> c p f", p=128, f=D)
        KTlo = dr["KTlo_all"].rearrange("r (c p k n) -> r c p k n", p=128, k=8, n=128)
        KThi = dr["KThi_all"].rearrange("r (c p k n) -> r c p k n", p=128, k=8, n=128)
        Vlo = dr["Vlo_all"].rearrange("r (c p f) -> r c p f", p=128, f=D)
        Vhi = dr["Vhi_all"].rearrange("r (c p f) -> r c p f", p=128, f=D)
        AOd = dr["AOT"].rearrange("(k p n) -> p k n", p=128, n=TCP)
        maskd = dr["na_mask"][jj]
        metab = A.alloc([16], F32)
        b_mb = P.buf("metab")
        P.dma("sp", metab[0:16, :], dr["na_meta_bias"][jj].rearrange("h m -> m h"), (), [b_mb], allow_slow_non_contiguous=True)
        KTm = A.alloc([8, 16], BF16)
        Vm = A.alloc([D], BF16)
        b_km = P.buf("kvm")
        P.dma("sp", KTm, KTo[0][:, :, 0:16], [self.b_KT], [b_km])
        P.dma("sp", Vm[0:16, :], Vo[0][0:16, :], [self.b_V], [b_km])
        mk = [A.alloc([16 * 6 * 128], F32) for _ in range(2)]
        b_mk = P.bufs(2, "mk")
        QTb = [A.alloc([8, 128], BF16) for _ in range(2)]
        b_q = P.bufs(2, "QTb")
        KTw = [A.alloc([6, 8 * 128], BF16) for _ in range(2)]
        Vw = [A.alloc([6, D], BF16) for _ in range(2)]
        b_kw = [P.bufs(6, "KTw") for _ in range(2)]
        b_vw = [P.bufs(6, "Vw") for _ in range(2)]
        sm = [A.alloc([6 * 128], F32) for _ in range(2)]
        b_sm = P.bufs(2, "sm")
        pt = [A.alloc([6 * 128], BF16) for _ in range(3)]
        b_pt = P.bufs(3, "pt")
        ptm = [A.alloc([128], BF16) for _ in range(3)]
        b_ptm = P.bufs(3, "ptm")
        rd = [A.alloc([128], F32) for _ in range(2)]
        b_rd = P.bufs(2, "rd")
        ao = [A.alloc([8, 128], BF16) for _ in range(2)]
        b_ao = P.bufs(2, "ao")

        blocks = [-1] + list(range(NGT))

        def issue_loads(bi):
            b = blocks[bi]
            sl = bi % 2
            chunk = 0 if b < 0 else b + 1
            nq = 16 if b < 0 else 128
            P.dma("sp", QTb[sl][:, :, :nq], QTd[chunk][:, :, 0:nq], [self.b_QT], [b_q[sl]])
            if b < 0:
                return
            offs = [-2, -1, 0, 1, 2] + ([3] if b == 0 else ([-3] if b == NGT - 1 else []))
            for s, of in enumerate(offs):
                l = b + of
                if l < 0:
                    ksrc, vsrc, rb = KThi[0][l + 2], Vhi[0][l + 2], [self.b_KTall, self.b_Vall]
                elif l >= NGT:
                    ksrc, vsrc, rb = KTlo[1][l - NGT], Vlo[1][l - NGT], [self.b_KTall, self.b_Vall]
                else:
                    ksrc, vsrc, rb = KTo[l + 1], Vo[l + 1], [self.b_KT, self.b_V]
                P.dma("sp", KTw[sl][:, s, :], ksrc.rearrange("p k n -> p (k n)"), rb, [b_kw[sl][s]])
                P.dma("pool", Vw[sl][:, s, :], vsrc, rb, [b_vw[sl][s]])

        cur_var = {"v": None, "slot": 0}

        def ensure_mask(b):
            v = cfg.var_of_block(b)
            if cur_var["v"] != v:
                cur_var["slot"] ^= 1
                cur_var["v"] = v
                P.dma("sp", mk[cur_var["slot"]], maskd[v], (), [b_mk[cur_var["slot"]]])
            return cur_var["slot"]

        issue_loads(0)
        hcount = 0
        for bi, b in enumerate(blocks):
            if bi + 1 < len(blocks):
                issue_loads(bi + 1)
            sl = bi % 2
            nq = 16 if b < 0 else 128
            nsl = 0 if b < 0 else (6 if b in (0, NGT - 1) else 5)
            if b >= 0:
                ms = ensure_mask(b)
                mkv = mk[ms].rearrange("p (h s q) -> p h s q", h=16, s=6)
            heads = []
            for hi in range(16):
                st_ = hcount % 2
                heads.append((hi, st_, hcount % 3))
                hcount += 1

            def emit_qk(hi, st_):
                k, hh = hi // 2, hi % 2
                r0 = 64 * hh
                bx, by = 2 * st_, 2 * st_ + 1
                for s in range(nsl):
                    outp = ps[:, bx, s * 128:s * 128 + nq] if s < 4 else ps[:, by, (s - 4) * 128:(s - 4) * 128 + nq]
                    P.mm(outp, KTw[sl][r0:r0 + 64, s, k * 128:(k + 1) * 128], QTb[sl][r0:r0 + 64, k, :nq], True, True,
                         [b_kw[sl][s], b_q[sl]], [pb[bx] if s < 4 else pb[by]])
                P.mm(ps[0:16, by, 256:256 + nq], KTm[r0:r0 + 64, k, :], QTb[sl][r0:r0 + 64, k, :nq], True, True,
                     [b_km, b_q[sl]], [pb[by]])

            emit_qk(heads[0][0], heads[0][1])
            for (hi, st_, pi) in heads:
                if hi + 1 < 16:
                    emit_qk(heads[hi + 1][0], heads[hi + 1][1])
                k, hh = hi // 2, hi % 2
                h = hi
                r0 = 64 * hh
                bx, by = 2 * st_, 2 * st_ + 1
                bo = 4 + k % 2
                if nsl:
                    smv = sm[st_]
                    P.stt(smv[:, 0:512], ps[:, bx, :], 0.125, mkv[:, h, 0:4, :].rearrange("p s q -> p (s q)"),
                          ALU.mult, ALU.add, [pb[bx], b_mk[ms]], [b_sm[st_]])
                    P.stt(smv[:, 512:nsl * 128], ps[:, by, 0:(nsl - 4) * 128], 0.125,
                          mkv[:, h, 4:nsl, :].rearrange("p s q -> p (s q)"), ALU.mult, ALU.add,
                          [pb[by], b_mk[ms]], [b_sm[st_]])
                    P.act(pt[pi][:, 0:nsl * 128], smv[:, 0:nsl * 128], AF.Exp, [b_sm[st_]], [b_pt[pi]])
                P.act(ptm[pi][0:16, :nq], ps[0:16, by, 256:256 + nq], AF.Exp, [pb[by], b_mb], [b_ptm[pi]],
                      bias=metab[0:16, h:h + 1], scale=0.125)
                for which in range(2):
                    col = 128 * which
                    for s in range(nsl):
                        lhs = Vw[sl][:, s, k * 128 + r0:k * 128 + r0 + 64] if which == 0 else self.ones_b[:, 0:64]
                        rds = [b_vw[sl][s], b_pt[pi]] if which == 0 else [self.b_c, b_pt[pi]]
                        P.mm(ps[r0:r0 + 64, bo, col:col + nq], lhs, pt[pi][:, s * 128:s * 128 + nq], s == 0, False, rds, [pb[bo]])
                    lhs = Vm[0:16, k * 128 + r0:k * 128 + r0 + 64] if which == 0 else self.ones_b[0:16, 0:64]
                    P.mm(ps[r0:r0 + 64, bo, col:col + nq], lhs, ptm[pi][0:16, :nq], nsl == 0, True,
                         [b_km, self.b_c, b_ptm[pi]], [pb[bo]])
                if hh == 1:
                    rs = k % 2
                    P.recip(rd[rs][:, :nq], ps[:, bo, 128:128 + nq], [pb[bo]], [b_rd[rs]])
                    P.tt("dve", ao[sl][:, k, :nq], ps[:, bo, 0:nq], rd[rs][:, :nq], ALU.mult, [pb[bo], b_rd[rs]], [b_ao[sl]])
            col0 = 0 if b < 0 else (b + 1) * 128
            P.dma("sp", AOd[:, :, col0:col0 + nq], ao[sl][:, :, :nq], [b_ao[sl]], [self.b_AOT])
        P.barrier()

    def stage_da(self, li):
        cfg, P, A, ps, pb, dr = self.cfg, self.P, self.A, self.ps, self.pb, self.dr
        A.reset()
        jj = li // 2
        TCP, NCH, NGT, NKC, NQ = cfg.TCP, cfg.NCH, cfg.NGT, cfg.NKC, cfg.NQ
        lam_init = 0.8 - 0.6 * math.exp(-0.3 * li)
        QTd = dr["QT"].rearrange("(k p n) -> k p n", p=128, n=TCP)
        KTa = dr["KT_all"].rearrange("k r (p n) -> k p r n", p=128)
        Va = dr["V_all"].rearrange("h r (p c d) -> h p r c d", p=128, c=NCH)
        AOd = dr["AOT"].rearrange("(k p n) -> k p n", p=128, n=TCP)
        lam = A.alloc([4, 64], F32)
        prod = A.alloc([2, 64], F32)
        sc = A.alloc([8], F32)
        b_l = P.buf("lam")
        P.dma("sp", lam, dr["da_lambda"][jj].rearrange("(a d) -> a d", a=4).partition_broadcast(128), (), [b_l])
        P.tt("dve", prod[:, 0, :], lam[:, 0, :], lam[:, 1, :], ALU.mult, [b_l], [b_l])
        P.tt("dve", prod[:, 1, :], lam[:, 2, :], lam[:, 3, :], ALU.mult, [b_l], [b_l])
        P.add("dve", lambda e: e.tensor_reduce(out=sc[:, 0:2], in_=prod, axis=mybir.AxisListType.X, op=ALU.add), [b_l], [b_l])
        P.act(sc[:, 2:4], sc[:, 0:2], AF.Exp, [b_l], [b_l])
        P.tt("dve", sc[:, 4:5], sc[:, 3:4], sc[:, 2:3], ALU.subtract, [b_l], [b_l])
        P.add("dve", lambda e: e.tensor_scalar(out=sc[:, 5:6], in0=sc[:, 4:5], scalar1=-lam_init, scalar2=None, op0=ALU.add), [b_l], [b_l])
        neglam = sc[:, 5:6]
        P.dma("sp", sc[:, 6:7], dr["da_subln_g"][jj].rearrange("(p o) -> p o", o=1), (), [b_l])
        P.add("dve", lambda e: e.tensor_scalar(out=sc[:, 7:8], in0=sc[:, 6:7], scalar1=1.0 - lam_init, scalar2=None, op0=ALU.mult), [b_l], [b_l])
        gsc = sc[:, 7:8]
        KTh = [A.alloc([2, TCP], BF16) for _ in range(2)]
        Vh = [A.alloc([2 * NCH, 128], BF16) for _ in range(2)]
        QTh = [A.alloc([TCP], BF16) for _ in range(2)]
        Gh = [A.alloc([2, 1152], F32) for _ in range(2)]
        Gmh = [A.alloc([512], F32) for _ in range(2)]
        Gxh = [A.alloc([2, 512], F32) for _ in range(2)]
        cbh = [A.alloc([(NQ + 1) * NKC], F32) for _ in range(2)]
        Bmh = [A.alloc([NKC, 16], F32) for _ in range(2)]
        b_hd = [P.bufs(8, "hd") for _ in range(2)]
        pt = [A.alloc([2, 512], BF16) for _ in range(3)]
        b_pt = P.bufs(3, "pt")
        tmpb = [A.alloc([2, 512], F32) for _ in range(2)]
        b_tmpb = P.bufs(2, "tmpb")
        o_raw = [A.alloc([2, 512], F32) for _ in range(2)]
        b_or = [P.bufs(2, "oraw") for _ in range(2)]
        r_raw = [A.alloc([512], F32) for _ in range(2)]
        b_rr = P.bufs(2, "rraw")
        pending = []

        def tick():
            for a in pending:
                a[0] -= 1
            while pending and pending[0][0] <= 0:
                pending.pop(0)[1]()

        def flush():
            while pending:
                pending.pop(0)[1]()
        o0 = A.alloc([512], F32)
        o1 = A.alloc([512], F32)
        sq = A.alloc([512], F32)
        rt = A.alloc([512], F32)
        b_o = P.bufs(4, "o")
        aob = [A.alloc([512], BF16) for _ in range(2)]
        b_aob = P.bufs(2, "aob")

        def load_head(h):
            sl = h % 2
            bh = b_hd[sl]
            gk, gv = (self.b_gh[h] if getattr(self, "b_gh", None) else (self.b_KTall, self.b_Vall))
            P.dma("sp", KTh[sl], KTa[h], [gk], [bh[0]])
            P.dma("pool", Vh[sl].rearrange("p (r c) d -> p r c d", r=2), Va[h], [gv], [bh[1]])
            P.dma("sp", QTh[sl], QTd[h], [self.b_QT], [bh[2]])
            P.dma("sp", Gh[sl].rearrange("p r x -> p (r x)"), dr["da_G"][h], (), [bh[3]])
            P.dma("sp", Gmh[sl][0:16, :], dr["da_Gm"][h], (), [bh[4]])
            P.dma("sp", Gxh[sl].rearrange("p r x -> p (r x)"), dr["da_Gx"][h], (), [bh[7]])
            P.dma("sp", cbh[sl], dr["da_cb"][h], (), [bh[5]])
            P.dma("sp", Bmh[sl].rearrange("p c q -> p (c q)"), dr["da_Bm"][h], (), [bh[6]])

        load_head(0)
        it = 0
        oc = 0
        for h in range(8):
            if h + 1 < 8:
                load_head(h + 1)
            sl = h % 2
            bh = b_hd[sl]
            for qc in range(NQ + 1):
                N = 16 if qc == 0 else 512
                q0 = 0 if qc == 0 else 128 + (qc - 1) * 512

                def kinfo(kc):
                    if kc == 0:
                        return 16, 0, 0
                    r, jl = (kc - 1) // NGT, (kc - 1) % NGT
                    return 128, r, jl + 1

                def qk(kc, itn):
                    nk, r, ch = kinfo(kc)
                    st_ = itn % 2
                    for s in range(2):
                        bk = 2 * st_ + s
                        P.mm(ps[:nk, bk, :N], KTh[sl][64 * s:64 * s + 64, r, ch * 128:ch * 128 + nk],
                             QTh[sl][64 * s:64 * s + 64, q0:q0 + N], True, True, [bh[0], bh[2]], [pb[bk]])

                qk(0, it)
                for kc in range(NKC):
                    if kc + 1 < NKC:
                        qk(kc + 1, it + 1)
                    nk, r, ch = kinfo(kc)
                    st_ = it % 2
                    pi = it % 3
                    it += 1
                    table = None
                    if qc == 0:
                        table, tb = Bmh[sl][:nk, kc, :], bh[6]
                    elif kc == 0:
                        if qc == 1:
                            table, tb = Gmh[sl][0:16, :], bh[4]
                    else:
                        d = (ch - 1) - 4 * (qc - 1)
                        if -1 <= d <= 4:
                            table, tb = Gh[sl][:, r, 512 - 128 * d:1024 - 128 * d], bh[3]
                        elif qc == 1 and r == 0 and ch == NGT:
                            table, tb = Gxh[sl][:, 0, :], bh[7]
                        elif qc == NQ and r == 1 and ch == 1:
                            table, tb = Gxh[sl][:, 1, :], bh[7]
                    b0, b1 = 2 * st_, 2 * st_ + 1
                    tsl = it % 2
                    if table is not None:
                        for s in range(2):
                            P.stt(tmpb[tsl][:nk, s, :N], ps[:nk, 2 * st_ + s, :N], 0.125, table, ALU.mult, ALU.add,
                                  [pb[2 * st_ + s], tb], [b_tmpb[tsl]])
                        P.act(pt[pi][:nk, :, :N], tmpb[tsl][:nk, :, :N], AF.Exp, [b_tmpb[tsl]], [b_pt[pi]])
                    else:
                        ci = qc * NKC + kc
                        P.act(pt[pi][:nk, :, :N], ps[:nk, b0:b1 + 1, :N], AF.Exp, [pb[b0], pb[b1], bh[5]], [b_pt[pi]],
                              bias=cbh[sl][:nk, ci:ci + 1], scale=0.125)
                    for s in range(2):
                        P.mm(ps[:, 4 + s, :N], Vh[sl][:nk, r * NCH + ch, :], pt[pi][:nk, s, :N], kc == 0, kc == NKC - 1,
                             [bh[1], b_pt[pi]], [pb[4 + s]])
                    for s in range(2):
                        P.mm(ps[64 * s:64 * s + 64, 6, :N], self.ones_b[:nk, 0:64], pt[pi][:nk, s, :N], kc == 0, kc == NKC - 1,
                             [self.b_c, b_pt[pi]], [pb[6]])
                    tick()
                par = oc % 2
                oc += 1
                flush()
                P.cp("dve", r_raw[par][:, :N], ps[:, 6, :N], [pb[6]], [b_rr[par]])
                P.cp("dve", o_raw[par][:, 0, :N], ps[:, 4, :N], [pb[4]], [b_or[par][0]])
                P.cp("act", o_raw[par][:, 1, :N], ps[:, 5, :N], [pb[5]], [b_or[par][1]])
                P.recip(r_raw[par][:, :N], r_raw[par][:, :N], [b_rr[par]], [b_rr[par]])

                def stepA(par=par, N=N):
                    P.mm(ps[:, 7, :N], self.sel[0], r_raw[par][:, :N], True, True, [self.b_c, b_rr[par]], [pb[7]])
                    P.tt("dve", o0[:, :N], o_raw[par][:, 0, :N], ps[:, 7, :N], ALU.mult, [b_or[par][0], pb[7]], [b_o[0]])

                def stepB(par=par, N=N):
                    P.mm(ps[:, 7, :N], self.sel[1], r_raw[par][:, :N], True, True, [self.b_c, b_rr[par]], [pb[7]])
                    P.stt(o1[:, :N], o_raw[par][:, 1, :N], neglam, ps[:, 7, :N], ALU.mult, ALU.mult,
                          [b_or[par][1], pb[7], b_l], [b_o[1]])
                    P.tt("dve", o0[:, :N], o0[:, :N], o1[:, :N], ALU.add, [b_o[0], b_o[1]], [b_o[0]])
                    P.tt("dve", sq[:, :N], o0[:, :N], o0[:, :N], ALU.mult, [b_o[0]], [b_o[2]])

                def stepC(par=par, N=N):
                    P.mm(ps[:, 7, :N], self.ones_f, sq[:, :N], True, True, [self.b_c, b_o[2]], [pb[7]])

                def stepD(par=par, N=N, h=h, q0=q0):
                    P.act(rt[:, :N], ps[:, 7, :N], AF.Sqrt, [pb[7], self.b_c], [b_o[3]], bias=self.eps, scale=1.0 / 128)
                    P.recip(rt[:, :N], rt[:, :N], [b_o[3]], [b_o[3]])
                    P.stt(aob[par][:, :N], o0[:, :N], gsc, rt[:, :N], ALU.mult, ALU.mult, [b_o[0], b_o[3], b_l], [b_aob[par]])
                    P.dma("sp", AOd[h][:, q0:q0 + N], aob[par][:, :N], [b_aob[par]], [self.b_AOT])

                pending.extend([[3, stepA], [6, stepB], [11, stepC], [14, stepD]])
        flush()
        extra = [b for pair in (getattr(self, "b_gh", None) or []) for b in pair]
        P.barrier(extra + [self.b_KT, self.b_V])
        self.b_gh = None


PARAM_NAMES = ["norm_g", "ffn_w_gate", "ffn_w_up", "ffn_w_down", "na_w_qkv", "na_b_qkv", "na_w_o", "na_b_o",
               "na_meta_bias", "da_w_qkv", "da_w_o", "da_lambda", "da_subln_g"]

MODE = "fused"
_last_ninst = None


def run_forward(cfg, inputs, mode=None):
    mode = mode or MODE
    B, SEQ, DEPTH = cfg.B, cfg.SEQ, cfg.DEPTH
    ncores = 2 * B
    x = np.asarray(inputs["x"], np.float32)
    meta = np.asarray(inputs["meta_tokens"], np.float32)
    params = {}
    for nm in PARAM_NAMES:
        a = np.ascontiguousarray(np.asarray(inputs[nm], np.float32))
        if nm == "da_lambda":
            a = a.reshape(a.shape[0], 256)
        params[nm] = a
    rpb = np.asarray(inputs["na_rpb"], np.float32)
    tbl = np.asarray(inputs["t5_rel_bias"], np.float32)
    per_core = []
    for c in range(ncores):
        b, half = c // 2, c % 2
        d = dict(params)
        d["na_mask"] = build_na_mask(cfg, rpb, half).reshape(cfg.NLA, cfg.NVAR, 128, 16 * 6 * 128)
        if cfg.NLB:
            G, Gm, cb, Bm, Gx = build_da_tables(cfg, tbl, half)
            d["da_G"] = G.reshape(8, 128, 2 * 1152)
            d["da_Gm"], d["da_cb"], d["da_Bm"], d["da_Gx"] = Gm, cb, Bm, Gx
        else:
            for nm in ("da_w_qkv", "da_w_o", "da_lambda", "da_subln_g"):
                d.pop(nm, None)
        h0 = np.zeros((cfg.TCP, D), np.float32)
        h0[:NMETA] = meta
        h0[128:] = x[b, half * cfg.NG:(half + 1) * cfg.NG]
        d["h_in"] = h0
        per_core.append(d)
    global _last_ninst
    if mode == "fused":
        bld = Builder(cfg, "fused", None)
        nc = bld.build()
        _last_ninst = bld.ninst
        in_maps = [{k: per_core[c][k] for k in bld.in_names} for c in range(ncores)]
        res = run_bass_kernel_spmd(nc, in_maps, core_ids=list(range(ncores)))
        outs = [res.results[c]["out"] for c in range(ncores)]
    else:
        state = [dict() for _ in range(ncores)]
        for seg in range(DEPTH + 1):
            bld = Builder(cfg, "multi", seg)
            nc = bld.build()
            _last_ninst = bld.ninst
            in_maps = []
            for c in range(ncores):
                m = {}
                for k in bld.in_names:
                    m[k] = state[c][k] if k in state[c] else per_core[c][k]
                in_maps.append(m)
            res = run_bass_kernel_spmd(nc, in_maps, core_ids=list(range(ncores)))
            if seg < DEPTH:
                for c in range(ncores):
                    r = res.results[c]
                    state[c] = {"h_in": r["h_out"], "QT": r["QT_o"], "KT": r["KT_o"], "V": r["V_o"]}
                CH = 131072
                HS = 128 * cfg.TCP
                for b in range(B):
                    c0, c1 = state[2 * b], state[2 * b + 1]
                    g = {}
                    if seg % 2 == 1:
                        for nm in ("KT", "V"):
                            a0, a1 = np.asarray(c0[nm]).reshape(8, HS), np.asarray(c1[nm]).reshape(8, HS)
                            g[nm + "_all"] = np.ascontiguousarray(np.stack([a0, a1], axis=1))
                    else:
                        for nm in ("KT", "V"):
                            a0, a1 = np.asarray(c0[nm]), np.asarray(c1[nm])
                            g[nm + "lo_all"] = np.stack([a0[CH:3 * CH], a1[CH:3 * CH]])
                            g[nm + "hi_all"] = np.stack([a0[(cfg.NGT - 1) * CH:(cfg.NGT + 1) * CH],
                                                         a1[(cfg.NGT - 1) * CH:(cfg.NGT + 1) * CH]])
                    for c in (2 * b, 2 * b + 1):
                        state[c].update(g)
            else:
                outs = [res.results[c]["out"] for c in range(ncores)]
    out = np.zeros((B, SEQ, D), np.float32)
    for c in range(ncores):
        b, half = c // 2, c % 2
        out[b, half * cfg.NG:(half + 1) * cfg.NG] = outs[c]
    return out


def kernel(**inputs):
    cfg = Cfg()
    return run_forward(cfg, inputs)
```
